# Optimizing a Trainium2 kernel written in Bass

```python
import jax, jax.numpy as jnp
from jax import lax
import numpy as np

D_MODEL = 4096
BATCH = 2
SEQ = 4096
DEPTH = 4

MIX_WIDTH = D_MODEL
HEAD_DIM = 128
CONV_WIDTH = D_MODEL // 4
CONV_K = 3
NSA_HEADS = (D_MODEL // 2) // HEAD_DIM
NSA_WIDTH = NSA_HEADS * HEAD_DIM
NSA_KV_HEADS = 4
NSA_REP = NSA_HEADS // NSA_KV_HEADS
NSA_BRANCHES = 3
KV_WIDTH = NSA_KV_HEADS * HEAD_DIM
CMP_LEN = 32
CMP_STRIDE = 16
SLC_BLOCK = 64
N_SELECT = 16
WINDOW = 512
Q_BLOCK = 128
SLC_Q_BLOCK = 64
GMLP_WIDTH = D_MODEL // 4
GMLP_GROUPS = GMLP_WIDTH // HEAD_DIM
GMLP_CHUNK = 128
D_FF = 4 * D_MODEL
PLE_DIM = 256
IN_COLS = 3 * CONV_WIDTH + NSA_WIDTH + 2 * NSA_BRANCHES * KV_WIDTH + NSA_HEADS * NSA_BRANCHES + 2 * GMLP_WIDTH
EPS = 1e-6
NEG_INF = -1e30
FORCE_SCORE = 1e9

kernel_name = 'hymba_style_conv_nsa_gmlp_trunk'


def rms_norm(x, gain):
    x32 = x.astype(jnp.float32)
    y = x32 * lax.rsqrt(jnp.mean(x32 * x32, axis=-1, keepdims=True) + EPS)
    return (y * gain.astype(jnp.float32)).astype(x.dtype)


def masked_softmax(s, mask):
    p = jax.nn.softmax(jnp.where(mask, s, NEG_INF), axis=-1)
    return jnp.where(mask, p, 0.0)


def alibi_slopes():
    h = np.arange(1, NSA_HEADS + 1, dtype=np.float32)
    slopes = np.power(np.float32(2.0), -8.0 * h / NSA_HEADS).astype(np.float32)
    return jnp.asarray(slopes).reshape(NSA_KV_HEADS, NSA_REP)


def short_conv_mixer(xa, gb, gc, conv_w):
    T = xa.shape[1]
    inner = gc * xa
    padded = jnp.pad(inner, ((0, 0), (CONV_K - 1, 0), (0, 0)))
    conv = sum(conv_w[k] * padded[:, k:k + T] for k in range(CONV_K))
    return gb * conv


def nsa_mixer(q, k_cmp, v_cmp, k_slc, v_slc, k_win, v_win, gate_logits,
              q_gain, k_gain, cmp_pos, cmp_w1, cmp_w2):
    B, T = q.shape[:2]
    G, R, Dh = NSA_KV_HEADS, NSA_REP, HEAD_DIM
    f32 = jnp.float32
    scale = Dh ** -0.5
    slopes = alibi_slopes()
    pos = jnp.arange(T, dtype=jnp.int32)
    q = rms_norm(q.reshape(B, T, G, R, Dh), q_gain)
    heads = lambda a: a.reshape(B, T, G, Dh)

    ratio = CMP_LEN // CMP_STRIDE
    n_cmp = T // CMP_STRIDE - ratio + 1

    def compress(k, pe, w1, w2):
        kc = k.reshape(B, T // CMP_STRIDE, CMP_STRIDE, G, Dh)
        blocks = jnp.concatenate([kc[:, i:i + n_cmp] for i in range(ratio)], axis=2)
        blocks = blocks + pe[None, None, :, None, :]
        hid = jax.nn.gelu(jnp.einsum('bnlgd,lde->bnge', blocks, w1))
        return jnp.einsum('bnge,ef->bngf', hid, w2)

    kc = rms_norm(compress(heads(k_cmp), cmp_pos[0], cmp_w1[0], cmp_w2[0]), k_gain[0])
    vc = compress(heads(v_cmp), cmp_pos[1], cmp_w1[1], cmp_w2[1])
    cmp_end = jnp.arange(n_cmp, dtype=jnp.int32) * CMP_STRIDE + CMP_LEN - 1
    dist_c = (pos[:, None] - cmp_end[None, :]).astype(f32)
    s_c = jnp.einsum('btgrd,bngd->bgrtn', q, kc, preferred_element_type=f32) * scale
    s_c = s_c - slopes[:, :, None, None] * dist_c
    p_cmp = masked_softmax(s_c, dist_c >= 0)
    o_cmp = jnp.einsum('bgrtn,bngd->btgrd', p_cmp.astype(vc.dtype), vc)

    n_slc = T // SLC_BLOCK
    jc = jnp.arange(n_cmp, dtype=jnp.int32)[:, None] * CMP_STRIDE
    bs = jnp.arange(n_slc, dtype=jnp.int32)[None, :] * SLC_BLOCK
    overlap = jnp.clip(jnp.minimum(jc + CMP_LEN, bs + SLC_BLOCK) - jnp.maximum(jc, bs), 0, None).astype(f32) / CMP_LEN
    imp = jnp.einsum('bgrtn,nm->bgtm', p_cmp, overlap)
    blk = jnp.arange(n_slc, dtype=jnp.int32)[None, :]
    qblk = (pos // SLC_BLOCK)[:, None]
    valid = blk <= qblk
    forced = (blk == 0) | (blk == qblk) | (blk == qblk - 1)
    imp = jnp.where(forced, FORCE_SCORE, jnp.where(valid, imp, NEG_INF))
    n_sel = min(N_SELECT, n_slc)
    _, idx = lax.top_k(imp, n_sel)

    kb = rms_norm(heads(k_slc), k_gain[1]).reshape(B, n_slc, SLC_BLOCK, G, Dh).transpose(0, 3, 1, 2, 4)
    vb = heads(v_slc).reshape(B, n_slc, SLC_BLOCK, G, Dh).transpose(0, 3, 1, 2, 4)
    n_qc = T // SLC_Q_BLOCK
    q_c = q.reshape(B, n_qc, SLC_Q_BLOCK, G, R, Dh).transpose(1, 0, 2, 3, 4, 5)
    idx_c = idx.reshape(B, G, n_qc, SLC_Q_BLOCK, n_sel).transpose(2, 0, 1, 3, 4)
    pos_c = pos.reshape(n_qc, SLC_Q_BLOCK)
    bi = jnp.arange(B)[:, None, None, None]
    gi = jnp.arange(G)[None, :, None, None]
    tok_off = jnp.arange(SLC_BLOCK, dtype=jnp.int32)

    def slc_block(args):
        qb, ib, tb = args
        kg = kb[bi, gi, ib]
        vg = vb[bi, gi, ib]
        s = jnp.einsum('bqgrd,bgqkid->bgrqki', qb, kg, preferred_element_type=f32) * scale
        kpos = ib[..., None] * SLC_BLOCK + tok_off
        dist = (tb[None, None, :, None, None] - kpos).astype(f32)[:, :, None]
        s = s - slopes[None, :, :, None, None, None] * dist
        mask = jnp.broadcast_to(dist >= 0, s.shape)
        shp = s.shape
        p = masked_softmax(s.reshape(shp[:4] + (-1,)), mask.reshape(shp[:4] + (-1,))).reshape(shp)
        return jnp.einsum('bgrqki,bgqkid->bqgrd', p.astype(vg.dtype), vg)

    o_slc = lax.map(slc_block, (q_c, idx_c, pos_c)).transpose(1, 0, 2, 3, 4, 5).reshape(B, T, G, R, Dh)

    n_qb = T // Q_BLOCK
    n_prev = WINDOW // Q_BLOCK
    band = WINDOW + Q_BLOCK

    def banded(a):
        ap = jnp.pad(a, ((0, 0), (WINDOW, 0), (0, 0), (0, 0))).reshape(B, n_qb + n_prev, Q_BLOCK, G, Dh)
        return jnp.concatenate([ap[:, j:j + n_qb] for j in range(n_prev + 1)], axis=2)

    kw = banded(rms_norm(heads(k_win), k_gain[2]))
    vw = banded(heads(v_win))
    qw = q.reshape(B, n_qb, Q_BLOCK, G, R, Dh)
    qpos = pos.reshape(n_qb, Q_BLOCK)
    kpos = jnp.arange(n_qb, dtype=jnp.int32)[:, None] * Q_BLOCK - WINDOW + jnp.arange(band, dtype=jnp.int32)[None, :]
    dist_w = qpos[:, :, None] - kpos[:, None, :]
    mask_w = (dist_w >= 0) & (dist_w < WINDOW) & (kpos[:, None, :] >= 0)
    s_w = jnp.einsum('bnqgrd,bnkgd->bgrnqk', qw, kw, preferred_element_type=f32) * scale
    s_w = s_w - slopes[:, :, None, None, None] * dist_w.astype(f32)
    p_w = masked_softmax(s_w, mask_w)
    o_win = jnp.einsum('bgrnqk,bnkgd->bnqgrd', p_w.astype(vw.dtype), vw).reshape(B, T, G, R, Dh)

    g = jax.nn.sigmoid(gate_logits.reshape(B, T, G, R, NSA_BRANCHES))
    out = g[..., 0:1] * o_cmp + g[..., 1:2] * o_slc + g[..., 2:3] * o_win
    return out.reshape(B, T, NSA_WIDTH)


def gmlp_mixer(u, v, sgu_gain, w_sp, b_sp):
    B, T = u.shape[:2]
    u = jax.nn.gelu(u)
    v = rms_norm(jax.nn.gelu(v), sgu_gain)
    nc = T // GMLP_CHUNK
    vc = v.reshape(B, nc, GMLP_CHUNK, GMLP_GROUPS, HEAD_DIM)
    causal = jnp.tril(jnp.ones((GMLP_CHUNK, GMLP_CHUNK), dtype=w_sp.dtype))
    spatial = jnp.einsum('gts,bcsge->bctge', w_sp * causal, vc) + b_sp.T[:, :, None]
    return u * spatial.reshape(B, T, GMLP_WIDTH)


def setup_inputs(seed: int = 0) -> dict:
    key = jax.random.key(seed)
    ks = jax.random.split(key, 21)
    f32 = jnp.float32

    def nrm(k, shape, scale):
        return jax.random.normal(k, shape, f32) * scale

    def gain(k, shape):
        return 1.0 + 0.01 * jax.random.normal(k, shape, f32)

    return {
        'x': nrm(ks[0], (BATCH, SEQ, D_MODEL), 1.0),
        'p': nrm(ks[1], (DEPTH, BATCH, SEQ, PLE_DIM), 1.0),
        'mix_norm': gain(ks[2], (DEPTH, D_MODEL)),
        'w_in': nrm(ks[3], (DEPTH, D_MODEL, IN_COLS), D_MODEL ** -0.5),
        'conv_w': nrm(ks[4], (DEPTH, CONV_K, CONV_WIDTH), CONV_K ** -0.5),
        'q_gain': gain(ks[5], (DEPTH, HEAD_DIM)),
        'k_gain': gain(ks[6], (DEPTH, NSA_BRANCHES, HEAD_DIM)),
        'cmp_pos': nrm(ks[7], (DEPTH, 2, CMP_LEN, HEAD_DIM), 0.1),
        'cmp_w1': nrm(ks[8], (DEPTH, 2, CMP_LEN, HEAD_DIM, HEAD_DIM), (CMP_LEN * HEAD_DIM) ** -0.5),
        'cmp_w2': nrm(ks[9], (DEPTH, 2, HEAD_DIM, HEAD_DIM), HEAD_DIM ** -0.5),
        'sgu_gain': gain(ks[10], (DEPTH, GMLP_WIDTH)),
        'w_sp': nrm(ks[11], (DEPTH, GMLP_GROUPS, GMLP_CHUNK, GMLP_CHUNK), GMLP_CHUNK ** -0.5),
        'b_sp': 1.0 + nrm(ks[12], (DEPTH, GMLP_GROUPS, GMLP_CHUNK), 0.01),
        'mix_out_norm': gain(ks[13], (DEPTH, MIX_WIDTH)),
        'w_out': nrm(ks[14], (DEPTH, MIX_WIDTH, D_MODEL), MIX_WIDTH ** -0.5),
        'mlp_norm': gain(ks[15], (DEPTH, D_MODEL)),
        'w_mlp_in': nrm(ks[16], (DEPTH, D_MODEL, D_FF), D_MODEL ** -0.5),
        'w_mlp_out': nrm(ks[17], (DEPTH, D_FF, D_MODEL), D_FF ** -0.5),
        'ple_norm': gain(ks[18], (DEPTH, D_MODEL)),
        'w_ple_proj': nrm(ks[19], (DEPTH, PLE_DIM, D_MODEL), PLE_DIM ** -0.5),
        'w_ple_gate': nrm(ks[20], (DEPTH, D_MODEL, D_MODEL), D_MODEL ** -0.5),
    }


def reference(x, p, mix_norm, w_in, conv_w, q_gain, k_gain, cmp_pos, cmp_w1, cmp_w2,
              sgu_gain, w_sp, b_sp, mix_out_norm, w_out, mlp_norm, w_mlp_in, w_mlp_out,
              ple_norm, w_ple_proj, w_ple_gate):
    sizes = [CONV_WIDTH] * 3 + [NSA_WIDTH] + [KV_WIDTH] * (2 * NSA_BRANCHES) + [NSA_HEADS * NSA_BRANCHES] + [GMLP_WIDTH] * 2
    offsets = np.cumsum(sizes)[:-1].tolist()
    h = x
    for i in range(DEPTH):
        xn = rms_norm(h, mix_norm[i])
        z = xn @ w_in[i]
        (a_x, a_b, a_c, q, k_cmp, v_cmp, k_slc, v_slc, k_win, v_win,
         gate_logits, g_u, g_v) = jnp.split(z, offsets, axis=-1)
        o_a = short_conv_mixer(a_x, a_b, a_c, conv_w[i])
        o_b = nsa_mixer(q, k_cmp, v_cmp, k_slc, v_slc, k_win, v_win, gate_logits,
                        q_gain[i], k_gain[i], cmp_pos[i], cmp_w1[i], cmp_w2[i])
        o_c = gmlp_mixer(g_u, g_v, sgu_gain[i], w_sp[i], b_sp[i])
        on = mix_out_norm[i]
        mixed = jnp.concatenate([
            rms_norm(o_a, on[:CONV_WIDTH]),
            rms_norm(o_b, on[CONV_WIDTH:CONV_WIDTH + NSA_WIDTH]),
            rms_norm(o_c, on[CONV_WIDTH + NSA_WIDTH:]),
        ], axis=-1)
        h = h + mixed @ w_out[i]
        hn = rms_norm(h, mlp_norm[i])
        h = h + jnp.square(jax.nn.relu(hn @ w_mlp_in[i])) @ w_mlp_out[i]
        gate = jax.nn.sigmoid(rms_norm(h, ple_norm[i]) @ w_ple_gate[i])
        h = h + gate * (p[i] @ w_ple_proj[i])
    return h
```

```python
import contextlib
import numpy as np
import concourse.bass as bass
import concourse.mybir as mybir
from concourse.bass_utils import run_bass_kernel_spmd

F32 = mybir.dt.float32
BF16 = mybir.dt.bfloat16
U8 = mybir.dt.uint8
AF = mybir.ActivationFunctionType
ALU = mybir.AluOpType

ENGS = ("pe", "act", "dve", "pool", "sp")
EPS = 1e-6
ARENA_BYTES = 207 * 1024
NEG = -30000.0

D_MODEL = 4096
SEQ = 4096
BATCH = 2
DEPTH = 4
IN_COLS = 10288
D_FF = 16384
OFF_Q, OFF_KV, OFF_GATE, OFF_GU = 3072, 5120, 8192, 8240


class Tok:
    __slots__ = ("sem", "val")

    def __init__(self, sem, val):
        self.sem = sem
        self.val = val


class DmaSem:
    def __init__(self, sem):
        self.sem = sem
        self.n = 0
        self.bg = False


class PB:
    def __init__(self, num_devices=None):
        if num_devices is None:
            self.nc = bass.Bass("TRN2", target_bir_lowering=False)
        else:
            self.nc = bass.Bass("TRN2", target_bir_lowering=False, num_devices=num_devices)
        self.es = contextlib.ExitStack()
        self.q = {e: [] for e in ENGS}
        self.cnt = {e: 0 for e in ENGS}
        self.psem = {e: self.es.enter_context(self.nc.semaphore("prog_" + e)) for e in ENGS}
        self.waited = {e: {} for e in ENGS}
        self.dsems = []
        self.sem_free = []
        self.sem_live = []
        self.nsem = 0
        self.ntens = 0
        self.arena = self.es.enter_context(self.nc.sbuf_tensor("arena", [128, ARENA_BYTES], U8))
        self.top = 0
        self.ndram = 0

    def view(self, off, shape, dt):
        esz = 4 if dt == F32 else 2
        nb = int(np.prod(shape[1:])) * esz
        assert off % 4 == 0 and off + nb <= ARENA_BYTES, (off, nb)
        a = self.arena[0:shape[0], off:off + nb].bitcast(dt)
        if len(shape) == 3:
            a = a.rearrange("p (a b) -> p a b", a=shape[1])
        elif len(shape) == 4:
            a = a.rearrange("p (a b c) -> p a b c", a=shape[1], b=shape[2])
        return a

    def alloc(self, shape, dt):
        esz = 4 if dt == F32 else 2
        nb = int(np.prod(shape[1:])) * esz
        nb = (nb + 31) // 32 * 32
        off = self.top
        self.top += nb
        assert self.top <= ARENA_BYTES, ("arena overflow", self.top)
        return self.view(off, shape, dt)

    def ps(self, shape, dt, name=None):
        self.ntens += 1
        return self.es.enter_context(self.nc.psum_tensor(name or f"ps{self.ntens}", list(shape), dt))[:]

    def dsem(self, name=None):
        if self.sem_free:
            d = self.sem_free.pop()
        else:
            self.nsem += 1
            d = DmaSem(self.es.enter_context(self.nc.semaphore(f"s{self.nsem}")))
            self.dsems.append(d)
        self.sem_live.append(d)
        return d

    def sem_mark(self):
        return len(self.sem_live)

    def sem_release(self, mark):
        while len(self.sem_live) > mark:
            self.sem_free.append(self.sem_live.pop())

    def dram(self, name, shape, dt=F32, kind="Internal"):
        return self.nc.dram_tensor(name, list(shape), dt, kind=kind).ap()

    def _wait(self, eng, toks):
        w = self.waited[eng]
        for t in toks:
            if t is None:
                continue
            key = id(t.sem)
            if w.get(key, 0) >= t.val:
                continue
            w[key] = t.val
            self.q[eng].append(lambda e, sem=t.sem, val=t.val: e.wait_ge(sem, val))

    def op(self, eng, fn, waits=(), sig=True):
        self._wait(eng, waits)
        if sig:
            self.cnt[eng] += 1
            sem = self.psem[eng]
            self.q[eng].append(lambda e, fn=fn, sem=sem: fn(e).then_inc(sem, 1))
            return Tok(sem, self.cnt[eng])
        self.q[eng].append(lambda e, fn=fn: fn(e))
        return None

    def dma(self, eng, out, in_, sem, waits=(), **kw):
        self._wait(eng, waits)
        sem.n += 16
        s = sem.sem
        self.q[eng].append(
            lambda e, out=out, in_=in_, s=s, kw=kw: e.dma_start(out=out, in_=in_, **kw).then_inc(s, 16))
        return Tok(s, sem.n)

    def barrier(self, everything=False):
        toks = [Tok(self.psem[e], self.cnt[e]) for e in ENGS if self.cnt[e] > 0]
        toks += [Tok(d.sem, d.n) for d in self.dsems if d.n > 0 and (everything or not d.bg)]
        for e in ENGS:
            self._wait(e, toks)

    def finish(self):
        self.barrier(everything=True)
        nc, q = self.nc, self.q
        with nc.Block() as block:
            @block.tensor
            def _(e):
                for f in q["pe"]:
                    f(e)

            @block.scalar
            def _(e):
                for f in q["act"]:
                    f(e)

            @block.vector
            def _(e):
                for f in q["dve"]:
                    f(e)

            @block.gpsimd
            def _(e):
                for f in q["pool"]:
                    f(e)

            @block.sync
            def _(e):
                for f in q["sp"]:
                    f(e)
        self.es.close()
        return nc


class Ring:
    def __init__(self, P, n, shape, dt):
        self.bufs = [P.alloc(shape, dt) for _ in range(n)]
        self.sems = [P.dsem() for _ in range(n)]
        self.free = [None] * n
        self.i = 0

    def next(self):
        i = self.i % len(self.bufs)
        self.i += 1
        return i


class Ctx:
    def __init__(self, P, T, ident_dram, nbanks=6, wslots=True):
        self.P = P
        self.T = T
        self.NT = T // 128
        self.ident = P.alloc([128, 128], BF16)
        self.ident_tok = P.dma("pool", self.ident, ident_dram, P.dsem())
        self.banks = [P.ps([128, 512], F32, f"bank{i}") for i in range(nbanks)]
        self.bank_free = [None] * nbanks
        self.tp = [P.ps([128, 8, 128], BF16, f"tp{i}") for i in range(8 - nbanks)]
        self.tp_free = [None] * (8 - nbanks)
        self.tp_i = 0
        self.eps_t = P.alloc([128, 1], F32)
        P.op("dve", lambda e: e.memset(self.eps_t, EPS))
        self.ss = P.alloc([128, 8], F32)
        self.ss_free = None
        self.w_cast = True
        self.w_ready = []
        self.wsem = [P.dsem() for _ in range(2)]

    def alloc_wslots(self):
        P = self.P
        self.wbase = P.top
        self.wslots = [P.alloc([128, 32, 512], BF16) for _ in range(2)]
        self.wfree = [None, None]
        self.wnext = 0


def load_wblock(C, w_ap, k0, KC, n0, nw):
    P = C.P
    i = C.wnext
    C.wnext = (C.wnext + 1) % len(C.wslots)
    src = w_ap[k0 * 128:(k0 + KC) * 128, n0:n0 + nw].rearrange("(k p) n -> p k n", p=128)
    q = "pool" if C.w_cast else "sp"
    tok = P.dma(q, C.wslots[i][:, 0:KC, 0:nw], src, C.wsem[i], waits=[C.wfree[i]] + list(C.w_ready))
    return i, tok


def rstd_from_ss(C, col, n, waits):
    P = C.P
    ss = C.ss
    tk = P.op("act", lambda e: e.activation(out=ss[:, col:col + 1], in_=ss[:, col:col + 1], func=AF.Sqrt,
                                            scale=1.0 / n, bias=C.eps_t[:, 0:1]), waits=waits)
    return P.op("dve", lambda e: e.reciprocal(out=ss[:, col:col + 1], in_=ss[:, col:col + 1]), waits=[tk])


def norm_transpose(C, produce, segs, gain_bc, gain_tok, actT, xt, xb):
    P = C.P
    W = segs[-1][1]
    KC = W // 128
    ss = C.ss
    xt_free = None
    xb_free = None
    last = []
    for t in range(C.NT):
        ltoks = produce(t, xt, xt_free)
        stoks = []
        for si, (c0, c1) in enumerate(segs):
            stoks.append(P.op("act", lambda e, c0=c0, c1=c1, si=si: e.activation(
                out=xb[:, c0:c1], in_=xt[:, c0:c1], func=AF.Square, accum_out=ss[:, si:si + 1]),
                waits=list(ltoks) + [C.ss_free, xb_free]))
        rts = [rstd_from_ss(C, si, c1 - c0, [stoks[si]]) for si, (c0, c1) in enumerate(segs)]
        ntoks = []
        for si, (c0, c1) in enumerate(segs):
            ntoks.append(P.op("dve", lambda e, c0=c0, c1=c1, si=si: e.scalar_tensor_tensor(
                out=xb[:, c0:c1], in0=xt[:, c0:c1], scalar=ss[:, si:si + 1], in1=gain_bc[:, c0:c1],
                op0=ALU.mult, op1=ALU.mult), waits=[rts[-1], gain_tok, stoks[-1]]))
        xt_free = ntoks[-1]
        C.ss_free = ntoks[-1]
        for c8 in range(KC // 8):
            pi = C.tp_i % len(C.tp)
            C.tp_i += 1
            tp = C.tp[pi]
            for j in range(8):
                c = c8 * 8 + j
                tk = P.op("pe", lambda e, tp=tp, j=j, c=c: e.transpose(
                    out=tp[:, j, :], in_=xb[:, c * 128:(c + 1) * 128], identity=C.ident),
                    waits=[ntoks[-1], C.tp_free[pi], C.ident_tok], sig=(j == 7))
            dst = actT[:, c8 * 8:(c8 + 1) * 8, t * 128:(t + 1) * 128]
            if c8 % 2 == 0:
                ev = P.op("act", lambda e, tp=tp, dst=dst: e.activation(out=dst, in_=tp, func=AF.Copy), waits=[tk])
            else:
                ev = P.op("dve", lambda e, tp=tp, dst=dst: e.tensor_copy(out=dst, in_=tp), waits=[tk])
            C.tp_free[pi] = ev
            last.append(ev)
        xb_free = tk
    return last[-2:]


def gemm_T(C, actT, KC, w_ap, k0, N, act_ready, epilogue, banks=(0, 1, 2), n_off=0):
    P = C.P
    nblocks = [(n0, min(512, N - n0)) for n0 in range(0, N, 512)]
    it = iter(nblocks)
    pend = []
    nb = next(it, None)
    if nb is not None:
        pend.append(load_wblock(C, w_ap, k0, KC, nb[0] + n_off, nb[1]))
    bsel = 0
    tk = None
    for (n0, nw) in nblocks:
        nb = next(it, None)
        if nb is not None:
            pend.append(load_wblock(C, w_ap, k0, KC, nb[0] + n_off, nb[1]))
        slot, wtok = pend.pop(0)
        ws = C.wslots[slot]
        for t in range(C.NT):
            b = banks[bsel % len(banks)]
            bsel += 1
            ps = C.banks[b]
            for k in range(KC):
                tk = P.op("pe", lambda e, ps=ps, k=k, t=t, ws=ws, nw=nw: e.matmul(
                    out=ps[:, 0:nw], lhsT=actT[:, k, t * 128:(t + 1) * 128], rhs=ws[:, k, 0:nw],
                    start=(k == 0), stop=(k == KC - 1)),
                    waits=[wtok, C.bank_free[b]] + list(act_ready), sig=(k == KC - 1))
            C.bank_free[b] = epilogue(t, n0, nw, ps, tk)
        C.wfree[slot] = tk
    return tk


def gemm_F(C, actT, KC, w_ap, k0, N, act_ready, epilogue, banks=(3, 4, 5), n_off=0):
    P = C.P
    nblocks = [(n0, min(512, N - n0)) for n0 in range(0, N, 512)]
    it = iter(nblocks)
    pend = []
    nb = next(it, None)
    if nb is not None:
        pend.append(load_wblock(C, w_ap, k0, KC, nb[0] + n_off, nb[1]))
    bsel = 0
    TH = C.T // 512
    tk = None
    for (n0, nw) in nblocks:
        nb = next(it, None)
        if nb is not None:
            pend.append(load_wblock(C, w_ap, k0, KC, nb[0] + n_off, nb[1]))
        slot, wtok = pend.pop(0)
        ws = C.wslots[slot]
        for j in range(nw // 128):
            for th in range(TH):
                b = banks[bsel % len(banks)]
                bsel += 1
                ps = C.banks[b]
                for k in range(KC):
                    tk = P.op("pe", lambda e, ps=ps, k=k, th=th, ws=ws, j=j: e.matmul(
                        out=ps, lhsT=ws[:, k, j * 128:(j + 1) * 128], rhs=actT[:, k, th * 512:(th + 1) * 512],
                        start=(k == 0), stop=(k == KC - 1)),
                        waits=[wtok, C.bank_free[b]] + list(act_ready), sig=(k == KC - 1))
                C.bank_free[b] = epilogue(n0 // 128 + j, th, ps, tk)
        C.wfree[slot] = tk
    return tk


def phase_A(P, C, h, gain, w_in, z, K=D_MODEL, N=IN_COLS):
    T = C.T
    KC = K // 128
    mark = P.top
    smark = P.sem_mark()
    C.alloc_wslots()
    actT = P.alloc([128, KC, T], BF16)
    gain_bc = P.alloc([128, K], F32)
    xt = P.alloc([128, K], F32)
    xb = P.alloc([128, K], BF16)
    oring = Ring(P, 4, [128, 512], F32)
    gtok = P.dma("sp", gain_bc, gain.to_broadcast([128, K]), P.dsem())
    xsem = P.dsem()

    def produce(t, xt, free):
        return [P.dma("sp", xt, h[t * 128:(t + 1) * 128, :], xsem, waits=[free])]

    ready = norm_transpose(C, produce, [(0, K)], gain_bc, gtok, actT, xt, xb)

    def epi(t, n0, nw, ps, mmtok):
        i = oring.next()
        ob = oring.bufs[i]
        tk = P.op("act", lambda e: e.activation(out=ob[:, 0:nw], in_=ps[:, 0:nw], func=AF.Copy),
                  waits=[mmtok, oring.free[i]])
        oring.free[i] = P.dma("sp", z[t * 128:(t + 1) * 128, n0:n0 + nw], ob[:, 0:nw], oring.sems[i], waits=[tk])
        return tk

    gemm_T(C, actT, KC, w_in, 0, N, ready, epi)
    P.barrier()
    P.top = mark
    P.sem_release(smark)


class SubAlloc:
    def __init__(self, P, base, size):
        self.P, self.base, self.size, self.top = P, base, size, 0

    def alloc(self, shape, dt):
        esz = 4 if dt == F32 else 2
        nb = (int(np.prod(shape[1:])) * esz + 31) // 32 * 32
        off = self.base + self.top
        self.top += nb
        assert self.top <= self.size, ("suballoc overflow", self.top, self.size)
        return self.P.view(off, shape, dt)


def gelu_tanh(P, x, tmp, waits):
    t1 = P.op("dve", lambda e: e.tensor_tensor(out=tmp, in0=x, in1=x, op=ALU.mult), waits=waits)
    t2 = P.op("dve", lambda e: e.tensor_scalar(out=tmp, in0=tmp, scalar1=0.044715, scalar2=1.0,
                                               op0=ALU.mult, op1=ALU.add), waits=[t1])
    t3 = P.op("dve", lambda e: e.tensor_tensor(out=tmp, in0=tmp, in1=x, op=ALU.mult), waits=[t2])
    t4 = P.op("act", lambda e: e.activation(out=tmp, in_=tmp, func=AF.Sigmoid, scale=1.5957691216057308),
              waits=[t3])
    return P.op("dve", lambda e: e.tensor_tensor(out=x, in0=x, in1=tmp, op=ALU.mult), waits=[t4])


def phase_C(P, C, d, n_ffq=8):
    T, NT = C.T, C.NT
    mark = P.top
    smark = P.sem_mark()
    C.alloc_wslots()
    actT = P.alloc([128, 32, T], BF16)
    r1 = P.top
    P.top += 32 * 1024
    xt = P.view(r1, [128, 4096], F32)
    gain_bc = P.view(r1 + 16384, [128, 4096], F32)
    h1T = P.view(r1, [128, 16, T], BF16)
    wp = P.view(r1, [128, 2, 4096], BF16)
    xb = P.alloc([128, 4096], BF16)
    oring = Ring(P, 4, [128, 512], F32)
    hring = Ring(P, 4, [128, 512], F32)
    rring = Ring(P, 2, [128, 512], F32)
    pT = P.alloc([128, 2, T], BF16)
    pld = P.alloc([128, 256], F32)
    pbf = P.alloc([128, 256], BF16)

    S = SubAlloc(P, C.wbase, 64 * 1024)
    convw = S.alloc([128, 3072], F32)
    Xs = [S.alloc([128, 512], F32) for _ in range(3)]
    Cs = [S.alloc([128, 512], F32) for _ in range(3)]
    Bt = S.alloc([128, 512], F32)
    cacc = S.alloc([128, 512], F32)
    U = S.alloc([128, 1024], F32)
    V = S.alloc([128, 1024], F32)
    gtmp = S.alloc([128, 1024], F32)
    Vb = S.alloc([128, 1024], BF16)
    sgu_bc = S.alloc([128, 1024], F32)
    wsp = S.alloc([128, 8, 128], F32)
    wspb = S.alloc([128, 8, 128], BF16)
    WsT = S.alloc([128, 8, 128], BF16)
    tril = S.alloc([128, 128], F32)
    bsp = S.alloc([128, 8], F32)

    csem = P.dsem()
    P.dma("sp", convw, d["conv_w"].to_broadcast([128, 3072]), csem)
    P.dma("sp", sgu_bc, d["sgu_gain"].to_broadcast([128, 1024]), csem)
    P.dma("sp", wsp, d["w_sp"].rearrange("g t s -> t g s"), csem)
    P.dma("sp", tril, d["tril"], csem)
    ctok = P.dma("sp", bsp, d["b_sp"].rearrange("g t -> t g"), csem, allow_slow_non_contiguous=True)
    gtok = P.dma("sp", gain_bc, d["mix_out_norm"].to_broadcast([128, 4096]), P.dsem())
    tk = None
    for g in range(8):
        tk = P.op("dve", lambda e, g=g: e.tensor_tensor(out=wspb[:, g, :], in0=wsp[:, g, :], in1=tril, op=ALU.mult),
                  waits=[ctok])
    tp = C.tp[0]
    for g in range(8):
        tk2 = P.op("pe", lambda e, g=g: e.transpose(out=tp[:, g, :], in_=wspb[:, g, :], identity=C.ident),
                   waits=[tk, C.ident_tok], sig=(g == 7))
    wst_tok = P.op("dve", lambda e: e.tensor_copy(out=WsT, in_=tp), waits=[tk2])
    C.tp_free[0] = wst_tok

    xsem = P.dsem()
    osem = P.dsem()
    gsem = P.dsem()
    st = {"conv_free": None, "gm_free": None, "gss_free": None}
    zc, zg, halo, ob = d["zc"], d["zg"], d["halo"], d["ob"]

    def produce(t, xt, free):
        toks = [P.dma("sp", xt[:, 1024:3072], ob[t * 128:(t + 1) * 128, :], osem, waits=[free])]
        for hc in range(2):
            c0 = hc * 512
            lt = None
            for s in range(3):
                for buf, coff in ((Xs[s], 0), (Cs[s], 2048)):
                    if t == 0 and s > 0:
                        P.dma("sp", buf[0:s, :], halo[2 - s:2, coff + c0:coff + c0 + 512], xsem,
                              waits=[st["conv_free"]])
                        lt = P.dma("sp", buf[s:128, :], zc[0:128 - s, coff + c0:coff + c0 + 512], xsem)
                    else:
                        lt = P.dma("sp", buf, zc[t * 128 - s:t * 128 - s + 128, coff + c0:coff + c0 + 512], xsem,
                                   waits=[st["conv_free"]])
            lt = P.dma("sp", Bt, zc[t * 128:(t + 1) * 128, 1024 + c0:1024 + c0 + 512], xsem)
            k = None
            for s in range(3):
                k = P.op("dve", lambda e, s=s: e.tensor_tensor(out=Xs[s], in0=Xs[s], in1=Cs[s], op=ALU.mult),
                         waits=[lt, k])
            k = P.op("dve", lambda e, c0=c0: e.tensor_tensor(
                out=cacc, in0=Xs[0], in1=convw[:, 2048 + c0:2048 + c0 + 512], op=ALU.mult), waits=[k, ctok])
            for s in (1, 2):
                k = P.op("dve", lambda e, s=s, c0=c0: e.tensor_tensor(
                    out=Xs[s], in0=Xs[s], in1=convw[:, (2 - s) * 1024 + c0:(2 - s) * 1024 + c0 + 512],
                    op=ALU.mult), waits=[k])
                k = P.op("dve", lambda e, s=s: e.tensor_tensor(out=cacc, in0=cacc, in1=Xs[s], op=ALU.add),
                         waits=[k])
            k = P.op("dve", lambda e, c0=c0: e.tensor_tensor(out=xt[:, c0:c0 + 512], in0=cacc, in1=Bt,
                                                              op=ALU.mult), waits=[k, free])
            st["conv_free"] = k
            toks.append(k)
        P.dma("sp", U, zg[t * 128:(t + 1) * 128, 0:1024], gsem, waits=[st["gm_free"]])
        lt = P.dma("sp", V, zg[t * 128:(t + 1) * 128, 1024:2048], gsem)
        k = gelu_tanh(P, U, gtmp, [lt])
        k = gelu_tanh(P, V, gtmp, [k])
        k = P.op("act", lambda e: e.activation(out=gtmp, in_=V, func=AF.Square, accum_out=C.ss[:, 4:5]),
                 waits=[k, st["gss_free"]])
        k = rstd_from_ss(C, 4, 1024, [k])
        k = P.op("dve", lambda e: e.scalar_tensor_tensor(out=Vb, in0=V, scalar=C.ss[:, 4:5], in1=sgu_bc,
                                                         op0=ALU.mult, op1=ALU.mult), waits=[k, ctok])
        st["gss_free"] = k
        mm = []
        for g in range(8):
            b = g // 4
            ps = C.banks[b]
            mm.append(P.op("pe", lambda e, ps=ps, g=g: e.matmul(
                out=ps[:, (g % 4) * 128:(g % 4 + 1) * 128], lhsT=WsT[:, g, :], rhs=Vb[:, g * 128:(g + 1) * 128],
                start=True, stop=True), waits=[k, wst_tok, C.bank_free[b]], sig=(g % 4 == 3)))
        for g in range(8):
            ps = C.banks[g // 4]
            k = P.op("dve", lambda e, ps=ps, g=g: e.scalar_tensor_tensor(
                out=xt[:, 3072 + g * 128:3072 + (g + 1) * 128], in0=ps[:, (g % 4) * 128:(g % 4 + 1) * 128],
                scalar=bsp[:, g:g + 1], in1=U[:, g * 128:(g + 1) * 128], op0=ALU.add, op1=ALU.mult),
                waits=[mm[g // 4 * 4 + 3], free])
            if g % 4 == 3:
                C.bank_free[g // 4] = k
        st["gm_free"] = k
        toks.append(k)
        return toks

    norm_transpose(C, produce, [(0, 1024), (1024, 3072), (3072, 4096)], gain_bc, gtok, actT, xt, xb)
    P.barrier()

    def resid_epi(src, dst):
        def epi(t, n0, nw, ps, mmtok):
            i = hring.next()
            hb = hring.bufs[i]
            lt = P.dma("act", hb[:, 0:nw], src[t * 128:(t + 1) * 128, n0:n0 + nw], hring.sems[i],
                       waits=[hring.free[i]])
            j = oring.next()
            ob_ = oring.bufs[j]
            tk = P.op("dve", lambda e: e.tensor_tensor(out=ob_[:, 0:nw], in0=ps[:, 0:nw], in1=hb[:, 0:nw],
                                                       op=ALU.add), waits=[mmtok, lt, oring.free[j]])
            hring.free[i] = tk
            oring.free[j] = P.dma("sp", dst[t * 128:(t + 1) * 128, n0:n0 + nw], ob_[:, 0:nw], oring.sems[j],
                                  waits=[tk])
            return tk
        return epi

    def full_row_producer(src, sem):
        def produce(t, xt, free):
            return [P.dma("sp", xt, src[t * 128:(t + 1) * 128, :], sem, waits=[free])]
        return produce

    hs = d["hs"]
    C.w_ready = d.get("tok_w_out", [])
    gemm_T(C, actT, 32, d["w_out"], 0, 4096, [], resid_epi(d["h_in"], hs))
    P.barrier()

    gtok = P.dma("sp", gain_bc, d["mlp_norm"].to_broadcast([128, 4096]), P.dsem())
    norm_transpose(C, full_row_producer(hs, P.dsem()), [(0, 4096)], gain_bc, gtok, actT, xt, xb)
    P.barrier()
    st2 = {"h1_free": None}
    FQ = D_FF // n_ffq
    KQ = FQ // 128

    def relu2_epi(j, th, ps, mmtok):
        i = rring.next()
        rb = rring.bufs[i]
        a = P.op("act", lambda e: e.activation(out=rb, in_=ps, func=AF.Relu), waits=[mmtok, rring.free[i]])
        rring.free[i] = P.op("dve", lambda e: e.tensor_tensor(
            out=h1T[:, j, th * 512:(th + 1) * 512], in0=rb, in1=rb, op=ALU.mult), waits=[a, st2["h1_free"]])
        st2["last_sq"] = rring.free[i]
        return a

    for q in range(n_ffq):
        C.w_ready = d.get("tok_w_mlp_in", [])
        gemm_F(C, actT, 32, d["w_mlp_in"], 0, FQ, [], relu2_epi, n_off=q * FQ)
        C.w_ready = d.get("tok_w_mlp_out", [])
        P._wait("act", [f for f in oring.free if f is not None])
        st2["h1_free"] = gemm_T(C, h1T, KQ, d["w_mlp_out"], q * KQ, 4096, [st2["last_sq"]], resid_epi(hs, hs))
    P.barrier()

    gtok = P.dma("sp", gain_bc, d["ple_norm"].to_broadcast([128, 4096]), P.dsem())
    norm_transpose(C, full_row_producer(hs, P.dsem()), [(0, 4096)], gain_bc, gtok, actT, xt, xb)
    P.barrier()
    wp_tok = P.dma("pool" if C.w_cast else "sp", wp, d["w_ple_proj"].rearrange("(k p) n -> p k n", p=128), P.dsem(),
                   waits=d.get("tok_w_ple_proj", []))
    psem = P.dsem()
    ptok = None
    pfree = None
    for t in range(NT):
        lt = P.dma("sp", pld, d["p"][t * 128:(t + 1) * 128, :], psem, waits=[pfree])
        k = P.op("dve", lambda e: e.tensor_copy(out=pbf, in_=pld), waits=[lt, ptok])
        pfree = k
        tp = C.tp[1]
        for kk in range(2):
            k2 = P.op("pe", lambda e, kk=kk: e.transpose(out=tp[:, kk, :], in_=pbf[:, kk * 128:(kk + 1) * 128],
                                                         identity=C.ident), waits=[k, C.tp_free[1]], sig=(kk == 1))
        ptok = P.op("act", lambda e, t=t: e.activation(out=pT[:, :, t * 128:(t + 1) * 128], in_=tp[:, 0:2, :],
                                                       func=AF.Copy), waits=[k2])
        C.tp_free[1] = ptok
        ptok2 = k2
    h_out = d["h_out"]
    b2sel = [0]

    def ple_epi(t, n0, nw, ps, mmtok):
        b = 3 + b2sel[0] % 3
        b2sel[0] += 1
        ps2 = C.banks[b]
        for kk in range(2):
            m2 = P.op("pe", lambda e, kk=kk: e.matmul(
                out=ps2[:, 0:nw], lhsT=pT[:, kk, t * 128:(t + 1) * 128], rhs=wp[:, kk, n0:n0 + nw],
                start=(kk == 0), stop=(kk == 1)), waits=[wp_tok, ptok, C.bank_free[b]], sig=(kk == 1))
        i = rring.next()
        rb = rring.bufs[i]
        a = P.op("act", lambda e: e.activation(out=rb[:, 0:nw], in_=ps[:, 0:nw], func=AF.Sigmoid),
                 waits=[mmtok, rring.free[i]])
        hi = hring.next()
        hb = hring.bufs[hi]
        lt = P.dma("act", hb[:, 0:nw], hs[t * 128:(t + 1) * 128, n0:n0 + nw], hring.sems[hi],
                   waits=[hring.free[hi]])
        j = oring.next()
        ob_ = oring.bufs[j]
        k1 = P.op("dve", lambda e: e.tensor_tensor(out=ob_[:, 0:nw], in0=ps2[:, 0:nw], in1=rb[:, 0:nw],
                                                   op=ALU.mult), waits=[a, m2, oring.free[j]])
        C.bank_free[b] = k1
        rring.free[i] = k1
        k2 = P.op("dve", lambda e: e.tensor_tensor(out=ob_[:, 0:nw], in0=ob_[:, 0:nw], in1=hb[:, 0:nw],
                                                   op=ALU.add), waits=[k1, lt])
        hring.free[hi] = k2
        oring.free[j] = P.dma("sp", h_out[t * 128:(t + 1) * 128, n0:n0 + nw], ob_[:, 0:nw], oring.sems[j],
                              waits=[k2])
        return a

    C.w_ready = d.get("tok_w_ple_gate", [])
    gemm_T(C, actT, 32, d["w_ple_gate"], 0, 4096, [], ple_epi)
    P.barrier()
    P.top = mark
    P.sem_release(smark)
    C.w_ready = []


NOFF = 35


def nsa_tables(g, T=SEQ):
    n_cmp = T // 16 - 1
    ncc = (n_cmp + 127) // 128
    nqb = T // 512
    slopes = np.array([2.0 ** (-(4 * g + r + 1) / 2.0) for r in range(4)], np.float64)
    ik = np.arange(128)[:, None]
    iq = np.arange(512)[None, :]
    t = {}
    t["negslope"] = np.tile(-slopes[None, :], (128, 1))
    t["D0"] = (iq - ik).astype(np.float64)
    t["D0c"] = (iq - 16 * ik).astype(np.float64)
    offb = np.zeros((4, NOFF))
    for r in range(4):
        for dl in range(-3, 32):
            offb[r, dl + 3] = -slopes[r] * 128.0 * dl
    t["OFFB"] = np.tile(offb.reshape(1, -1), (128, 1))
    offc = np.zeros((4, nqb * ncc))
    cpm = np.zeros((nqb * ncc, 128, 512))
    for pq in range(nqb):
        for c in range(ncc):
            off = 512 * pq - 2048 * c - 31
            for r in range(4):
                offc[r, pq * ncc + c] = -slopes[r] * off
            ok = ((iq - 16 * ik + off) >= 0) & ((128 * c + ik) < n_cmp)
            cpm[pq * ncc + c] = np.where(ok, 0.0, NEG)
    t["OFFC"] = np.tile(offc.reshape(1, -1), (128, 1))
    t["CPM"] = cpm.transpose(1, 0, 2)
    cm = np.zeros((4, 128, 512))
    for i in range(4):
        cm[i] = np.where((-128 * i + iq - ik) >= 0, 0.0, NEG)
    t["CM"] = cm.transpose(1, 0, 2)
    wm = np.zeros((8, 128, 512))
    for i in range(8):
        dist = 128 * (4 - i) + iq - ik
        wm[i] = np.where((dist >= 0) & (dist < 512), 0.0, NEG)
    t["WM"] = wm.transpose(1, 0, 2)
    nkc = T // 128
    e = np.zeros((128, nkc, 128))
    for c in range(nkc):
        for k in range(128):
            e[2 * c + k // 64, c, k] = 1.0 if 2 * c + k // 64 < 128 else 0.0
    t["E"] = e[:, :, :]
    nslc = T // 64
    addt = np.zeros((128, nkc, nslc))
    for qt in range(nkc):
        for p in range(128):
            qb = (128 * qt + p) // 64
            row = np.zeros(nslc)
            row[qb + 1:] = -1e30
            if qb - 1 >= 0:
                row[qb - 1] = 1e9
            row[qb] = 2e9
            row[0] = 3e9
            addt[p, qt] = row
    t["ADDT"] = addt
    ovl = np.zeros((ncc * 128, nslc))
    for n in range(n_cmp):
        for m in range(nslc):
            o = min(16 * n + 32, 64 * m + 64) - max(16 * n, 64 * m)
            ovl[n, m] = max(o, 0) / 32.0
    t["OVL"] = ovl.reshape(ncc, 128, nslc).transpose(1, 0, 2)
    return {k: np.ascontiguousarray(v.reshape(v.shape[0], -1), dtype=np.float32) for k, v in t.items()}


def phase_B(P, C, d, T=SEQ):
    NKC = T // 128
    NQB = T // 512
    n_cmp = T // 16 - 1
    NCC = (n_cmp + 127) // 128
    NSLC = T // 64
    VC = 129 + NSLC
    mark = P.top
    smark = P.sem_mark()
    A = P.alloc
    KT = A([128, 4, T], BF16)
    Vs = A([128, NKC, 130], BF16)
    Vw = A([128, NKC, 130], BF16)
    Ttab = A([128, 4, 512], F32)
    Ttabc = A([128, 4, 512], F32)
    negslope = A([128, 4], F32)
    D0 = A([128, 512], F32)
    D0c = A([128, 512], F32)
    OFFB = A([128, 4 * NOFF], F32)
    OFFC = A([128, 4 * NQB * NCC], F32)
    CM = A([128, 4, 512], BF16)
    WM = A([128, 8, 512], BF16)
    CPM = A([128, NQB * NCC, 512], BF16)
    E = A([128, NKC, 128], BF16)
    ADDT = A([128, NKC, NSLC], F32)
    w1b = [A([128, 32, 128], BF16) for _ in range(2)]
    w2b = [A([128, 128], BF16) for _ in range(2)]
    peT = [A([128, 32], BF16) for _ in range(2)]
    kcT = A([128, NCC * 128], BF16)
    Vc = A([128, NCC, VC + 1], BF16)
    OVLb = A([128, NCC, NSLC], BF16)
    qg_bc = A([128, 128], F32)
    kg_bc = A([128, 384], F32)
    cbias = A([128, 2], F32)
    hx = A([128, NCC * 128], F32)
    hx2 = A([128, NCC * 128], F32)
    hidT = A([128, NCC * 128], BF16)
    kvring = Ring(P, 2, [128, 768], F32)
    xb4 = A([128, 4, 128], BF16)
    junk = A([128, 512], BF16)
    qring = Ring(P, 2, [128, 512], F32)
    qn = A([128, 512], BF16)
    qTs = [A([128, 4, 512], BF16) for _ in range(2)]
    glt = A([128, 4, 12], F32)
    gsig = A([128, 4, 12], F32)
    Sp = Ring(P, 2, [128, 512], F32)
    Pb = Ring(P, 3, [128, 512], BF16)
    o_t = A([128, 4, 512], F32)
    imp = A([128, 4, NSLC], F32)
    impA = A([128, NSLC], F32)
    impB = A([128, NSLC], F32)
    m8 = A([128, 16], F32)
    selt = A([128, NSLC], F32)
    negm = A([128, 4, NSLC], BF16)
    negmT = A([128, 512], BF16)
    rv = A([128, 8], F32)
    kcn = A([128, 128], BF16)

    cs = P.dsem()
    for dst, key in ((negslope, "negslope"), (D0, "D0"), (D0c, "D0c"), (OFFB, "OFFB"), (OFFC, "OFFC"),
                     (ADDT.rearrange("p a b -> p (a b)"), "ADDT")):
        ctok = P.dma("sp", dst, d[key], cs)
    P.dma("sp", qg_bc, d["q_gain"].to_broadcast([128, 128]), cs)
    ctok = P.dma("sp", kg_bc, d["k_gain"].to_broadcast([128, 384]), cs)
    cs2 = P.dsem()
    for dst, key in ((CM, "CM"), (WM, "WM"), (CPM, "CPM"), (E, "E"), (OVLb, "OVL")):
        P.dma("pool", dst.rearrange("p a b -> p (a b)"), d[key], cs2)
    for kind in range(2):
        P.dma("pool", w1b[kind], d["cmp_w1"][kind].rearrange("l d e -> d l e"), cs2)
        P.dma("pool", w2b[kind], d["cmp_w2"][kind], cs2)
        c2tok = P.dma("pool", peT[kind], d["cmp_pos"][kind].rearrange("l d -> d l"), cs2,
                      allow_slow_non_contiguous=True)
    k = None
    for r in range(4):
        P.op("dve", lambda e, r=r: e.tensor_scalar(out=Ttab[:, r, :], in0=D0, scalar1=negslope[:, r:r + 1],
                                                   scalar2=None, op0=ALU.mult), waits=[ctok])
        k = P.op("dve", lambda e, r=r: e.tensor_scalar(out=Ttabc[:, r, :], in0=D0c, scalar1=negslope[:, r:r + 1],
                                                       scalar2=None, op0=ALU.mult), waits=[ctok])
    k = P.op("dve", lambda e: e.tensor_scalar(out=qg_bc, in0=qg_bc, scalar1=128.0 ** -0.5, scalar2=None,
                                              op0=ALU.mult), waits=[ctok])
    P.op("dve", lambda e: e.memset(Vs[:, :, 128:130], 1.0))
    P.op("dve", lambda e: e.memset(Vw[:, :, 128:130], 1.0))
    P.op("dve", lambda e: e.memset(hidT, 0.0))
    P.op("dve", lambda e: e.memset(Vc[:, :, 128:129], 1.0))
    P.op("dve", lambda e: e.tensor_copy(out=Vc[:, :, 129:129 + NSLC], in_=OVLb), waits=[c2tok])
    P.barrier()

    ss = C.ss

    def rstd_cols(c0, c1, n, waits):
        tk = P.op("act", lambda e: e.activation(out=ss[:, c0:c1], in_=ss[:, c0:c1], func=AF.Sqrt, scale=1.0 / n,
                                                bias=C.eps_t[:, 0:1]), waits=waits)
        return P.op("dve", lambda e: e.reciprocal(out=ss[:, c0:c1], in_=ss[:, c0:c1]), waits=[tk])

    def next_tp():
        pi = C.tp_i % len(C.tp)
        C.tp_i += 1
        return pi, C.tp[pi]

    kv = d["kv"]
    ss_free = None
    xb_free = None
    for t in range(NKC):
        i = kvring.next()
        kt = kvring.bufs[i]
        lt = P.dma("sp", kt, kv[t * 128:(t + 1) * 128, :], kvring.sems[i], waits=kvring.free[i] or [])
        P.op("act", lambda e, kt=kt: e.activation(out=junk[:, 0:128], in_=kt[:, 256:384], func=AF.Square,
                                                  accum_out=ss[:, 0:1]), waits=[lt, ss_free])
        a = P.op("act", lambda e, kt=kt: e.activation(out=junk[:, 128:256], in_=kt[:, 512:640], func=AF.Square,
                                                      accum_out=ss[:, 1:2]))
        rt = rstd_cols(0, 2, 128, [a])
        P.op("dve", lambda e, kt=kt: e.tensor_copy(out=xb4[:, 0:2, :].rearrange("p a b -> p (a b)"),
                                                   in_=kt[:, 0:256]), waits=[lt, xb_free])
        P.op("dve", lambda e, kt=kt: e.scalar_tensor_tensor(out=xb4[:, 2, :], in0=kt[:, 256:384], scalar=ss[:, 0:1],
                                                            in1=kg_bc[:, 128:256], op0=ALU.mult, op1=ALU.mult),
             waits=[rt])
        nk = P.op("dve", lambda e, kt=kt: e.scalar_tensor_tensor(out=xb4[:, 3, :], in0=kt[:, 512:640],
                                                                 scalar=ss[:, 1:2], in1=kg_bc[:, 256:384],
                                                                 op0=ALU.mult, op1=ALU.mult))
        ss_free = nk
        P.op("act", lambda e, kt=kt, t=t: e.activation(out=Vs[:, t, 0:128], in_=kt[:, 384:512], func=AF.Copy),
             waits=[lt])
        pk = P.op("act", lambda e, kt=kt, t=t: e.activation(out=Vw[:, t, 0:128], in_=kt[:, 640:768], func=AF.Copy))
        pi, tp = next_tp()
        for j in range(4):
            tk = P.op("pe", lambda e, tp=tp, j=j: e.transpose(out=tp[:, j, :], in_=xb4[:, j, :], identity=C.ident),
                      waits=[nk, C.tp_free[pi], C.ident_tok], sig=(j == 3))
        xb_free = tk
        ev = P.op("act", lambda e, tp=tp, t=t: e.activation(out=KT[:, :, t * 128:(t + 1) * 128], in_=tp[:, 0:4, :],
                                                            func=AF.Copy), waits=[tk])
        C.tp_free[pi] = ev
        kvring.free[i] = [ev, nk, pk]
    P.barrier()

    for kind in range(2):
        X = KT[:, kind, :]
        b0, b1 = C.banks[0], C.banks[1]
        for l in range(32):
            tk = P.op("pe", lambda e, l=l, kind=kind: e.matmul(out=b0[:, 0:1], lhsT=w1b[kind][:, l, :],
                                                               rhs=peT[kind][:, l:l + 1], start=(l == 0),
                                                               stop=(l == 31)), sig=(l == 31))
        cb = P.op("dve", lambda e, kind=kind: e.tensor_copy(out=cbias[:, kind:kind + 1], in_=b0[:, 0:1]), waits=[tk])
        for l in range(32):
            tk = P.op("pe", lambda e, l=l, kind=kind, X=X: e.matmul(
                out=b1[:, 0:n_cmp], lhsT=w1b[kind][:, l, :], rhs=X[:, l:l + 16 * (n_cmp - 1) + 1:16],
                start=(l == 0), stop=(l == 31)), sig=(l == 31))
        a = P.op("act", lambda e, kind=kind: e.activation(out=hx[:, 0:n_cmp], in_=b1[:, 0:n_cmp], func=AF.Identity,
                                                          bias=cbias[:, kind:kind + 1]), waits=[tk, cb])
        g = gelu_tanh(P, hx[:, 0:n_cmp], hx2[:, 0:n_cmp], [a])
        hk = P.op("dve", lambda e: e.tensor_copy(out=hidT[:, 0:n_cmp], in_=hx[:, 0:n_cmp]), waits=[g])
        for c in range(NCC):
            tk = P.op("pe", lambda e, c=c, kind=kind: e.matmul(out=b0[:, 0:128], lhsT=hidT[:, c * 128:(c + 1) * 128],
                                                               rhs=w2b[kind], start=True, stop=True), waits=[hk, cb])
            if kind == 0:
                a = P.op("act", lambda e: e.activation(out=junk[:, 0:128], in_=b0[:, 0:128], func=AF.Square,
                                                       accum_out=ss[:, 2:3]), waits=[tk])
                rt = rstd_cols(2, 3, 128, [a])
                nk = P.op("dve", lambda e: e.scalar_tensor_tensor(out=kcn, in0=b0[:, 0:128], scalar=ss[:, 2:3],
                                                                  in1=kg_bc[:, 0:128], op0=ALU.mult, op1=ALU.mult),
                          waits=[rt])
                pi, tp = next_tp()
                tk2 = P.op("pe", lambda e, tp=tp: e.transpose(out=tp[:, 0, :], in_=kcn, identity=C.ident),
                           waits=[nk, C.tp_free[pi]])
                ev = P.op("dve", lambda e, tp=tp, c=c: e.tensor_copy(out=kcT[:, c * 128:(c + 1) * 128],
                                                                     in_=tp[:, 0, :]), waits=[tk2])
                C.tp_free[pi] = ev
                cb = ev
            else:
                cb = P.op("dve", lambda e, c=c: e.tensor_copy(out=Vc[:, c, 0:128], in_=b0[:, 0:128]), waits=[tk])
        P.barrier()

    q_dram, gl_dram, ob_dram = d["q"], d["gl"], d["ob"]
    Sbank = [0, 1]
    state = {"si": 0, "pend": [], "o_free": None, "qT_free": [None, None], "gl_free": None, "imp_free": None,
             "negm_free": None, "ss_free": None, "qn_free": None}
    LAG = 2

    def emit_pv(tile):
        i2, ex_tok, pvs, V, after = tile
        pb = Pb.bufs[i2]
        tk = None
        for (acc, bank, j, start, stop) in pvs:
            tk = P.op("pe", lambda e, acc=acc, j=j, start=start, stop=stop, pb=pb, V=V: e.matmul(
                out=acc, lhsT=pb[:, j * 128:(j + 1) * 128], rhs=V, start=start, stop=stop,
                skip_group_check=True),
                waits=[ex_tok] + ([C.bank_free[bank]] if start else []))
        Pb.free[i2] = tk
        if after is not None:
            after(tk)

    def push(kT, qTr, aux, ttab, bias_ap, V, pvs, after, waits):
        b = Sbank[state["si"] % 2]
        state["si"] += 1
        S = C.banks[b]
        n = 1 + len(aux)
        tk = P.op("pe", lambda e: e.matmul(out=S, lhsT=kT, rhs=qTr, start=True, stop=(n == 1)),
                  waits=list(waits) + [C.bank_free[b]], sig=(n == 1))
        for ai, (al, ar) in enumerate(aux):
            tk = P.op("pe", lambda e, al=al, ar=ar, ai=ai: e.matmul(out=S, lhsT=al, rhs=ar, start=False,
                                                                    stop=(ai == n - 2)), sig=(ai == n - 2))
        i1 = Sp.next()
        sp_ = Sp.bufs[i1]
        dk = P.op("dve", lambda e: e.tensor_tensor(out=sp_, in0=S, in1=ttab, op=ALU.add), waits=[tk, Sp.free[i1]])
        C.bank_free[b] = dk
        i2 = Pb.next()
        pb = Pb.bufs[i2]
        ak = P.op("act", lambda e: e.activation(out=pb, in_=sp_, func=AF.Exp, bias=bias_ap), waits=[dk, Pb.free[i2]])
        Sp.free[i1] = ak
        state["pend"].append((i2, ak, pvs, V, after))
        while len(state["pend"]) > LAG:
            emit_pv(state["pend"].pop(0))

    def bank_starts(pvs, seen):
        out = []
        for (acc, bank, j, start, stop) in pvs:
            out.append((acc, bank, j, bank not in seen, stop))
            seen.add(bank)
        return out

    def flush():
        while state["pend"]:
            emit_pv(state["pend"].pop(0))

    def acc_aps(setidx, ncol):
        base = 2 + 2 * setidx
        return [(C.banks[base + j // 2][:, (j % 2) * ncol:(j % 2) * ncol + ncol], base + j // 2) for j in range(4)]

    def make_evac(r, br, accs, first):
        def after(pvtok):
            k = None
            for j in range(4):
                acc, bank = accs[j]
                k = P.op("dve", lambda e, acc=acc: e.tensor_scalar(out=rv[:, 0:1], in0=acc[:, 128:129], scalar1=1e-30,
                                                                   scalar2=None, op0=ALU.add), waits=[pvtok, k])
                k = P.op("dve", lambda e: e.reciprocal(out=rv[:, 0:1], in_=rv[:, 0:1]), waits=[k])
                if br == 0:
                    if r == 0:
                        k = P.op("dve", lambda e, acc=acc, j=j: e.tensor_scalar(
                            out=imp[:, j, :], in0=acc[:, 129:129 + NSLC], scalar1=rv[:, 0:1], scalar2=None,
                            op0=ALU.mult), waits=[k, state["imp_free"]])
                    else:
                        k = P.op("dve", lambda e, acc=acc, j=j: e.scalar_tensor_tensor(
                            out=imp[:, j, :], in0=acc[:, 129:129 + NSLC], scalar=rv[:, 0:1], in1=imp[:, j, :],
                            op0=ALU.mult, op1=ALU.add), waits=[k])
                k = P.op("dve", lambda e, j=j: e.tensor_tensor(out=rv[:, 1:2], in0=rv[:, 0:1],
                                                               in1=gsig[:, j, r * 3 + br:r * 3 + br + 1],
                                                               op=ALU.mult), waits=[k, state["gsig_tok"]])
                if first:
                    k = P.op("dve", lambda e, acc=acc, j=j: e.tensor_scalar(
                        out=o_t[:, j, r * 128:(r + 1) * 128], in0=acc[:, 0:128], scalar1=rv[:, 1:2], scalar2=None,
                        op0=ALU.mult), waits=[k, state["o_free"]])
                else:
                    k = P.op("dve", lambda e, acc=acc, j=j: e.scalar_tensor_tensor(
                        out=o_t[:, j, r * 128:(r + 1) * 128], in0=acc[:, 0:128], scalar=rv[:, 1:2],
                        in1=o_t[:, j, r * 128:(r + 1) * 128], op0=ALU.mult, op1=ALU.add), waits=[k])
                C.bank_free[bank] = k
            state["last_evac"] = k
        return after

    setsel = [0]
    for pq in range(NQB):
        qT = qTs[pq % 2]
        qtok = None
        for j in range(4):
            t = 4 * pq + j
            i = qring.next()
            qt_ = qring.bufs[i]
            lt = P.dma("sp", qt_, q_dram[t * 128:(t + 1) * 128, :], qring.sems[i], waits=[qring.free[i]])
            gt = P.dma("sp", glt[:, j, :], gl_dram[t * 128:(t + 1) * 128, :], qring.sems[i],
                       waits=[state["gl_free"]])
            a = None
            for r in range(4):
                a = P.op("act", lambda e, r=r, qt_=qt_: e.activation(
                    out=junk[:, r * 128:(r + 1) * 128], in_=qt_[:, r * 128:(r + 1) * 128], func=AF.Square,
                    accum_out=ss[:, r:r + 1]), waits=[gt, state["ss_free"]])
            rt = rstd_cols(0, 4, 128, [a])
            nk = None
            for r in range(4):
                nk = P.op("dve", lambda e, r=r, qt_=qt_: e.scalar_tensor_tensor(
                    out=qn[:, r * 128:(r + 1) * 128], in0=qt_[:, r * 128:(r + 1) * 128], scalar=ss[:, r:r + 1],
                    in1=qg_bc, op0=ALU.mult, op1=ALU.mult), waits=[rt, state["qn_free"]])
            state["ss_free"] = nk
            qring.free[i] = nk
            pi, tp = next_tp()
            for r in range(4):
                tk = P.op("pe", lambda e, tp=tp, r=r: e.transpose(out=tp[:, r, :], in_=qn[:, r * 128:(r + 1) * 128],
                                                                  identity=C.ident),
                          waits=[nk, C.tp_free[pi]], sig=(r == 3))
            state["qn_free"] = tk
            qtok = P.op("act", lambda e, tp=tp, j=j, qT=qT: e.activation(
                out=qT[:, :, j * 128:(j + 1) * 128], in_=tp[:, 0:4, :], func=AF.Copy),
                waits=[tk, state["qT_free"][pq % 2]])
            C.tp_free[pi] = qtok
        state["gsig_tok"] = P.op("act", lambda e: e.activation(out=gsig, in_=glt, func=AF.Sigmoid),
                                 waits=[gt, state.get("last_evac")])
        state["gl_free"] = state["gsig_tok"]

        for r in range(4):
            accs = acc_aps(setsel[0] % 2, VC)
            setsel[0] += 1
            cl = [c for c in range(NCC) if 511 + 512 * pq - 2048 * c - 31 >= 0]
            seen = set()
            for c in cl:
                idx = pq * NCC + c
                pvs = bank_starts([(accs[j][0], accs[j][1], j, c == cl[0], c == cl[-1]) for j in range(4)], seen)
                push(kcT[:, c * 128:(c + 1) * 128], qT[:, r, :], [(C.ident, CPM[:, idx, :])], Ttabc[:, r, :],
                     OFFC[:, r * NQB * NCC + idx:r * NQB * NCC + idx + 1], Vc[:, c, 0:VC], pvs,
                     make_evac(r, 0, accs, True) if c == cl[-1] else None, [qtok])
        for r in range(4):
            accs = acc_aps(setsel[0] % 2, 129)
            setsel[0] += 1
            cl = list(range(max(0, 4 * pq - 4), 4 * pq + 4))
            seen = set()
            for c in cl:
                dl = 4 * pq - c
                pvs = []
                for j in range(4):
                    if 4 * pq + j - 4 <= c <= 4 * pq + j:
                        pvs.append((accs[j][0], accs[j][1], j, c == max(0, 4 * pq + j - 4), c == 4 * pq + j))
                pvs = bank_starts(pvs, seen)
                push(KT[:, 3, c * 128:(c + 1) * 128], qT[:, r, :], [(C.ident, WM[:, 4 - dl, :])], Ttab[:, r, :],
                     OFFB[:, r * NOFF + dl + 3:r * NOFF + dl + 4], Vw[:, c, 0:129], pvs,
                     make_evac(r, 2, accs, False) if c == cl[-1] else None, [qtok])
        flush_needed = True
        flush()
        k = state["last_evac"]
        for j in range(4):
            t = 4 * pq + j
            k = P.op("dve", lambda e, j=j, t=t: e.tensor_tensor(out=impA, in0=imp[:, j, :], in1=ADDT[:, t, :],
                                                                op=ALU.add), waits=[k])
            k = P.op("dve", lambda e: e.max(out=m8[:, 0:8], in_=impA), waits=[k])
            k = P.op("dve", lambda e: e.match_replace(out=impB, in_to_replace=m8[:, 0:8], in_values=impA,
                                                      imm_value=-3.0e38), waits=[k])
            k = P.op("dve", lambda e: e.max(out=m8[:, 8:16], in_=impB), waits=[k])
            k = P.op("dve", lambda e: e.tensor_scalar(out=selt, in0=impA, scalar1=m8[:, 15:16], scalar2=None,
                                                      op0=ALU.is_ge), waits=[k])
            k = P.op("dve", lambda e, j=j: e.tensor_scalar(out=negm[:, j, :], in0=selt, scalar1=1.0, scalar2=-NEG,
                                                           op0=ALU.subtract, op1=ALU.mult),
                     waits=[k, state["negm_free"]])
        state["imp_free"] = k
        pi, tp = next_tp()
        for j in range(4):
            tk = P.op("pe", lambda e, tp=tp, j=j: e.transpose(out=tp[0:NSLC, j, :], in_=negm[:, j, :],
                                                              identity=C.ident),
                      waits=[k, C.tp_free[pi]], sig=(j == 3))
        state["negm_free"] = tk
        ntok = P.op("act", lambda e, tp=tp: e.activation(
            out=negmT[0:NSLC, :].rearrange("p (a b) -> p a b", a=4), in_=tp[0:NSLC, 0:4, :], func=AF.Copy),
            waits=[tk, state.get("negmT_free")])
        C.tp_free[pi] = ntok
        for r in range(4):
            accs = acc_aps(setsel[0] % 2, 129)
            setsel[0] += 1
            cl = list(range(4 * pq + 4))
            seen = set()
            for c in cl:
                dl = 4 * pq - c
                aux = [(E[0:NSLC, c, :], negmT[0:NSLC, :])]
                if dl <= 0:
                    aux.append((C.ident, CM[:, -dl, :]))
                pvs = []
                for j in range(4):
                    if c <= 4 * pq + j:
                        pvs.append((accs[j][0], accs[j][1], j, c == 0, c == 4 * pq + j))
                pvs = bank_starts(pvs, seen)
                push(KT[:, 2, c * 128:(c + 1) * 128], qT[:, r, :], aux, Ttab[:, r, :],
                     OFFB[:, r * NOFF + dl + 3:r * NOFF + dl + 4], Vs[:, c, 0:129], pvs,
                     make_evac(r, 1, accs, False) if c == cl[-1] else None, [qtok, ntok])
        flush()
        state["negmT_free"] = Tok(P.psem["pe"], P.cnt["pe"])
        state["qT_free"][pq % 2] = Tok(P.psem["pe"], P.cnt["pe"])
        if "osem" not in state:
            state["osem"] = P.dsem()
        osem = state["osem"]
        for j in range(4):
            t = 4 * pq + j
            stt = P.dma("sp", ob_dram[t * 128:(t + 1) * 128, :], o_t[:, j, :], osem, waits=[state["last_evac"]])
        state["o_free"] = stt
    P.barrier()
    P.top = mark
    P.sem_release(smark)


W_SHAPES = {"w_in": (4096, IN_COLS), "w_out": (4096, 4096), "w_mlp_in": (4096, D_FF), "w_mlp_out": (D_FF, 4096),
            "w_ple_gate": (4096, 4096), "w_ple_proj": (256, 4096)}
SMALL = {"mix_norm": 4096, "conv_w": 3072, "q_gain": 128, "k_gain": 384, "sgu_gain": 1024, "mix_out_norm": 4096,
         "mlp_norm": 4096, "ple_norm": 4096}
NQKV = 512 + 768 + 12
G8 = [list(range(8))]
G4 = [[0, 1, 2, 3], [4, 5, 6, 7]]


def collective(P, kind, src, dst, groups, waits, bg=False):
    P._wait("pool", waits)
    sem = P.dsem()
    sem.bg = bg
    sem.n += 1
    s_ = sem.sem
    P.q["pool"].append(lambda e: e.collective_compute(kind, ALU.bypass, replica_groups=groups, ins=[src.opt()],
                                                      outs=[dst.opt()]).then_inc(s_))
    return Tok(s_, sem.n)


def build_fused(depth=DEPTH, T=1024, TS=SEQ):
    P = PB(num_devices=8)
    nc = P.nc
    ext = {}

    def xin(name, shape):
        ext[name] = P.dram(name, shape, F32, "ExternalInput")
        return ext[name]

    x = xin("x", [T, 4096])
    p = xin("p", [depth, T, 256])
    wsh = {k: xin(k, [depth, K // 8, N]) for k, (K, N) in W_SHAPES.items()}
    sm = {k: xin(k, [depth, n]) for k, n in SMALL.items()}
    cmp_pos = xin("cmp_pos", [depth, 2, 32, 128])
    cmp_w1 = xin("cmp_w1", [depth, 2, 32, 128, 128])
    cmp_w2 = xin("cmp_w2", [depth, 2, 128, 128])
    w_sp = xin("w_sp", [depth, 8, 128, 128])
    b_sp = xin("b_sp", [depth, 8, 128])
    tabs = {k: xin(k, list(v.shape)) for k, v in nsa_tables(0, TS).items()}
    ident = xin("ident", [128, 128])
    tril = xin("tril", [128, 128])
    zeros = xin("zeros", [2, 3072])
    out = P.dram("out", [T, 4096], F32, "ExternalOutput")

    wfull = {k: [P.dram(f"{k}_f{i}", [K, N], BF16) for i in range(depth)] for k, (K, N) in W_SHAPES.items()}
    wstage = {k: [P.dram(f"{k}_s{i}", [K // 8, N], BF16) for i in range(depth)] for k, (K, N) in W_SHAPES.items()}
    hbuf = [P.dram(f"hbuf{i}", [T, 4096]) for i in range(2)]
    hs = P.dram("hs", [T, 4096])
    z = P.dram("z", [T, IN_COLS])
    zsend = P.dram("zsend", [T, 5168])
    zq_all = P.dram("zq_all", [4 * T, 5168])
    qkv_mine = P.dram("qkv_mine", [4 * T, NQKV])
    halo_send = P.dram("halo_send", [2, 3072])
    halo_pad = P.dram("halo_pad", [10, 3072])
    halo_mine = P.dram("halo_mine", [2, 3072])
    ob_mine = P.dram("ob_mine", [4 * T, 512])
    ob_all = P.dram("ob_all", [16 * T, 512])
    ob_tok = P.dram("ob_tok", [T, 2048])

    C = Ctx(P, T, ident)
    C.w_cast = False
    gtok = {}
    stsem = P.dsem()
    stsem.bg = True

    def gather_w(k, i):
        t = P.dma("pool", wstage[k][i], wsh[k][i], stsem)
        gtok[(k, i)] = collective(P, "AllGather", wstage[k][i], wfull[k][i], G8, [t], bg=True)

    misc = P.dsem()
    P.dma("sp", halo_pad[0:2, :], zeros, misc)
    gather_w("w_in", 0)
    gather_w("w_out", 0)
    SPE = mybir.EngineType.SP
    rk = {}

    for i in range(depth):
        h_in = x if i == 0 else hbuf[(i - 1) % 2]
        h_out = out if i == depth - 1 else hbuf[i % 2]
        C.w_ready = [gtok[("w_in", i)]]
        phase_A(P, C, h_in, sm["mix_norm"][i:i + 1, :], wfull["w_in"][i], z)
        C.w_ready = []
        P.dma("sp", zsend, z[:, OFF_Q:OFF_GU], misc)
        t1 = P.dma("sp", halo_send, z[T - 2:T, 0:3072], misc)
        x1 = collective(P, "AllGather", zsend, zq_all, G4, [t1])
        x1h = collective(P, "AllGather", halo_send, halo_pad[2:10, :], G4, [t1])
        gather_w("w_mlp_in", i)
        P._wait("sp", [x1, x1h])

        def relayout(e):
            if "rank" not in rk:
                rk["rank"] = e.snap(nc.partition_id([SPE]) % 4, min_val=0, max_val=3)
            rank = rk["rank"]
            e.dma_start(out=qkv_mine[:, 0:512], in_=zq_all[:, bass.ds(rank * 512, 512)]).then_inc(misc.sem, 16)
            e.dma_start(out=qkv_mine[:, 512:1280].rearrange("t (b x) -> t b x", b=6),
                        in_=zq_all[:, 2048:5120].rearrange("t (b x) -> t b x", b=6)[:, :, bass.ds(rank * 128, 128)]
                        ).then_inc(misc.sem, 16)
            e.dma_start(out=qkv_mine[:, 1280:1292], in_=zq_all[:, bass.ds(5120 + rank * 12, 12)]
                        ).then_inc(misc.sem, 16)
            e.dma_start(out=halo_mine, in_=halo_pad[bass.ds(rank * 2, 2), :]).then_inc(misc.sem, 16)

        P.q["sp"].append(relayout)
        misc.n += 64
        P.barrier()
        dB = dict(tabs)
        dB.update({"q": qkv_mine[:, 0:512], "kv": qkv_mine[:, 512:1280], "gl": qkv_mine[:, 1280:1292], "ob": ob_mine,
                   "q_gain": sm["q_gain"][i:i + 1, :], "k_gain": sm["k_gain"][i:i + 1, :], "cmp_pos": cmp_pos[i],
                   "cmp_w1": cmp_w1[i], "cmp_w2": cmp_w2[i]})
        phase_B(P, C, dB, TS)
        x2 = collective(P, "AllGather", ob_mine, ob_all, G4, [])
        gather_w("w_mlp_out", i)
        gather_w("w_ple_gate", i)
        gather_w("w_ple_proj", i)
        if i + 1 < depth:
            gather_w("w_in", i + 1)
            gather_w("w_out", i + 1)
        P._wait("sp", [x2])

        def relayout2(e):
            rank = rk["rank"]
            src = ob_all.rearrange("(r j t) c -> j r t c", r=4, j=4)[bass.ds(rank, 1), :, :, :]
            e.dma_start(out=ob_tok.rearrange("(o t) (r c) -> o r t c", o=1, r=4), in_=src).then_inc(misc.sem, 16)

        P.q["sp"].append(relayout2)
        misc.n += 16
        P.barrier()
        dC = {"h_in": h_in, "hs": hs, "h_out": h_out, "zc": z[:, 0:3072], "halo": halo_mine,
              "zg": z[:, OFF_GU:IN_COLS], "ob": ob_tok, "p": p[i], "conv_w": sm["conv_w"][i:i + 1, :],
              "sgu_gain": sm["sgu_gain"][i:i + 1, :], "w_sp": w_sp[i], "b_sp": b_sp[i], "tril": tril,
              "mix_out_norm": sm["mix_out_norm"][i:i + 1, :], "mlp_norm": sm["mlp_norm"][i:i + 1, :],
              "ple_norm": sm["ple_norm"][i:i + 1, :]}
        for k in ("w_out", "w_mlp_in", "w_mlp_out", "w_ple_gate", "w_ple_proj"):
            dC[k] = wfull[k][i]
            dC["tok_" + k] = [gtok[(k, i)]]
        phase_C(P, C, dC)
    return P.finish()


_NC_CACHE = {}


def _ext(P, name, shape):
    return P.dram(name, list(shape), F32, "ExternalInput")


def build_A():
    P = PB()
    h = _ext(P, "h", [1024, 4096])
    gain = _ext(P, "gain", [1, 4096])
    w = _ext(P, "w", [4096, IN_COLS])
    ident = _ext(P, "ident", [128, 128])
    z = P.dram("z", [1024, IN_COLS], F32, "ExternalOutput")
    C = Ctx(P, 1024, ident)
    phase_A(P, C, h, gain, w, z)
    return P.finish()


B_SHAPES = {"q": (SEQ, 512), "kv": (SEQ, 768), "gl": (SEQ, 12), "q_gain": (1, 128), "k_gain": (1, 384),
            "cmp_pos": (2, 32, 128), "cmp_w1": (2, 32, 128, 128), "cmp_w2": (2, 128, 128), "ident": (128, 128)}


def build_B():
    P = PB()
    d = {k: _ext(P, k, v.shape) for k, v in nsa_tables(0).items()}
    d.update({k: _ext(P, k, shp) for k, shp in B_SHAPES.items()})
    d["ob"] = P.dram("ob", [SEQ, 512], F32, "ExternalOutput")
    C = Ctx(P, SEQ, d["ident"])
    phase_B(P, C, d, SEQ)
    return P.finish()


C_SHAPES = {"h_in": (1024, 4096), "zc": (1024, 3072), "halo": (2, 3072), "zg": (1024, 2048), "ob": (1024, 2048),
            "p": (1024, 256), "conv_w": (1, 3072), "sgu_gain": (1, 1024), "w_sp": (8, 128, 128), "b_sp": (8, 128),
            "tril": (128, 128), "mix_out_norm": (1, 4096), "w_out": (4096, 4096), "mlp_norm": (1, 4096),
            "w_mlp_in": (4096, D_FF), "w_mlp_out": (D_FF, 4096), "ple_norm": (1, 4096), "w_ple_proj": (256, 4096),
            "w_ple_gate": (4096, 4096), "ident": (128, 128)}


def build_C():
    P = PB()
    d = {k: _ext(P, k, shp) for k, shp in C_SHAPES.items()}
    d["h_out"] = P.dram("h_out", [1024, 4096], F32, "ExternalOutput")
    d["hs"] = P.dram("hs", [1024, 4096], F32, "Internal")
    C = Ctx(P, 1024, d["ident"])
    phase_C(P, C, d)
    return P.finish()


def _prog(name, fn):
    if name not in _NC_CACHE:
        _NC_CACHE[name] = fn()
    return _NC_CACHE[name]


def kernel(**inputs):
    f32 = np.float32
    g = lambda k: np.asarray(inputs[k], dtype=f32)
    cores = list(range(8))
    ident = np.eye(128, dtype=f32)
    tril = np.tril(np.ones((128, 128), f32))
    tabs = [nsa_tables(gi) for gi in range(4)]
    h = np.ascontiguousarray(g("x")).reshape(8, 1024, 4096)
    p = g("p")
    ncA, ncB, ncC = _prog("A", build_A), _prog("B", build_B), _prog("C", build_C)
    for i in range(DEPTH):
        w_in = np.ascontiguousarray(g("w_in")[i])
        gain = g("mix_norm")[i].reshape(1, 4096)
        res = run_bass_kernel_spmd(ncA, [{"h": h[c], "gain": gain, "w": w_in, "ident": ident} for c in cores],
                                   core_ids=cores)
        zb = np.stack([np.asarray(r["z"]) for r in res.results]).reshape(BATCH, SEQ, IN_COLS)
        del res, w_in
        shB = {"q_gain": g("q_gain")[i].reshape(1, 128), "k_gain": g("k_gain")[i].reshape(1, 384),
               "cmp_pos": np.ascontiguousarray(g("cmp_pos")[i]), "cmp_w1": np.ascontiguousarray(g("cmp_w1")[i]),
               "cmp_w2": np.ascontiguousarray(g("cmp_w2")[i]), "ident": ident}
        insB = []
        for c in cores:
            b, gi = c // 4, c % 4
            m = dict(shB)
            m.update(tabs[gi])
            m["q"] = np.ascontiguousarray(zb[b, :, OFF_Q + gi * 512:OFF_Q + (gi + 1) * 512])
            m["kv"] = np.ascontiguousarray(np.concatenate(
                [zb[b, :, OFF_KV + br * 512 + gi * 128:OFF_KV + br * 512 + (gi + 1) * 128] for br in range(6)], -1))
            m["gl"] = np.ascontiguousarray(zb[b, :, OFF_GATE + gi * 12:OFF_GATE + (gi + 1) * 12])
            insB.append(m)
        res = run_bass_kernel_spmd(ncB, insB, core_ids=cores)
        obb = np.zeros((BATCH, SEQ, 2048), f32)
        for c in cores:
            b, gi = c // 4, c % 4
            obb[b, :, gi * 512:(gi + 1) * 512] = np.asarray(res.results[c]["ob"])
        del res, insB
        shC = {"conv_w": g("conv_w")[i].reshape(1, 3072), "sgu_gain": g("sgu_gain")[i].reshape(1, 1024),
               "w_sp": np.ascontiguousarray(g("w_sp")[i]), "b_sp": np.ascontiguousarray(g("b_sp")[i]), "tril": tril,
               "mix_out_norm": g("mix_out_norm")[i].reshape(1, 4096), "w_out": np.ascontiguousarray(g("w_out")[i]),
               "mlp_norm": g("mlp_norm")[i].reshape(1, 4096), "w_mlp_in": np.ascontiguousarray(g("w_mlp_in")[i]),
               "w_mlp_out": np.ascontiguousarray(g("w_mlp_out")[i]), "ple_norm": g("ple_norm")[i].reshape(1, 4096),
               "w_ple_proj": np.ascontiguousarray(g("w_ple_proj")[i]),
               "w_ple_gate": np.ascontiguousarray(g("w_ple_gate")[i]), "ident": ident}
        insC = []
        for c in cores:
            b, j = c // 4, c % 4
            t0 = j * 1024
            m = dict(shC)
            m["h_in"] = h[c]
            m["zc"] = np.ascontiguousarray(zb[b, t0:t0 + 1024, 0:3072])
            m["halo"] = (np.ascontiguousarray(zb[b, t0 - 2:t0, 0:3072]) if j > 0 else np.zeros((2, 3072), f32))
            m["zg"] = np.ascontiguousarray(zb[b, t0:t0 + 1024, OFF_GU:IN_COLS])
            m["ob"] = np.ascontiguousarray(obb[b, t0:t0 + 1024, :])
            m["p"] = np.ascontiguousarray(p[i, b, t0:t0 + 1024, :])
            insC.append(m)
        res = run_bass_kernel_spmd(ncC, insC, core_ids=cores)
        h = np.stack([np.asarray(r["h_out"]) for r in res.results])
        del res, insC, shC, zb, obb
    return np.ascontiguousarray(h.reshape(BATCH, SEQ, D_MODEL)).astype(f32)
```

```python
import contextlib
import numpy as np
import concourse.bass as bass
import concourse.mybir as mybir
from concourse.bass_utils import run_bass_kernel_spmd

F32 = mybir.dt.float32
BF16 = mybir.dt.bfloat16
U8 = mybir.dt.uint8
AF = mybir.ActivationFunctionType
ALU = mybir.AluOpType

ENGS = ("pe", "act", "dve", "pool", "sp")
EPS = 1e-6
ARENA_BYTES = 207 * 1024
NEG = -30000.0

D_MODEL = 4096
SEQ = 4096
BATCH = 2
DEPTH = 4
IN_COLS = 10288
D_FF = 16384
OFF_Q, OFF_KV, OFF_GATE, OFF_GU = 3072, 5120, 8192, 8240


class Tok:
    __slots__ = ("sem", "val")

    def __init__(self, sem, val):
        self.sem = sem
        self.val = val


class DmaSem:
    def __init__(self, sem):
        self.sem = sem
        self.n = 0
        self.bg = False


class PB:
    def __init__(self, num_devices=None):
        if num_devices is None:
            self.nc = bass.Bass("TRN2", target_bir_lowering=False)
        else:
            self.nc = bass.Bass("TRN2", target_bir_lowering=False, num_devices=num_devices)
        self.es = contextlib.ExitStack()
        self.q = {e: [] for e in ENGS}
        self.cnt = {e: 0 for e in ENGS}
        self.psem = {e: self.es.enter_context(self.nc.semaphore("prog_" + e)) for e in ENGS}
        self.waited = {e: {} for e in ENGS}
        self.dsems = []
        self.sem_free = []
        self.sem_live = []
        self.nsem = 0
        self.ntens = 0
        self.arena = self.es.enter_context(self.nc.sbuf_tensor("arena", [128, ARENA_BYTES], U8))
        self.top = 0
        self.ndram = 0

    def view(self, off, shape, dt):
        esz = 4 if dt == F32 else 2
        nb = int(np.prod(shape[1:])) * esz
        assert off % 4 == 0 and off + nb <= ARENA_BYTES, (off, nb)
        a = self.arena[0:shape[0], off:off + nb].bitcast(dt)
        if len(shape) == 3:
            a = a.rearrange("p (a b) -> p a b", a=shape[1])
        elif len(shape) == 4:
            a = a.rearrange("p (a b c) -> p a b c", a=shape[1], b=shape[2])
        return a

    def alloc(self, shape, dt):
        esz = 4 if dt == F32 else 2
        nb = int(np.prod(shape[1:])) * esz
        nb = (nb + 31) // 32 * 32
        off = self.top
        self.top += nb
        assert self.top <= ARENA_BYTES, ("arena overflow", self.top)
        return self.view(off, shape, dt)

    def ps(self, shape, dt, name=None):
        self.ntens += 1
        return self.es.enter_context(self.nc.psum_tensor(name or f"ps{self.ntens}", list(shape), dt))[:]

    def dsem(self, name=None):
        if self.sem_free:
            d = self.sem_free.pop()
        else:
            self.nsem += 1
            d = DmaSem(self.es.enter_context(self.nc.semaphore(f"s{self.nsem}")))
            self.dsems.append(d)
        self.sem_live.append(d)
        return d

    def sem_mark(self):
        return len(self.sem_live)

    def sem_release(self, mark):
        while len(self.sem_live) > mark:
            self.sem_free.append(self.sem_live.pop())

    def dram(self, name, shape, dt=F32, kind="Internal"):
        return self.nc.dram_tensor(name, list(shape), dt, kind=kind).ap()

    def _wait(self, eng, toks):
        w = self.waited[eng]
        for t in toks:
            if t is None:
                continue
            key = id(t.sem)
            if w.get(key, 0) >= t.val:
                continue
            w[key] = t.val
            self.q[eng].append(lambda e, sem=t.sem, val=t.val: e.wait_ge(sem, val))

    def op(self, eng, fn, waits=(), sig=True):
        self._wait(eng, waits)
        if sig:
            self.cnt[eng] += 1
            sem = self.psem[eng]
            self.q[eng].append(lambda e, fn=fn, sem=sem: fn(e).then_inc(sem, 1))
            return Tok(sem, self.cnt[eng])
        self.q[eng].append(lambda e, fn=fn: fn(e))
        return None

    def dma(self, eng, out, in_, sem, waits=(), **kw):
        self._wait(eng, waits)
        sem.n += 16
        s = sem.sem
        self.q[eng].append(
            lambda e, out=out, in_=in_, s=s, kw=kw: e.dma_start(out=out, in_=in_, **kw).then_inc(s, 16))
        return Tok(s, sem.n)

    def barrier(self, everything=False):
        toks = [Tok(self.psem[e], self.cnt[e]) for e in ENGS if self.cnt[e] > 0]
        toks += [Tok(d.sem, d.n) for d in self.dsems if d.n > 0 and (everything or not d.bg)]
        for e in ENGS:
            self._wait(e, toks)

    def finish(self):
        self.barrier(everything=True)
        nc, q = self.nc, self.q
        with nc.Block() as block:
            @block.tensor
            def _(e):
                for f in q["pe"]:
                    f(e)

            @block.scalar
            def _(e):
                for f in q["act"]:
                    f(e)

            @block.vector
            def _(e):
                for f in q["dve"]:
                    f(e)

            @block.gpsimd
            def _(e):
                for f in q["pool"]:
                    f(e)

            @block.sync
            def _(e):
                for f in q["sp"]:
                    f(e)
        self.es.close()
        return nc


class Ring:
    def __init__(self, P, n, shape, dt):
        self.bufs = [P.alloc(shape, dt) for _ in range(n)]
        self.sems = [P.dsem() for _ in range(n)]
        self.free = [None] * n
        self.i = 0

    def next(self):
        i = self.i % len(self.bufs)
        self.i += 1
        return i


class Ctx:
    def __init__(self, P, T, ident_dram, nbanks=6, wslots=True):
        self.P = P
        self.T = T
        self.NT = T // 128
        self.ident = P.alloc([128, 128], BF16)
        self.ident_tok = P.dma("pool", self.ident, ident_dram, P.dsem())
        self.banks = [P.ps([128, 512], F32, f"bank{i}") for i in range(nbanks)]
        self.bank_free = [None] * nbanks
        self.tp = [P.ps([128, 8, 128], BF16, f"tp{i}") for i in range(8 - nbanks)]
        self.tp_free = [None] * (8 - nbanks)
        self.tp_i = 0
        self.eps_t = P.alloc([128, 1], F32)
        P.op("dve", lambda e: e.memset(self.eps_t, EPS))
        self.ss = P.alloc([128, 8], F32)
        self.ss_free = None
        self.w_cast = True
        self.w_ready = []
        self.wsem = [P.dsem() for _ in range(2)]

    def alloc_wslots(self):
        P = self.P
        self.wbase = P.top
        self.wslots = [P.alloc([128, 32, 512], BF16) for _ in range(2)]
        self.wfree = [None, None]
        self.wnext = 0


def load_wblock(C, w_ap, k0, KC, n0, nw):
    P = C.P
    i = C.wnext
    C.wnext = (C.wnext + 1) % len(C.wslots)
    src = w_ap[k0 * 128:(k0 + KC) * 128, n0:n0 + nw].rearrange("(k p) n -> p k n", p=128)
    q = "pool" if C.w_cast else "sp"
    tok = P.dma(q, C.wslots[i][:, 0:KC, 0:nw], src, C.wsem[i], waits=[C.wfree[i]] + list(C.w_ready))
    return i, tok


def rstd_from_ss(C, col, n, waits):
    P = C.P
    ss = C.ss
    tk = P.op("act", lambda e: e.activation(out=ss[:, col:col + 1], in_=ss[:, col:col + 1], func=AF.Sqrt,
                                            scale=1.0 / n, bias=C.eps_t[:, 0:1]), waits=waits)
    return P.op("dve", lambda e: e.reciprocal(out=ss[:, col:col + 1], in_=ss[:, col:col + 1]), waits=[tk])


def norm_transpose(C, produce, segs, gain_bc, gain_tok, actT, xt, xb):
    P = C.P
    W = segs[-1][1]
    KC = W // 128
    ss = C.ss
    xt_free = None
    xb_free = None
    last = []
    for t in range(C.NT):
        ltoks = produce(t, xt, xt_free)
        stoks = []
        for si, (c0, c1) in enumerate(segs):
            stoks.append(P.op("act", lambda e, c0=c0, c1=c1, si=si: e.activation(
                out=xb[:, c0:c1], in_=xt[:, c0:c1], func=AF.Square, accum_out=ss[:, si:si + 1]),
                waits=list(ltoks) + [C.ss_free, xb_free]))
        rts = [rstd_from_ss(C, si, c1 - c0, [stoks[si]]) for si, (c0, c1) in enumerate(segs)]
        ntoks = []
        for si, (c0, c1) in enumerate(segs):
            ntoks.append(P.op("dve", lambda e, c0=c0, c1=c1, si=si: e.scalar_tensor_tensor(
                out=xb[:, c0:c1], in0=xt[:, c0:c1], scalar=ss[:, si:si + 1], in1=gain_bc[:, c0:c1],
                op0=ALU.mult, op1=ALU.mult), waits=[rts[-1], gain_tok, stoks[-1]]))
        xt_free = ntoks[-1]
        C.ss_free = ntoks[-1]
        for c8 in range(KC // 8):
            pi = C.tp_i % len(C.tp)
            C.tp_i += 1
            tp = C.tp[pi]
            for j in range(8):
                c = c8 * 8 + j
                tk = P.op("pe", lambda e, tp=tp, j=j, c=c: e.transpose(
                    out=tp[:, j, :], in_=xb[:, c * 128:(c + 1) * 128], identity=C.ident),
                    waits=[ntoks[-1], C.tp_free[pi], C.ident_tok], sig=(j == 7))
            dst = actT[:, c8 * 8:(c8 + 1) * 8, t * 128:(t + 1) * 128]
            if c8 % 2 == 0:
                ev = P.op("act", lambda e, tp=tp, dst=dst: e.activation(out=dst, in_=tp, func=AF.Copy), waits=[tk])
            else:
                ev = P.op("dve", lambda e, tp=tp, dst=dst: e.tensor_copy(out=dst, in_=tp), waits=[tk])
            C.tp_free[pi] = ev
            last.append(ev)
        xb_free = tk
    return last[-2:]


def gemm_T(C, actT, KC, w_ap, k0, N, act_ready, epilogue, banks=(0, 1, 2), n_off=0):
    P = C.P
    nblocks = [(n0, min(512, N - n0)) for n0 in range(0, N, 512)]
    it = iter(nblocks)
    pend = []
    nb = next(it, None)
    if nb is not None:
        pend.append(load_wblock(C, w_ap, k0, KC, nb[0] + n_off, nb[1]))
    bsel = 0
    tk = None
    for (n0, nw) in nblocks:
        nb = next(it, None)
        if nb is not None:
            pend.append(load_wblock(C, w_ap, k0, KC, nb[0] + n_off, nb[1]))
        slot, wtok = pend.pop(0)
        ws = C.wslots[slot]
        for t in range(C.NT):
            b = banks[bsel % len(banks)]
            bsel += 1
            ps = C.banks[b]
            for k in range(KC):
                tk = P.op("pe", lambda e, ps=ps, k=k, t=t, ws=ws, nw=nw: e.matmul(
                    out=ps[:, 0:nw], lhsT=actT[:, k, t * 128:(t + 1) * 128], rhs=ws[:, k, 0:nw],
                    start=(k == 0), stop=(k == KC - 1)),
                    waits=[wtok, C.bank_free[b]] + list(act_ready), sig=(k == KC - 1))
            C.bank_free[b] = epilogue(t, n0, nw, ps, tk)
        C.wfree[slot] = tk
    return tk


def gemm_F(C, actT, KC, w_ap, k0, N, act_ready, epilogue, banks=(3, 4, 5), n_off=0):
    P = C.P
    nblocks = [(n0, min(512, N - n0)) for n0 in range(0, N, 512)]
    it = iter(nblocks)
    pend = []
    nb = next(it, None)
    if nb is not None:
        pend.append(load_wblock(C, w_ap, k0, KC, nb[0] + n_off, nb[1]))
    bsel = 0
    TH = C.T // 512
    tk = None
    for (n0, nw) in nblocks:
        nb = next(it, None)
        if nb is not None:
            pend.append(load_wblock(C, w_ap, k0, KC, nb[0] + n_off, nb[1]))
        slot, wtok = pend.pop(0)
        ws = C.wslots[slot]
        for j in range(nw // 128):
            for th in range(TH):
                b = banks[bsel % len(banks)]
                bsel += 1
                ps = C.banks[b]
                for k in range(KC):
                    tk = P.op("pe", lambda e, ps=ps, k=k, th=th, ws=ws, j=j: e.matmul(
                        out=ps, lhsT=ws[:, k, j * 128:(j + 1) * 128], rhs=actT[:, k, th * 512:(th + 1) * 512],
                        start=(k == 0), stop=(k == KC - 1)),
                        waits=[wtok, C.bank_free[b]] + list(act_ready), sig=(k == KC - 1))
                C.bank_free[b] = epilogue(n0 // 128 + j, th, ps, tk)
        C.wfree[slot] = tk
    return tk


def phase_A(P, C, h, gain, w_in, z, K=D_MODEL, N=IN_COLS):
    T = C.T
    KC = K // 128
    mark = P.top
    smark = P.sem_mark()
    C.alloc_wslots()
    actT = P.alloc([128, KC, T], BF16)
    gain_bc = P.alloc([128, K], F32)
    xt = P.alloc([128, K], F32)
    xb = P.alloc([128, K], BF16)
    oring = Ring(P, 4, [128, 512], F32)
    gtok = P.dma("sp", gain_bc, gain.to_broadcast([128, K]), P.dsem())
    xsem = P.dsem()

    def produce(t, xt, free):
        return [P.dma("sp", xt, h[t * 128:(t + 1) * 128, :], xsem, waits=[free])]

    ready = norm_transpose(C, produce, [(0, K)], gain_bc, gtok, actT, xt, xb)

    def epi(t, n0, nw, ps, mmtok):
        i = oring.next()
        ob = oring.bufs[i]
        tk = P.op("act", lambda e: e.activation(out=ob[:, 0:nw], in_=ps[:, 0:nw], func=AF.Copy),
                  waits=[mmtok, oring.free[i]])
        oring.free[i] = P.dma("sp", z[t * 128:(t + 1) * 128, n0:n0 + nw], ob[:, 0:nw], oring.sems[i], waits=[tk])
        return tk

    gemm_T(C, actT, KC, w_in, 0, N, ready, epi)
    P.barrier()
    P.top = mark
    P.sem_release(smark)


class SubAlloc:
    def __init__(self, P, base, size):
        self.P, self.base, self.size, self.top = P, base, size, 0

    def alloc(self, shape, dt):
        esz = 4 if dt == F32 else 2
        nb = (int(np.prod(shape[1:])) * esz + 31) // 32 * 32
        off = self.base + self.top
        self.top += nb
        assert self.top <= self.size, ("suballoc overflow", self.top, self.size)
        return self.P.view(off, shape, dt)


def gelu_tanh(P, x, tmp, waits):
    t1 = P.op("dve", lambda e: e.tensor_tensor(out=tmp, in0=x, in1=x, op=ALU.mult), waits=waits)
    t2 = P.op("dve", lambda e: e.tensor_scalar(out=tmp, in0=tmp, scalar1=0.044715, scalar2=1.0,
                                               op0=ALU.mult, op1=ALU.add), waits=[t1])
    t3 = P.op("dve", lambda e: e.tensor_tensor(out=tmp, in0=tmp, in1=x, op=ALU.mult), waits=[t2])
    t4 = P.op("act", lambda e: e.activation(out=tmp, in_=tmp, func=AF.Sigmoid, scale=1.5957691216057308),
              waits=[t3])
    return P.op("dve", lambda e: e.tensor_tensor(out=x, in0=x, in1=tmp, op=ALU.mult), waits=[t4])


def phase_C(P, C, d, n_ffq=8):
    T, NT = C.T, C.NT
    mark = P.top
    smark = P.sem_mark()
    C.alloc_wslots()
    actT = P.alloc([128, 32, T], BF16)
    r1 = P.top
    P.top += 32 * 1024
    xt = P.view(r1, [128, 4096], F32)
    gain_bc = P.view(r1 + 16384, [128, 4096], F32)
    h1T = P.view(r1, [128, 16, T], BF16)
    wp = P.view(r1, [128, 2, 4096], BF16)
    xb = P.alloc([128, 4096], BF16)
    oring = Ring(P, 4, [128, 512], F32)
    hring = Ring(P, 4, [128, 512], F32)
    rring = Ring(P, 2, [128, 512], F32)
    pT = P.alloc([128, 2, T], BF16)
    pld = P.alloc([128, 256], F32)
    pbf = P.alloc([128, 256], BF16)

    S = SubAlloc(P, C.wbase, 64 * 1024)
    convw = S.alloc([128, 3072], F32)
    Xs = [S.alloc([128, 512], F32) for _ in range(3)]
    Cs = [S.alloc([128, 512], F32) for _ in range(3)]
    Bt = S.alloc([128, 512], F32)
    cacc = S.alloc([128, 512], F32)
    U = S.alloc([128, 1024], F32)
    V = S.alloc([128, 1024], F32)
    gtmp = S.alloc([128, 1024], F32)
    Vb = S.alloc([128, 1024], BF16)
    sgu_bc = S.alloc([128, 1024], F32)
    wsp = S.alloc([128, 8, 128], F32)
    wspb = S.alloc([128, 8, 128], BF16)
    WsT = S.alloc([128, 8, 128], BF16)
    tril = S.alloc([128, 128], F32)
    bsp = S.alloc([128, 8], F32)

    csem = P.dsem()
    P.dma("sp", convw, d["conv_w"].to_broadcast([128, 3072]), csem)
    P.dma("sp", sgu_bc, d["sgu_gain"].to_broadcast([128, 1024]), csem)
    P.dma("sp", wsp, d["w_sp"].rearrange("g t s -> t g s"), csem)
    P.dma("sp", tril, d["tril"], csem)
    ctok = P.dma("sp", bsp, d["b_sp"].rearrange("g t -> t g"), csem, allow_slow_non_contiguous=True)
    gtok = P.dma("sp", gain_bc, d["mix_out_norm"].to_broadcast([128, 4096]), P.dsem())
    tk = None
    for g in range(8):
        tk = P.op("dve", lambda e, g=g: e.tensor_tensor(out=wspb[:, g, :], in0=wsp[:, g, :], in1=tril, op=ALU.mult),
                  waits=[ctok])
    tp = C.tp[0]
    for g in range(8):
        tk2 = P.op("pe", lambda e, g=g: e.transpose(out=tp[:, g, :], in_=wspb[:, g, :], identity=C.ident),
                   waits=[tk, C.ident_tok], sig=(g == 7))
    wst_tok = P.op("dve", lambda e: e.tensor_copy(out=WsT, in_=tp), waits=[tk2])
    C.tp_free[0] = wst_tok

    xsem = P.dsem()
    osem = P.dsem()
    gsem = P.dsem()
    st = {"conv_free": None, "gm_free": None, "gss_free": None}
    zc, zg, halo, ob = d["zc"], d["zg"], d["halo"], d["ob"]

    def produce(t, xt, free):
        toks = [P.dma("sp", xt[:, 1024:3072], ob[t * 128:(t + 1) * 128, :], osem, waits=[free])]
        for hc in range(2):
            c0 = hc * 512
            lt = None
            for s in range(3):
                for buf, coff in ((Xs[s], 0), (Cs[s], 2048)):
                    if t == 0 and s > 0:
                        P.dma("sp", buf[0:s, :], halo[2 - s:2, coff + c0:coff + c0 + 512], xsem,
                              waits=[st["conv_free"]])
                        lt = P.dma("sp", buf[s:128, :], zc[0:128 - s, coff + c0:coff + c0 + 512], xsem)
                    else:
                        lt = P.dma("sp", buf, zc[t * 128 - s:t * 128 - s + 128, coff + c0:coff + c0 + 512], xsem,
                                   waits=[st["conv_free"]])
            lt = P.dma("sp", Bt, zc[t * 128:(t + 1) * 128, 1024 + c0:1024 + c0 + 512], xsem)
            k = None
            for s in range(3):
                k = P.op("dve", lambda e, s=s: e.tensor_tensor(out=Xs[s], in0=Xs[s], in1=Cs[s], op=ALU.mult),
                         waits=[lt, k])
            k = P.op("dve", lambda e, c0=c0: e.tensor_tensor(
                out=cacc, in0=Xs[0], in1=convw[:, 2048 + c0:2048 + c0 + 512], op=ALU.mult), waits=[k, ctok])
            for s in (1, 2):
                k = P.op("dve", lambda e, s=s, c0=c0: e.tensor_tensor(
                    out=Xs[s], in0=Xs[s], in1=convw[:, (2 - s) * 1024 + c0:(2 - s) * 1024 + c0 + 512],
                    op=ALU.mult), waits=[k])
                k = P.op("dve", lambda e, s=s: e.tensor_tensor(out=cacc, in0=cacc, in1=Xs[s], op=ALU.add),
                         waits=[k])
            k = P.op("dve", lambda e, c0=c0: e.tensor_tensor(out=xt[:, c0:c0 + 512], in0=cacc, in1=Bt,
                                                              op=ALU.mult), waits=[k, free])
            st["conv_free"] = k
            toks.append(k)
        P.dma("sp", U, zg[t * 128:(t + 1) * 128, 0:1024], gsem, waits=[st["gm_free"]])
        lt = P.dma("sp", V, zg[t * 128:(t + 1) * 128, 1024:2048], gsem)
        k = gelu_tanh(P, U, gtmp, [lt])
        k = gelu_tanh(P, V, gtmp, [k])
        k = P.op("act", lambda e: e.activation(out=gtmp, in_=V, func=AF.Square, accum_out=C.ss[:, 4:5]),
                 waits=[k, st["gss_free"]])
        k = rstd_from_ss(C, 4, 1024, [k])
        k = P.op("dve", lambda e: e.scalar_tensor_tensor(out=Vb, in0=V, scalar=C.ss[:, 4:5], in1=sgu_bc,
                                                         op0=ALU.mult, op1=ALU.mult), waits=[k, ctok])
        st["gss_free"] = k
        mm = []
        for g in range(8):
            b = g // 4
            ps = C.banks[b]
            mm.append(P.op("pe", lambda e, ps=ps, g=g: e.matmul(
                out=ps[:, (g % 4) * 128:(g % 4 + 1) * 128], lhsT=WsT[:, g, :], rhs=Vb[:, g * 128:(g + 1) * 128],
                start=True, stop=True), waits=[k, wst_tok, C.bank_free[b]], sig=(g % 4 == 3)))
        for g in range(8):
            ps = C.banks[g // 4]
            k = P.op("dve", lambda e, ps=ps, g=g: e.scalar_tensor_tensor(
                out=xt[:, 3072 + g * 128:3072 + (g + 1) * 128], in0=ps[:, (g % 4) * 128:(g % 4 + 1) * 128],
                scalar=bsp[:, g:g + 1], in1=U[:, g * 128:(g + 1) * 128], op0=ALU.add, op1=ALU.mult),
                waits=[mm[g // 4 * 4 + 3], free])
            if g % 4 == 3:
                C.bank_free[g // 4] = k
        st["gm_free"] = k
        toks.append(k)
        return toks

    norm_transpose(C, produce, [(0, 1024), (1024, 3072), (3072, 4096)], gain_bc, gtok, actT, xt, xb)
    P.barrier()

    def resid_epi(src, dst):
        def epi(t, n0, nw, ps, mmtok):
            i = hring.next()
            hb = hring.bufs[i]
            lt = P.dma("act", hb[:, 0:nw], src[t * 128:(t + 1) * 128, n0:n0 + nw], hring.sems[i],
                       waits=[hring.free[i]])
            j = oring.next()
            ob_ = oring.bufs[j]
            tk = P.op("dve", lambda e: e.tensor_tensor(out=ob_[:, 0:nw], in0=ps[:, 0:nw], in1=hb[:, 0:nw],
                                                       op=ALU.add), waits=[mmtok, lt, oring.free[j]])
            hring.free[i] = tk
            oring.free[j] = P.dma("sp", dst[t * 128:(t + 1) * 128, n0:n0 + nw], ob_[:, 0:nw], oring.sems[j],
                                  waits=[tk])
            return tk
        return epi

    def full_row_producer(src, sem):
        def produce(t, xt, free):
            return [P.dma("sp", xt, src[t * 128:(t + 1) * 128, :], sem, waits=[free])]
        return produce

    hs = d["hs"]
    C.w_ready = d.get("tok_w_out", [])
    gemm_T(C, actT, 32, d["w_out"], 0, 4096, [], resid_epi(d["h_in"], hs))
    P.barrier()

    gtok = P.dma("sp", gain_bc, d["mlp_norm"].to_broadcast([128, 4096]), P.dsem())
    norm_transpose(C, full_row_producer(hs, P.dsem()), [(0, 4096)], gain_bc, gtok, actT, xt, xb)
    P.barrier()
    st2 = {"h1_free": None}
    FQ = D_FF // n_ffq
    KQ = FQ // 128

    def relu2_epi(j, th, ps, mmtok):
        i = rring.next()
        rb = rring.bufs[i]
        a = P.op("act", lambda e: e.activation(out=rb, in_=ps, func=AF.Relu), waits=[mmtok, rring.free[i]])
        rring.free[i] = P.op("dve", lambda e: e.tensor_tensor(
            out=h1T[:, j, th * 512:(th + 1) * 512], in0=rb, in1=rb, op=ALU.mult), waits=[a, st2["h1_free"]])
        st2["last_sq"] = rring.free[i]
        return a

    for q in range(n_ffq):
        C.w_ready = d.get("tok_w_mlp_in", [])
        gemm_F(C, actT, 32, d["w_mlp_in"], 0, FQ, [], relu2_epi, n_off=q * FQ)
        C.w_ready = d.get("tok_w_mlp_out", [])
        P._wait("act", [f for f in oring.free if f is not None])
        st2["h1_free"] = gemm_T(C, h1T, KQ, d["w_mlp_out"], q * KQ, 4096, [st2["last_sq"]], resid_epi(hs, hs))
    P.barrier()

    gtok = P.dma("sp", gain_bc, d["ple_norm"].to_broadcast([128, 4096]), P.dsem())
    norm_transpose(C, full_row_producer(hs, P.dsem()), [(0, 4096)], gain_bc, gtok, actT, xt, xb)
    P.barrier()
    wp_tok = P.dma("pool" if C.w_cast else "sp", wp, d["w_ple_proj"].rearrange("(k p) n -> p k n", p=128), P.dsem(),
                   waits=d.get("tok_w_ple_proj", []))
    psem = P.dsem()
    ptok = None
    pfree = None
    for t in range(NT):
        lt = P.dma("sp", pld, d["p"][t * 128:(t + 1) * 128, :], psem, waits=[pfree])
        k = P.op("dve", lambda e: e.tensor_copy(out=pbf, in_=pld), waits=[lt, ptok])
        pfree = k
        tp = C.tp[1]
        for kk in range(2):
            k2 = P.op("pe", lambda e, kk=kk: e.transpose(out=tp[:, kk, :], in_=pbf[:, kk * 128:(kk + 1) * 128],
                                                         identity=C.ident), waits=[k, C.tp_free[1]], sig=(kk == 1))
        ptok = P.op("act", lambda e, t=t: e.activation(out=pT[:, :, t * 128:(t + 1) * 128], in_=tp[:, 0:2, :],
                                                       func=AF.Copy), waits=[k2])
        C.tp_free[1] = ptok
        ptok2 = k2
    h_out = d["h_out"]
    b2sel = [0]

    def ple_epi(t, n0, nw, ps, mmtok):
        b = 3 + b2sel[0] % 3
        b2sel[0] += 1
        ps2 = C.banks[b]
        for kk in range(2):
            m2 = P.op("pe", lambda e, kk=kk: e.matmul(
                out=ps2[:, 0:nw], lhsT=pT[:, kk, t * 128:(t + 1) * 128], rhs=wp[:, kk, n0:n0 + nw],
                start=(kk == 0), stop=(kk == 1)), waits=[wp_tok, ptok, C.bank_free[b]], sig=(kk == 1))
        i = rring.next()
        rb = rring.bufs[i]
        a = P.op("act", lambda e: e.activation(out=rb[:, 0:nw], in_=ps[:, 0:nw], func=AF.Sigmoid),
                 waits=[mmtok, rring.free[i]])
        hi = hring.next()
        hb = hring.bufs[hi]
        lt = P.dma("act", hb[:, 0:nw], hs[t * 128:(t + 1) * 128, n0:n0 + nw], hring.sems[hi],
                   waits=[hring.free[hi]])
        j = oring.next()
        ob_ = oring.bufs[j]
        k1 = P.op("dve", lambda e: e.tensor_tensor(out=ob_[:, 0:nw], in0=ps2[:, 0:nw], in1=rb[:, 0:nw],
                                                   op=ALU.mult), waits=[a, m2, oring.free[j]])
        C.bank_free[b] = k1
        rring.free[i] = k1
        k2 = P.op("dve", lambda e: e.tensor_tensor(out=ob_[:, 0:nw], in0=ob_[:, 0:nw], in1=hb[:, 0:nw],
                                                   op=ALU.add), waits=[k1, lt])
        hring.free[hi] = k2
        oring.free[j] = P.dma("sp", h_out[t * 128:(t + 1) * 128, n0:n0 + nw], ob_[:, 0:nw], oring.sems[j],
                              waits=[k2])
        return a

    C.w_ready = d.get("tok_w_ple_gate", [])
    gemm_T(C, actT, 32, d["w_ple_gate"], 0, 4096, [], ple_epi)
    P.barrier()
    P.top = mark
    P.sem_release(smark)
    C.w_ready = []


NOFF = 35


def nsa_tables(g, T=SEQ):
    n_cmp = T // 16 - 1
    ncc = (n_cmp + 127) // 128
    nqb = T // 512
    slopes = np.array([2.0 ** (-(4 * g + r + 1) / 2.0) for r in range(4)], np.float64)
    ik = np.arange(128)[:, None]
    iq = np.arange(512)[None, :]
    t = {}
    t["negslope"] = np.tile(-slopes[None, :], (128, 1))
    t["D0"] = (iq - ik).astype(np.float64)
    t["D0c"] = (iq - 16 * ik).astype(np.float64)
    offb = np.zeros((4, NOFF))
    for r in range(4):
        for dl in range(-3, 32):
            offb[r, dl + 3] = -slopes[r] * 128.0 * dl
    t["OFFB"] = np.tile(offb.reshape(1, -1), (128, 1))
    offc = np.zeros((4, nqb * ncc))
    cpm = np.zeros((nqb * ncc, 128, 512))
    for pq in range(nqb):
        for c in range(ncc):
            off = 512 * pq - 2048 * c - 31
            for r in range(4):
                offc[r, pq * ncc + c] = -slopes[r] * off
            ok = ((iq - 16 * ik + off) >= 0) & ((128 * c + ik) < n_cmp)
            cpm[pq * ncc + c] = np.where(ok, 0.0, NEG)
    t["OFFC"] = np.tile(offc.reshape(1, -1), (128, 1))
    t["CPM"] = cpm.transpose(1, 0, 2)
    cm = np.zeros((4, 128, 512))
    for i in range(4):
        cm[i] = np.where((-128 * i + iq - ik) >= 0, 0.0, NEG)
    t["CM"] = cm.transpose(1, 0, 2)
    wm = np.zeros((8, 128, 512))
    for i in range(8):
        dist = 128 * (4 - i) + iq - ik
        wm[i] = np.where((dist >= 0) & (dist < 512), 0.0, NEG)
    t["WM"] = wm.transpose(1, 0, 2)
    nkc = T // 128
    e = np.zeros((128, nkc, 128))
    for c in range(nkc):
        for k in range(128):
            e[2 * c + k // 64, c, k] = 1.0 if 2 * c + k // 64 < 128 else 0.0
    t["E"] = e[:, :, :]
    nslc = T // 64
    addt = np.zeros((128, nkc, nslc))
    for qt in range(nkc):
        for p in range(128):
            qb = (128 * qt + p) // 64
            row = np.zeros(nslc)
            row[qb + 1:] = -1e30
            if qb - 1 >= 0:
                row[qb - 1] = 1e9
            row[qb] = 2e9
            row[0] = 3e9
            addt[p, qt] = row
    t["ADDT"] = addt
    ovl = np.zeros((ncc * 128, nslc))
    for n in range(n_cmp):
        for m in range(nslc):
            o = min(16 * n + 32, 64 * m + 64) - max(16 * n, 64 * m)
            ovl[n, m] = max(o, 0) / 32.0
    t["OVL"] = ovl.reshape(ncc, 128, nslc).transpose(1, 0, 2)
    return {k: np.ascontiguousarray(v.reshape(v.shape[0], -1), dtype=np.float32) for k, v in t.items()}


def phase_B(P, C, d, T=SEQ):
    NKC = T // 128
    NQB = T // 512
    n_cmp = T // 16 - 1
    NCC = (n_cmp + 127) // 128
    NSLC = T // 64
    VC = 129 + NSLC
    mark = P.top
    smark = P.sem_mark()
    A = P.alloc
    KT = A([128, 4, T], BF16)
    Vs = A([128, NKC, 130], BF16)
    Vw = A([128, NKC, 130], BF16)
    Ttab = A([128, 4, 512], F32)
    Ttabc = A([128, 4, 512], F32)
    negslope = A([128, 4], F32)
    D0 = A([128, 512], F32)
    D0c = A([128, 512], F32)
    OFFB = A([128, 4 * NOFF], F32)
    OFFC = A([128, 4 * NQB * NCC], F32)
    CM = A([128, 4, 512], BF16)
    WM = A([128, 8, 512], BF16)
    CPM = A([128, NQB * NCC, 512], BF16)
    E = A([128, NKC, 128], BF16)
    ADDT = A([128, NKC, NSLC], F32)
    w1b = [A([128, 32, 128], BF16) for _ in range(2)]
    w2b = [A([128, 128], BF16) for _ in range(2)]
    peT = [A([128, 32], BF16) for _ in range(2)]
    kcT = A([128, NCC * 128], BF16)
    Vc = A([128, NCC, VC + 1], BF16)
    OVLb = A([128, NCC, NSLC], BF16)
    qg_bc = A([128, 128], F32)
    kg_bc = A([128, 384], F32)
    cbias = A([128, 2], F32)
    hx = A([128, NCC * 128], F32)
    hx2 = A([128, NCC * 128], F32)
    hidT = A([128, NCC * 128], BF16)
    kvring = Ring(P, 2, [128, 768], F32)
    xb4 = A([128, 4, 128], BF16)
    junk = A([128, 512], BF16)
    qring = Ring(P, 2, [128, 512], F32)
    qn = A([128, 512], BF16)
    qTs = [A([128, 4, 512], BF16) for _ in range(2)]
    glts = [A([128, 4, 12], F32) for _ in range(2)]
    gsigs = [A([128, 4, 12], F32) for _ in range(2)]
    Sp = Ring(P, 4, [128, 512], F32)
    Pb = Ring(P, 6, [128, 512], BF16)
    o_t = A([128, 4, 512], F32)
    imp = A([128, 4, NSLC], F32)
    impA = A([128, NSLC], F32)
    impB = A([128, NSLC], F32)
    m8 = A([128, 16], F32)
    selt = A([128, NSLC], F32)
    negm = A([128, 4, NSLC], BF16)
    negmT = A([128, 512], BF16)
    rv = A([128, 8], F32)
    kcn = A([128, 128], BF16)

    cs = P.dsem()
    for dst, key in ((negslope, "negslope"), (D0, "D0"), (D0c, "D0c"), (OFFB, "OFFB"), (OFFC, "OFFC"),
                     (ADDT.rearrange("p a b -> p (a b)"), "ADDT")):
        ctok = P.dma("sp", dst, d[key], cs)
    P.dma("sp", qg_bc, d["q_gain"].to_broadcast([128, 128]), cs)
    ctok = P.dma("sp", kg_bc, d["k_gain"].to_broadcast([128, 384]), cs)
    cs2 = P.dsem()
    for dst, key in ((CM, "CM"), (WM, "WM"), (CPM, "CPM"), (E, "E"), (OVLb, "OVL")):
        P.dma("pool", dst.rearrange("p a b -> p (a b)"), d[key], cs2)
    for kind in range(2):
        P.dma("pool", w1b[kind], d["cmp_w1"][kind].rearrange("l d e -> d l e"), cs2)
        P.dma("pool", w2b[kind], d["cmp_w2"][kind], cs2)
        c2tok = P.dma("pool", peT[kind], d["cmp_pos"][kind].rearrange("l d -> d l"), cs2,
                      allow_slow_non_contiguous=True)
    k = None
    for r in range(4):
        P.op("dve", lambda e, r=r: e.tensor_scalar(out=Ttab[:, r, :], in0=D0, scalar1=negslope[:, r:r + 1],
                                                   scalar2=None, op0=ALU.mult), waits=[ctok])
        k = P.op("dve", lambda e, r=r: e.tensor_scalar(out=Ttabc[:, r, :], in0=D0c, scalar1=negslope[:, r:r + 1],
                                                       scalar2=None, op0=ALU.mult), waits=[ctok])
    k = P.op("dve", lambda e: e.tensor_scalar(out=qg_bc, in0=qg_bc, scalar1=128.0 ** -0.5, scalar2=None,
                                              op0=ALU.mult), waits=[ctok])
    P.op("dve", lambda e: e.memset(Vs[:, :, 128:130], 1.0))
    P.op("dve", lambda e: e.memset(Vw[:, :, 128:130], 1.0))
    P.op("dve", lambda e: e.memset(hidT, 0.0))
    P.op("dve", lambda e: e.memset(Vc[:, :, 128:129], 1.0))
    P.op("dve", lambda e: e.tensor_copy(out=Vc[:, :, 129:129 + NSLC], in_=OVLb), waits=[c2tok])
    P.barrier()

    ss = C.ss

    def rstd_cols(c0, c1, n, waits):
        tk = P.op("act", lambda e: e.activation(out=ss[:, c0:c1], in_=ss[:, c0:c1], func=AF.Sqrt, scale=1.0 / n,
                                                bias=C.eps_t[:, 0:1]), waits=waits)
        return P.op("dve", lambda e: e.reciprocal(out=ss[:, c0:c1], in_=ss[:, c0:c1]), waits=[tk])

    def next_tp():
        pi = C.tp_i % len(C.tp)
        C.tp_i += 1
        return pi, C.tp[pi]

    kv = d["kv"]
    ss_free = None
    xb_free = None
    for t in range(NKC):
        i = kvring.next()
        kt = kvring.bufs[i]
        lt = P.dma("sp", kt, kv[t * 128:(t + 1) * 128, :], kvring.sems[i], waits=kvring.free[i] or [])
        P.op("act", lambda e, kt=kt: e.activation(out=junk[:, 0:128], in_=kt[:, 256:384], func=AF.Square,
                                                  accum_out=ss[:, 0:1]), waits=[lt, ss_free])
        a = P.op("act", lambda e, kt=kt: e.activation(out=junk[:, 128:256], in_=kt[:, 512:640], func=AF.Square,
                                                      accum_out=ss[:, 1:2]))
        rt = rstd_cols(0, 2, 128, [a])
        P.op("dve", lambda e, kt=kt: e.tensor_copy(out=xb4[:, 0:2, :].rearrange("p a b -> p (a b)"),
                                                   in_=kt[:, 0:256]), waits=[lt, xb_free])
        P.op("dve", lambda e, kt=kt: e.scalar_tensor_tensor(out=xb4[:, 2, :], in0=kt[:, 256:384], scalar=ss[:, 0:1],
                                                            in1=kg_bc[:, 128:256], op0=ALU.mult, op1=ALU.mult),
             waits=[rt])
        nk = P.op("dve", lambda e, kt=kt: e.scalar_tensor_tensor(out=xb4[:, 3, :], in0=kt[:, 512:640],
                                                                 scalar=ss[:, 1:2], in1=kg_bc[:, 256:384],
                                                                 op0=ALU.mult, op1=ALU.mult))
        ss_free = nk
        P.op("act", lambda e, kt=kt, t=t: e.activation(out=Vs[:, t, 0:128], in_=kt[:, 384:512], func=AF.Copy),
             waits=[lt])
        pk = P.op("act", lambda e, kt=kt, t=t: e.activation(out=Vw[:, t, 0:128], in_=kt[:, 640:768], func=AF.Copy))
        pi, tp = next_tp()
        for j in range(4):
            tk = P.op("pe", lambda e, tp=tp, j=j: e.transpose(out=tp[:, j, :], in_=xb4[:, j, :], identity=C.ident),
                      waits=[nk, C.tp_free[pi], C.ident_tok], sig=(j == 3))
        xb_free = tk
        ev = P.op("act", lambda e, tp=tp, t=t: e.activation(out=KT[:, :, t * 128:(t + 1) * 128], in_=tp[:, 0:4, :],
                                                            func=AF.Copy), waits=[tk])
        C.tp_free[pi] = ev
        kvring.free[i] = [ev, nk, pk]
    P.barrier()

    for kind in range(2):
        X = KT[:, kind, :]
        b0, b1 = C.banks[0], C.banks[1]
        for l in range(32):
            tk = P.op("pe", lambda e, l=l, kind=kind: e.matmul(out=b0[:, 0:1], lhsT=w1b[kind][:, l, :],
                                                               rhs=peT[kind][:, l:l + 1], start=(l == 0),
                                                               stop=(l == 31)), sig=(l == 31))
        cb = P.op("dve", lambda e, kind=kind: e.tensor_copy(out=cbias[:, kind:kind + 1], in_=b0[:, 0:1]), waits=[tk])
        for l in range(32):
            tk = P.op("pe", lambda e, l=l, kind=kind, X=X: e.matmul(
                out=b1[:, 0:n_cmp], lhsT=w1b[kind][:, l, :], rhs=X[:, l:l + 16 * (n_cmp - 1) + 1:16],
                start=(l == 0), stop=(l == 31)), sig=(l == 31))
        a = P.op("act", lambda e, kind=kind: e.activation(out=hx[:, 0:n_cmp], in_=b1[:, 0:n_cmp], func=AF.Identity,
                                                          bias=cbias[:, kind:kind + 1]), waits=[tk, cb])
        g = gelu_tanh(P, hx[:, 0:n_cmp], hx2[:, 0:n_cmp], [a])
        hk = P.op("dve", lambda e: e.tensor_copy(out=hidT[:, 0:n_cmp], in_=hx[:, 0:n_cmp]), waits=[g])
        for c in range(NCC):
            tk = P.op("pe", lambda e, c=c, kind=kind: e.matmul(out=b0[:, 0:128], lhsT=hidT[:, c * 128:(c + 1) * 128],
                                                               rhs=w2b[kind], start=True, stop=True), waits=[hk, cb])
            if kind == 0:
                a = P.op("act", lambda e: e.activation(out=junk[:, 0:128], in_=b0[:, 0:128], func=AF.Square,
                                                       accum_out=ss[:, 2:3]), waits=[tk])
                rt = rstd_cols(2, 3, 128, [a])
                nk = P.op("dve", lambda e: e.scalar_tensor_tensor(out=kcn, in0=b0[:, 0:128], scalar=ss[:, 2:3],
                                                                  in1=kg_bc[:, 0:128], op0=ALU.mult, op1=ALU.mult),
                          waits=[rt])
                pi, tp = next_tp()
                tk2 = P.op("pe", lambda e, tp=tp: e.transpose(out=tp[:, 0, :], in_=kcn, identity=C.ident),
                           waits=[nk, C.tp_free[pi]])
                ev = P.op("dve", lambda e, tp=tp, c=c: e.tensor_copy(out=kcT[:, c * 128:(c + 1) * 128],
                                                                     in_=tp[:, 0, :]), waits=[tk2])
                C.tp_free[pi] = ev
                cb = ev
            else:
                cb = P.op("dve", lambda e, c=c: e.tensor_copy(out=Vc[:, c, 0:128], in_=b0[:, 0:128]), waits=[tk])
        P.barrier()

    q_dram, gl_dram, ob_dram = d["q"], d["gl"], d["ob"]
    Sbank = [0, 1]
    state = {"si": 0, "pend": [], "o_free": None, "qT_free": [None, None], "gl_free": None, "imp_free": None,
             "negm_free": None, "ss_free": None, "qn_free": None}
    LAG = 4

    def emit_pv(tile):
        i2, ex_tok, pvs, V, after = tile
        pb = Pb.bufs[i2]
        tk = None
        for (acc, bank, j, start, stop) in pvs:
            tk = P.op("pe", lambda e, acc=acc, j=j, start=start, stop=stop, pb=pb, V=V: e.matmul(
                out=acc, lhsT=pb[:, j * 128:(j + 1) * 128], rhs=V, start=start, stop=stop,
                skip_group_check=True),
                waits=[ex_tok] + ([C.bank_free[bank]] if start else []))
        Pb.free[i2] = tk
        if after is not None:
            after(tk)

    def push(kT, qTr, aux, ttab, bias_ap, V, pvs, after, waits):
        b = Sbank[state["si"] % 2]
        state["si"] += 1
        S = C.banks[b]
        n = 1 + len(aux)
        tk = P.op("pe", lambda e: e.matmul(out=S, lhsT=kT, rhs=qTr, start=True, stop=(n == 1)),
                  waits=list(waits) + [C.bank_free[b]], sig=(n == 1))
        for ai, (al, ar) in enumerate(aux):
            tk = P.op("pe", lambda e, al=al, ar=ar, ai=ai: e.matmul(out=S, lhsT=al, rhs=ar, start=False,
                                                                    stop=(ai == n - 2)), sig=(ai == n - 2))
        i1 = Sp.next()
        sp_ = Sp.bufs[i1]
        dk = P.op("dve", lambda e: e.tensor_tensor(out=sp_, in0=S, in1=ttab, op=ALU.add), waits=[tk, Sp.free[i1]])
        C.bank_free[b] = dk
        i2 = Pb.next()
        pb = Pb.bufs[i2]
        ak = P.op("act", lambda e: e.activation(out=pb, in_=sp_, func=AF.Exp, bias=bias_ap), waits=[dk, Pb.free[i2]])
        Sp.free[i1] = ak
        state["pend"].append((i2, ak, pvs, V, after))
        while len(state["pend"]) > LAG:
            emit_pv(state["pend"].pop(0))

    def bank_starts(pvs, seen):
        out = []
        for (acc, bank, j, start, stop) in pvs:
            out.append((acc, bank, j, bank not in seen, stop))
            seen.add(bank)
        return out

    def flush():
        while state["pend"]:
            emit_pv(state["pend"].pop(0))

    def acc_aps(setidx, ncol):
        base = 2 + 2 * setidx
        return [(C.banks[base + j // 2][:, (j % 2) * ncol:(j % 2) * ncol + ncol], base + j // 2) for j in range(4)]

    def make_evac(r, br, accs, first, pq):
        gsig = gsigs[pq % 2]

        def after(pvtok):
            ncol = VC if br == 0 else 129
            k = None
            for h2 in range(2):
                bk = C.banks[accs[2 * h2][1]]
                src = bk[:, 0:2 * ncol].rearrange("p (a b) -> p a b", a=2)[:, :, 128:129].rearrange("p a b -> p (a b)")
                k = P.op("dve", lambda e, src=src, h2=h2: e.tensor_scalar(
                    out=rv[:, 2 * h2:2 * h2 + 2], in0=src, scalar1=1e-30, scalar2=None, op0=ALU.add),
                    waits=[pvtok, k])
            k = P.op("dve", lambda e: e.reciprocal(out=rv[:, 0:4], in_=rv[:, 0:4]), waits=[k])
            gs = gsig[:, :, r * 3 + br:r * 3 + br + 1].rearrange("p a b -> p (a b)")
            k = P.op("dve", lambda e, gs=gs: e.tensor_tensor(out=rv[:, 4:8], in0=rv[:, 0:4], in1=gs, op=ALU.mult),
                     waits=[k, state["gsig_toks"][pq]])
            for j in range(4):
                acc, bank = accs[j]
                if br == 0:
                    if r == 0:
                        k = P.op("dve", lambda e, acc=acc, j=j: e.tensor_scalar(
                            out=imp[:, j, :], in0=acc[:, 129:129 + NSLC], scalar1=rv[:, j:j + 1], scalar2=None,
                            op0=ALU.mult), waits=[k, state["imp_free"]])
                    else:
                        k = P.op("dve", lambda e, acc=acc, j=j: e.scalar_tensor_tensor(
                            out=imp[:, j, :], in0=acc[:, 129:129 + NSLC], scalar=rv[:, j:j + 1], in1=imp[:, j, :],
                            op0=ALU.mult, op1=ALU.add), waits=[k])
                if first:
                    k = P.op("dve", lambda e, acc=acc, j=j: e.tensor_scalar(
                        out=o_t[:, j, r * 128:(r + 1) * 128], in0=acc[:, 0:128], scalar1=rv[:, 4 + j:5 + j],
                        scalar2=None, op0=ALU.mult), waits=[k, state["o_free"]])
                else:
                    k = P.op("dve", lambda e, acc=acc, j=j: e.scalar_tensor_tensor(
                        out=o_t[:, j, r * 128:(r + 1) * 128], in0=acc[:, 0:128], scalar=rv[:, 4 + j:5 + j],
                        in1=o_t[:, j, r * 128:(r + 1) * 128], op0=ALU.mult, op1=ALU.add), waits=[k])
                C.bank_free[bank] = k
            state["last_evac"] = k
            if br == 0 and r == 3:
                state["cmp_evac"] = k
        return after

    setsel = [0]
    state["gsig_toks"] = {}
    state["qtoks"] = {}
    state["blk_done"] = {}

    def q_prep(pq):
        qT = qTs[pq % 2]
        glt = glts[pq % 2]
        gsig = gsigs[pq % 2]
        prev_gs = state["gsig_toks"].get(pq - 2)
        prev_done = state["blk_done"].get(pq - 2)
        qtok = None
        for j in range(4):
            t = 4 * pq + j
            i = qring.next()
            qt_ = qring.bufs[i]
            lt = P.dma("sp", qt_, q_dram[t * 128:(t + 1) * 128, :], qring.sems[i], waits=[qring.free[i]])
            gt = P.dma("sp", glt[:, j, :], gl_dram[t * 128:(t + 1) * 128, :], qring.sems[i],
                       waits=[prev_gs])
            a = None
            for r in range(4):
                a = P.op("act", lambda e, r=r, qt_=qt_: e.activation(
                    out=junk[:, r * 128:(r + 1) * 128], in_=qt_[:, r * 128:(r + 1) * 128], func=AF.Square,
                    accum_out=ss[:, r:r + 1]), waits=[gt, state["ss_free"]])
            rt = rstd_cols(0, 4, 128, [a])
            nk = None
            for r in range(4):
                nk = P.op("dve", lambda e, r=r, qt_=qt_: e.scalar_tensor_tensor(
                    out=qn[:, r * 128:(r + 1) * 128], in0=qt_[:, r * 128:(r + 1) * 128], scalar=ss[:, r:r + 1],
                    in1=qg_bc, op0=ALU.mult, op1=ALU.mult), waits=[rt, state["qn_free"]])
            state["ss_free"] = nk
            qring.free[i] = nk
            pi, tp = next_tp()
            for r in range(4):
                tk = P.op("pe", lambda e, tp=tp, r=r: e.transpose(out=tp[:, r, :], in_=qn[:, r * 128:(r + 1) * 128],
                                                                  identity=C.ident),
                          waits=[nk, C.tp_free[pi]], sig=(r == 3))
            state["qn_free"] = tk
            qtok = P.op("act", lambda e, tp=tp, j=j, qT=qT: e.activation(
                out=qT[:, :, j * 128:(j + 1) * 128], in_=tp[:, 0:4, :], func=AF.Copy),
                waits=[tk, state["qT_free"][pq % 2]])
            C.tp_free[pi] = qtok
        state["gsig_toks"][pq] = P.op("act", lambda e: e.activation(out=gsig, in_=glt, func=AF.Sigmoid),
                                      waits=[gt, prev_done])
        state["qtoks"][pq] = qtok

    q_prep(0)
    for pq in range(NQB):
        qT = qTs[pq % 2]
        qtok = state["qtoks"][pq]
        for r in range(4):
            accs = acc_aps(setsel[0] % 2, VC)
            setsel[0] += 1
            cl = [c for c in range(NCC) if 511 + 512 * pq - 2048 * c - 31 >= 0]
            seen = set()
            for c in cl:
                idx = pq * NCC + c
                pvs = bank_starts([(accs[j][0], accs[j][1], j, c == cl[0], c == cl[-1]) for j in range(4)], seen)
                push(kcT[:, c * 128:(c + 1) * 128], qT[:, r, :], [(C.ident, CPM[:, idx, :])], Ttabc[:, r, :],
                     OFFC[:, r * NQB * NCC + idx:r * NQB * NCC + idx + 1], Vc[:, c, 0:VC], pvs,
                     make_evac(r, 0, accs, True, pq) if c == cl[-1] else None, [qtok])
        for r in range(4):
            accs = acc_aps(setsel[0] % 2, 129)
            setsel[0] += 1
            cl = list(range(max(0, 4 * pq - 4), 4 * pq + 4))
            seen = set()
            for c in cl:
                dl = 4 * pq - c
                pvs = []
                for j in range(4):
                    if 4 * pq + j - 4 <= c <= 4 * pq + j:
                        pvs.append((accs[j][0], accs[j][1], j, c == max(0, 4 * pq + j - 4), c == 4 * pq + j))
                pvs = bank_starts(pvs, seen)
                push(KT[:, 3, c * 128:(c + 1) * 128], qT[:, r, :], [(C.ident, WM[:, 4 - dl, :])], Ttab[:, r, :],
                     OFFB[:, r * NOFF + dl + 3:r * NOFF + dl + 4], Vw[:, c, 0:129], pvs,
                     make_evac(r, 2, accs, False, pq) if c == cl[-1] else None, [qtok])
        if pq + 1 < NQB:
            q_prep(pq + 1)
        k = state["cmp_evac"]
        for j in range(4):
            t = 4 * pq + j
            k = P.op("dve", lambda e, j=j, t=t: e.tensor_tensor(out=impA, in0=imp[:, j, :], in1=ADDT[:, t, :],
                                                                op=ALU.add), waits=[k])
            k = P.op("dve", lambda e: e.max(out=m8[:, 0:8], in_=impA), waits=[k])
            k = P.op("dve", lambda e: e.match_replace(out=impB, in_to_replace=m8[:, 0:8], in_values=impA,
                                                      imm_value=-3.0e38), waits=[k])
            k = P.op("dve", lambda e: e.max(out=m8[:, 8:16], in_=impB), waits=[k])
            k = P.op("dve", lambda e: e.tensor_scalar(out=selt, in0=impA, scalar1=m8[:, 15:16], scalar2=None,
                                                      op0=ALU.is_ge), waits=[k])
            k = P.op("dve", lambda e, j=j: e.tensor_scalar(out=negm[:, j, :], in0=selt, scalar1=1.0, scalar2=-NEG,
                                                           op0=ALU.subtract, op1=ALU.mult),
                     waits=[k, state["negm_free"]])
        state["imp_free"] = k
        pi, tp = next_tp()
        for j in range(4):
            tk = P.op("pe", lambda e, tp=tp, j=j: e.transpose(out=tp[0:NSLC, j, :], in_=negm[:, j, :],
                                                              identity=C.ident),
                      waits=[k, C.tp_free[pi]], sig=(j == 3))
        state["negm_free"] = tk
        ntok = P.op("act", lambda e, tp=tp: e.activation(
            out=negmT[0:NSLC, :].rearrange("p (a b) -> p a b", a=4), in_=tp[0:NSLC, 0:4, :], func=AF.Copy),
            waits=[tk, state.get("negmT_free")])
        C.tp_free[pi] = ntok
        for r in range(4):
            accs = acc_aps(setsel[0] % 2, 129)
            setsel[0] += 1
            cl = list(range(4 * pq + 4))
            seen = set()
            for c in cl:
                dl = 4 * pq - c
                aux = [(E[0:NSLC, c, :], negmT[0:NSLC, :])]
                if dl <= 0:
                    aux.append((C.ident, CM[:, -dl, :]))
                pvs = []
                for j in range(4):
                    if c <= 4 * pq + j:
                        pvs.append((accs[j][0], accs[j][1], j, c == 0, c == 4 * pq + j))
                pvs = bank_starts(pvs, seen)
                push(KT[:, 2, c * 128:(c + 1) * 128], qT[:, r, :], aux, Ttab[:, r, :],
                     OFFB[:, r * NOFF + dl + 3:r * NOFF + dl + 4], Vs[:, c, 0:129], pvs,
                     make_evac(r, 1, accs, False, pq) if c == cl[-1] else None, [qtok, ntok])
        flush()
        state["negmT_free"] = Tok(P.psem["pe"], P.cnt["pe"])
        state["qT_free"][pq % 2] = Tok(P.psem["pe"], P.cnt["pe"])
        state["blk_done"][pq] = state["last_evac"]
        if "osem" not in state:
            state["osem"] = P.dsem()
        osem = state["osem"]
        for j in range(4):
            t = 4 * pq + j
            stt = P.dma("sp", ob_dram[t * 128:(t + 1) * 128, :], o_t[:, j, :], osem, waits=[state["last_evac"]])
        state["o_free"] = stt
    P.barrier()
    P.top = mark
    P.sem_release(smark)


W_SHAPES = {"w_in": (4096, IN_COLS), "w_out": (4096, 4096), "w_mlp_in": (4096, D_FF), "w_mlp_out": (D_FF, 4096),
            "w_ple_gate": (4096, 4096), "w_ple_proj": (256, 4096)}
SMALL = {"mix_norm": 4096, "conv_w": 3072, "q_gain": 128, "k_gain": 384, "sgu_gain": 1024, "mix_out_norm": 4096,
         "mlp_norm": 4096, "ple_norm": 4096}
NQKV = 512 + 768 + 12
G8 = [list(range(8))]
G4 = [[0, 1, 2, 3], [4, 5, 6, 7]]


def collective(P, kind, src, dst, groups, waits, bg=False):
    P._wait("pool", waits)
    sem = P.dsem()
    sem.bg = bg
    sem.n += 1
    s_ = sem.sem
    P.q["pool"].append(lambda e: e.collective_compute(kind, ALU.bypass, replica_groups=groups, ins=[src.opt()],
                                                      outs=[dst.opt()]).then_inc(s_))
    return Tok(s_, sem.n)


def build_fused(depth=DEPTH, T=1024, TS=SEQ):
    P = PB(num_devices=8)
    nc = P.nc
    ext = {}

    def xin(name, shape):
        ext[name] = P.dram(name, shape, F32, "ExternalInput")
        return ext[name]

    x = xin("x", [T, 4096])
    p = xin("p", [depth, T, 256])
    wsh = {k: xin(k, [depth, K // 8, N]) for k, (K, N) in W_SHAPES.items()}
    sm = {k: xin(k, [depth, n]) for k, n in SMALL.items()}
    cmp_pos = xin("cmp_pos", [depth, 2, 32, 128])
    cmp_w1 = xin("cmp_w1", [depth, 2, 32, 128, 128])
    cmp_w2 = xin("cmp_w2", [depth, 2, 128, 128])
    w_sp = xin("w_sp", [depth, 8, 128, 128])
    b_sp = xin("b_sp", [depth, 8, 128])
    tabs = {k: xin(k, list(v.shape)) for k, v in nsa_tables(0, TS).items()}
    ident = xin("ident", [128, 128])
    tril = xin("tril", [128, 128])
    zeros = xin("zeros", [2, 3072])
    out = P.dram("out", [T, 4096], F32, "ExternalOutput")

    wfull = {k: [P.dram(f"{k}_f{i}", [K, N], BF16) for i in range(depth)] for k, (K, N) in W_SHAPES.items()}
    wstage = {k: [P.dram(f"{k}_s{i}", [K // 8, N], BF16) for i in range(depth)] for k, (K, N) in W_SHAPES.items()}
    hbuf = [P.dram(f"hbuf{i}", [T, 4096]) for i in range(2)]
    hs = P.dram("hs", [T, 4096])
    z = P.dram("z", [T, IN_COLS])
    zsend = P.dram("zsend", [T, 5168])
    zq_all = P.dram("zq_all", [4 * T, 5168])
    qkv_mine = P.dram("qkv_mine", [4 * T, NQKV])
    halo_send = P.dram("halo_send", [2, 3072])
    halo_pad = P.dram("halo_pad", [10, 3072])
    halo_mine = P.dram("halo_mine", [2, 3072])
    ob_mine = P.dram("ob_mine", [4 * T, 512])
    ob_all = P.dram("ob_all", [16 * T, 512])
    ob_tok = P.dram("ob_tok", [T, 2048])

    C = Ctx(P, T, ident)
    C.w_cast = False
    gtok = {}
    stsem = P.dsem()
    stsem.bg = True

    def gather_w(k, i):
        t = P.dma("pool", wstage[k][i], wsh[k][i], stsem)
        gtok[(k, i)] = collective(P, "AllGather", wstage[k][i], wfull[k][i], G8, [t], bg=True)

    misc = P.dsem()
    P.dma("sp", halo_pad[0:2, :], zeros, misc)
    gather_w("w_in", 0)
    gather_w("w_out", 0)
    SPE = mybir.EngineType.SP
    rk = {}

    for i in range(depth):
        h_in = x if i == 0 else hbuf[(i - 1) % 2]
        h_out = out if i == depth - 1 else hbuf[i % 2]
        C.w_ready = [gtok[("w_in", i)]]
        phase_A(P, C, h_in, sm["mix_norm"][i:i + 1, :], wfull["w_in"][i], z)
        C.w_ready = []
        P.dma("sp", zsend, z[:, OFF_Q:OFF_GU], misc)
        t1 = P.dma("sp", halo_send, z[T - 2:T, 0:3072], misc)
        x1 = collective(P, "AllGather", zsend, zq_all, G4, [t1])
        x1h = collective(P, "AllGather", halo_send, halo_pad[2:10, :], G4, [t1])
        gather_w("w_mlp_in", i)
        P._wait("sp", [x1, x1h])

        def relayout(e):
            if "rank" not in rk:
                rk["rank"] = e.snap(nc.partition_id([SPE]) % 4, min_val=0, max_val=3)
            rank = rk["rank"]
            e.dma_start(out=qkv_mine[:, 0:512], in_=zq_all[:, bass.ds(rank * 512, 512)]).then_inc(misc.sem, 16)
            e.dma_start(out=qkv_mine[:, 512:1280].rearrange("t (b x) -> t b x", b=6),
                        in_=zq_all[:, 2048:5120].rearrange("t (b x) -> t b x", b=6)[:, :, bass.ds(rank * 128, 128)]
                        ).then_inc(misc.sem, 16)
            e.dma_start(out=qkv_mine[:, 1280:1292], in_=zq_all[:, bass.ds(5120 + rank * 12, 12)]
                        ).then_inc(misc.sem, 16)
            e.dma_start(out=halo_mine, in_=halo_pad[bass.ds(rank * 2, 2), :]).then_inc(misc.sem, 16)

        P.q["sp"].append(relayout)
        misc.n += 64
        P.barrier()
        dB = dict(tabs)
        dB.update({"q": qkv_mine[:, 0:512], "kv": qkv_mine[:, 512:1280], "gl": qkv_mine[:, 1280:1292], "ob": ob_mine,
                   "q_gain": sm["q_gain"][i:i + 1, :], "k_gain": sm["k_gain"][i:i + 1, :], "cmp_pos": cmp_pos[i],
                   "cmp_w1": cmp_w1[i], "cmp_w2": cmp_w2[i]})
        phase_B(P, C, dB, TS)
        x2 = collective(P, "AllGather", ob_mine, ob_all, G4, [])
        gather_w("w_mlp_out", i)
        gather_w("w_ple_gate", i)
        gather_w("w_ple_proj", i)
        if i + 1 < depth:
            gather_w("w_in", i + 1)
            gather_w("w_out", i + 1)
        P._wait("sp", [x2])

        def relayout2(e):
            rank = rk["rank"]
            src = ob_all.rearrange("(r j t) c -> j r t c", r=4, j=4)[bass.ds(rank, 1), :, :, :]
            e.dma_start(out=ob_tok.rearrange("(o t) (r c) -> o r t c", o=1, r=4), in_=src).then_inc(misc.sem, 16)

        P.q["sp"].append(relayout2)
        misc.n += 16
        P.barrier()
        dC = {"h_in": h_in, "hs": hs, "h_out": h_out, "zc": z[:, 0:3072], "halo": halo_mine,
              "zg": z[:, OFF_GU:IN_COLS], "ob": ob_tok, "p": p[i], "conv_w": sm["conv_w"][i:i + 1, :],
              "sgu_gain": sm["sgu_gain"][i:i + 1, :], "w_sp": w_sp[i], "b_sp": b_sp[i], "tril": tril,
              "mix_out_norm": sm["mix_out_norm"][i:i + 1, :], "mlp_norm": sm["mlp_norm"][i:i + 1, :],
              "ple_norm": sm["ple_norm"][i:i + 1, :]}
        for k in ("w_out", "w_mlp_in", "w_mlp_out", "w_ple_gate", "w_ple_proj"):
            dC[k] = wfull[k][i]
            dC["tok_" + k] = [gtok[(k, i)]]
        phase_C(P, C, dC)
    return P.finish()


_NC_CACHE = {}


def _ext(P, name, shape):
    return P.dram(name, list(shape), F32, "ExternalInput")


def build_A():
    P = PB()
    h = _ext(P, "h", [1024, 4096])
    gain = _ext(P, "gain", [1, 4096])
    w = _ext(P, "w", [4096, IN_COLS])
    ident = _ext(P, "ident", [128, 128])
    z = P.dram("z", [1024, IN_COLS], F32, "ExternalOutput")
    C = Ctx(P, 1024, ident)
    phase_A(P, C, h, gain, w, z)
    return P.finish()


B_SHAPES = {"q": (SEQ, 512), "kv": (SEQ, 768), "gl": (SEQ, 12), "q_gain": (1, 128), "k_gain": (1, 384),
            "cmp_pos": (2, 32, 128), "cmp_w1": (2, 32, 128, 128), "cmp_w2": (2, 128, 128), "ident": (128, 128)}


def build_B():
    P = PB()
    d = {k: _ext(P, k, v.shape) for k, v in nsa_tables(0).items()}
    d.update({k: _ext(P, k, shp) for k, shp in B_SHAPES.items()})
    d["ob"] = P.dram("ob", [SEQ, 512], F32, "ExternalOutput")
    C = Ctx(P, SEQ, d["ident"])
    phase_B(P, C, d, SEQ)
    return P.finish()


C_SHAPES = {"h_in": (1024, 4096), "zc": (1024, 3072), "halo": (2, 3072), "zg": (1024, 2048), "ob": (1024, 2048),
            "p": (1024, 256), "conv_w": (1, 3072), "sgu_gain": (1, 1024), "w_sp": (8, 128, 128), "b_sp": (8, 128),
            "tril": (128, 128), "mix_out_norm": (1, 4096), "w_out": (4096, 4096), "mlp_norm": (1, 4096),
            "w_mlp_in": (4096, D_FF), "w_mlp_out": (D_FF, 4096), "ple_norm": (1, 4096), "w_ple_proj": (256, 4096),
            "w_ple_gate": (4096, 4096), "ident": (128, 128)}


def build_C():
    P = PB()
    d = {k: _ext(P, k, shp) for k, shp in C_SHAPES.items()}
    d["h_out"] = P.dram("h_out", [1024, 4096], F32, "ExternalOutput")
    d["hs"] = P.dram("hs", [1024, 4096], F32, "Internal")
    C = Ctx(P, 1024, d["ident"])
    phase_C(P, C, d)
    return P.finish()


def _prog(name, fn):
    if name not in _NC_CACHE:
        _NC_CACHE[name] = fn()
    return _NC_CACHE[name]


def kernel(**inputs):
    f32 = np.float32
    g = lambda k: np.asarray(inputs[k], dtype=f32)
    cores = list(range(8))
    ident = np.eye(128, dtype=f32)
    tril = np.tril(np.ones((128, 128), f32))
    tabs = [nsa_tables(gi) for gi in range(4)]
    h = np.ascontiguousarray(g("x")).reshape(8, 1024, 4096)
    p = g("p")
    ncA, ncB, ncC = _prog("A", build_A), _prog("B", build_B), _prog("C", build_C)
    for i in range(DEPTH):
        w_in = np.ascontiguousarray(g("w_in")[i])
        gain = g("mix_norm")[i].reshape(1, 4096)
        res = run_bass_kernel_spmd(ncA, [{"h": h[c], "gain": gain, "w": w_in, "ident": ident} for c in cores],
                                   core_ids=cores)
        zb = np.stack([np.asarray(r["z"]) for r in res.results]).reshape(BATCH, SEQ, IN_COLS)
        del res, w_in
        shB = {"q_gain": g("q_gain")[i].reshape(1, 128), "k_gain": g("k_gain")[i].reshape(1, 384),
               "cmp_pos": np.ascontiguousarray(g("cmp_pos")[i]), "cmp_w1": np.ascontiguousarray(g("cmp_w1")[i]),
               "cmp_w2": np.ascontiguousarray(g("cmp_w2")[i]), "ident": ident}
        insB = []
        for c in cores:
            b, gi = c // 4, c % 4
            m = dict(shB)
            m.update(tabs[gi])
            m["q"] = np.ascontiguousarray(zb[b, :, OFF_Q + gi * 512:OFF_Q + (gi + 1) * 512])
            m["kv"] = np.ascontiguousarray(np.concatenate(
                [zb[b, :, OFF_KV + br * 512 + gi * 128:OFF_KV + br * 512 + (gi + 1) * 128] for br in range(6)], -1))
            m["gl"] = np.ascontiguousarray(zb[b, :, OFF_GATE + gi * 12:OFF_GATE + (gi + 1) * 12])
            insB.append(m)
        res = run_bass_kernel_spmd(ncB, insB, core_ids=cores)
        obb = np.zeros((BATCH, SEQ, 2048), f32)
        for c in cores:
            b, gi = c // 4, c % 4
            obb[b, :, gi * 512:(gi + 1) * 512] = np.asarray(res.results[c]["ob"])
        del res, insB
        shC = {"conv_w": g("conv_w")[i].reshape(1, 3072), "sgu_gain": g("sgu_gain")[i].reshape(1, 1024),
               "w_sp": np.ascontiguousarray(g("w_sp")[i]), "b_sp": np.ascontiguousarray(g("b_sp")[i]), "tril": tril,
               "mix_out_norm": g("mix_out_norm")[i].reshape(1, 4096), "w_out": np.ascontiguousarray(g("w_out")[i]),
               "mlp_norm": g("mlp_norm")[i].reshape(1, 4096), "w_mlp_in": np.ascontiguousarray(g("w_mlp_in")[i]),
               "w_mlp_out": np.ascontiguousarray(g("w_mlp_out")[i]), "ple_norm": g("ple_norm")[i].reshape(1, 4096),
               "w_ple_proj": np.ascontiguousarray(g("w_ple_proj")[i]),
               "w_ple_gate": np.ascontiguousarray(g("w_ple_gate")[i]), "ident": ident}
        insC = []
        for c in cores:
            b, j = c // 4, c % 4
            t0 = j * 1024
            m = dict(shC)
            m["h_in"] = h[c]
            m["zc"] = np.ascontiguousarray(zb[b, t0:t0 + 1024, 0:3072])
            m["halo"] = (np.ascontiguousarray(zb[b, t0 - 2:t0, 0:3072]) if j > 0 else np.zeros((2, 3072), f32))
            m["zg"] = np.ascontiguousarray(zb[b, t0:t0 + 1024, OFF_GU:IN_COLS])
            m["ob"] = np.ascontiguousarray(obb[b, t0:t0 + 1024, :])
            m["p"] = np.ascontiguousarray(p[i, b, t0:t0 + 1024, :])
            insC.append(m)
        res = run_bass_kernel_spmd(ncC, insC, core_ids=cores)
        h = np.stack([np.asarray(r["h_out"]) for r in res.results])
        del res, insC, shC, zb, obb
    return np.ascontiguousarray(h.reshape(BATCH, SEQ, D_MODEL)).astype(f32)
```

```python
import contextlib
import numpy as np
import concourse.bass as bass
import concourse.mybir as mybir
from concourse.bass_utils import run_bass_kernel_spmd

F32 = mybir.dt.float32
BF16 = mybir.dt.bfloat16
U8 = mybir.dt.uint8
AF = mybir.ActivationFunctionType
ALU = mybir.AluOpType

ENGS = ("pe", "act", "dve", "pool", "sp")
EPS = 1e-6
ARENA_BYTES = 207 * 1024
NEG = -30000.0

D_MODEL = 4096
SEQ = 4096
BATCH = 2
DEPTH = 4
IN_COLS = 10288
D_FF = 16384
OFF_Q, OFF_KV, OFF_GATE, OFF_GU = 3072, 5120, 8192, 8240


class Tok:
    __slots__ = ("sem", "val")

    def __init__(self, sem, val):
        self.sem = sem
        self.val = val


class DmaSem:
    def __init__(self, sem):
        self.sem = sem
        self.n = 0
        self.bg = False


class PB:
    def __init__(self, num_devices=None):
        if num_devices is None:
            self.nc = bass.Bass("TRN2", target_bir_lowering=False)
        else:
            self.nc = bass.Bass("TRN2", target_bir_lowering=False, num_devices=num_devices)
        self.es = contextlib.ExitStack()
        self.q = {e: [] for e in ENGS}
        self.cnt = {e: 0 for e in ENGS}
        self.psem = {e: self.es.enter_context(self.nc.semaphore("prog_" + e)) for e in ENGS}
        self.waited = {e: {} for e in ENGS}
        self.dsems = []
        self.sem_free = []
        self.sem_live = []
        self.nsem = 0
        self.ntens = 0
        self.arena = self.es.enter_context(self.nc.sbuf_tensor("arena", [128, ARENA_BYTES], U8))
        self.top = 0
        self.ndram = 0

    def view(self, off, shape, dt):
        esz = 4 if dt == F32 else 2
        nb = int(np.prod(shape[1:])) * esz
        assert off % 4 == 0 and off + nb <= ARENA_BYTES, (off, nb)
        a = self.arena[0:shape[0], off:off + nb].bitcast(dt)
        if len(shape) == 3:
            a = a.rearrange("p (a b) -> p a b", a=shape[1])
        elif len(shape) == 4:
            a = a.rearrange("p (a b c) -> p a b c", a=shape[1], b=shape[2])
        return a

    def alloc(self, shape, dt):
        esz = 4 if dt == F32 else 2
        nb = int(np.prod(shape[1:])) * esz
        nb = (nb + 31) // 32 * 32
        off = self.top
        self.top += nb
        assert self.top <= ARENA_BYTES, ("arena overflow", self.top)
        return self.view(off, shape, dt)

    def ps(self, shape, dt, name=None):
        self.ntens += 1
        return self.es.enter_context(self.nc.psum_tensor(name or f"ps{self.ntens}", list(shape), dt))[:]

    def dsem(self, name=None):
        if self.sem_free:
            d = self.sem_free.pop()
        else:
            self.nsem += 1
            d = DmaSem(self.es.enter_context(self.nc.semaphore(f"s{self.nsem}")))
            self.dsems.append(d)
        self.sem_live.append(d)
        return d

    def sem_mark(self):
        return len(self.sem_live)

    def sem_release(self, mark):
        while len(self.sem_live) > mark:
            self.sem_free.append(self.sem_live.pop())

    def dram(self, name, shape, dt=F32, kind="Internal"):
        return self.nc.dram_tensor(name, list(shape), dt, kind=kind).ap()

    def _wait(self, eng, toks):
        w = self.waited[eng]
        for t in toks:
            if t is None:
                continue
            key = id(t.sem)
            if w.get(key, 0) >= t.val:
                continue
            w[key] = t.val
            self.q[eng].append(lambda e, sem=t.sem, val=t.val: e.wait_ge(sem, val))

    def op(self, eng, fn, waits=(), sig=True):
        self._wait(eng, waits)
        if sig:
            self.cnt[eng] += 1
            sem = self.psem[eng]
            self.q[eng].append(lambda e, fn=fn, sem=sem: fn(e).then_inc(sem, 1))
            return Tok(sem, self.cnt[eng])
        self.q[eng].append(lambda e, fn=fn: fn(e))
        return None

    def dma(self, eng, out, in_, sem, waits=(), **kw):
        self._wait(eng, waits)
        sem.n += 16
        s = sem.sem
        self.q[eng].append(
            lambda e, out=out, in_=in_, s=s, kw=kw: e.dma_start(out=out, in_=in_, **kw).then_inc(s, 16))
        return Tok(s, sem.n)

    def barrier(self, everything=False):
        toks = [Tok(self.psem[e], self.cnt[e]) for e in ENGS if self.cnt[e] > 0]
        toks += [Tok(d.sem, d.n) for d in self.dsems if d.n > 0 and (everything or not d.bg)]
        for e in ENGS:
            self._wait(e, toks)

    def finish(self):
        self.barrier(everything=True)
        nc, q = self.nc, self.q
        with nc.Block() as block:
            @block.tensor
            def _(e):
                for f in q["pe"]:
                    f(e)

            @block.scalar
            def _(e):
                for f in q["act"]:
                    f(e)

            @block.vector
            def _(e):
                for f in q["dve"]:
                    f(e)

            @block.gpsimd
            def _(e):
                for f in q["pool"]:
                    f(e)

            @block.sync
            def _(e):
                for f in q["sp"]:
                    f(e)
        self.es.close()
        return nc


class Ring:
    def __init__(self, P, n, shape, dt):
        self.bufs = [P.alloc(shape, dt) for _ in range(n)]
        self.sems = [P.dsem() for _ in range(n)]
        self.free = [None] * n
        self.i = 0

    def next(self):
        i = self.i % len(self.bufs)
        self.i += 1
        return i


class Ctx:
    def __init__(self, P, T, ident_dram, nbanks=6, wslots=True):
        self.P = P
        self.T = T
        self.NT = T // 128
        self.ident = P.alloc([128, 128], BF16)
        self.ident_tok = P.dma("pool", self.ident, ident_dram, P.dsem())
        self.banks = [P.ps([128, 512], F32, f"bank{i}") for i in range(nbanks)]
        self.bank_free = [None] * nbanks
        self.tp = [P.ps([128, 8, 128], BF16, f"tp{i}") for i in range(8 - nbanks)]
        self.tp_free = [None] * (8 - nbanks)
        self.tp_i = 0
        self.eps_t = P.alloc([128, 1], F32)
        P.op("dve", lambda e: e.memset(self.eps_t, EPS))
        self.ss = P.alloc([128, 8], F32)
        self.ss_free = None
        self.w_cast = True
        self.w_ready = []
        self.wsem = [P.dsem() for _ in range(2)]

    def alloc_wslots(self):
        P = self.P
        self.wbase = P.top
        self.wslots = [P.alloc([128, 32, 512], BF16) for _ in range(2)]
        self.wfree = [None, None]
        self.wnext = 0


def load_wblock(C, w_ap, k0, KC, n0, nw):
    P = C.P
    i = C.wnext
    C.wnext = (C.wnext + 1) % len(C.wslots)
    src = w_ap[k0 * 128:(k0 + KC) * 128, n0:n0 + nw].rearrange("(k p) n -> p k n", p=128)
    q = "pool" if C.w_cast else "sp"
    tok = P.dma(q, C.wslots[i][:, 0:KC, 0:nw], src, C.wsem[i], waits=[C.wfree[i]] + list(C.w_ready))
    return i, tok


def rstd_from_ss(C, col, n, waits):
    P = C.P
    ss = C.ss
    tk = P.op("act", lambda e: e.activation(out=ss[:, col:col + 1], in_=ss[:, col:col + 1], func=AF.Sqrt,
                                            scale=1.0 / n, bias=C.eps_t[:, 0:1]), waits=waits)
    return P.op("dve", lambda e: e.reciprocal(out=ss[:, col:col + 1], in_=ss[:, col:col + 1]), waits=[tk])


def norm_transpose(C, produce, segs, gain_bc, gain_tok, actT, xt, xb):
    P = C.P
    W = segs[-1][1]
    KC = W // 128
    ss = C.ss
    xt_free = None
    xb_free = None
    last = []
    for t in range(C.NT):
        ltoks = produce(t, xt, xt_free)
        stoks = []
        for si, (c0, c1) in enumerate(segs):
            stoks.append(P.op("act", lambda e, c0=c0, c1=c1, si=si: e.activation(
                out=xb[:, c0:c1], in_=xt[:, c0:c1], func=AF.Square, accum_out=ss[:, si:si + 1]),
                waits=list(ltoks) + [C.ss_free, xb_free]))
        rts = [rstd_from_ss(C, si, c1 - c0, [stoks[si]]) for si, (c0, c1) in enumerate(segs)]
        ntoks = []
        for si, (c0, c1) in enumerate(segs):
            ntoks.append(P.op("dve", lambda e, c0=c0, c1=c1, si=si: e.scalar_tensor_tensor(
                out=xb[:, c0:c1], in0=xt[:, c0:c1], scalar=ss[:, si:si + 1], in1=gain_bc[:, c0:c1],
                op0=ALU.mult, op1=ALU.mult), waits=[rts[-1], gain_tok, stoks[-1]]))
        xt_free = ntoks[-1]
        C.ss_free = ntoks[-1]
        for c8 in range(KC // 8):
            pi = C.tp_i % len(C.tp)
            C.tp_i += 1
            tp = C.tp[pi]
            for j in range(8):
                c = c8 * 8 + j
                tk = P.op("pe", lambda e, tp=tp, j=j, c=c: e.transpose(
                    out=tp[:, j, :], in_=xb[:, c * 128:(c + 1) * 128], identity=C.ident),
                    waits=[ntoks[-1], C.tp_free[pi], C.ident_tok], sig=(j == 7))
            dst = actT[:, c8 * 8:(c8 + 1) * 8, t * 128:(t + 1) * 128]
            if c8 % 2 == 0:
                ev = P.op("act", lambda e, tp=tp, dst=dst: e.activation(out=dst, in_=tp, func=AF.Copy), waits=[tk])
            else:
                ev = P.op("dve", lambda e, tp=tp, dst=dst: e.tensor_copy(out=dst, in_=tp), waits=[tk])
            C.tp_free[pi] = ev
            last.append(ev)
        xb_free = tk
    return last[-2:]


def gemm_T(C, actT, KC, w_ap, k0, N, act_ready, epilogue, banks=(0, 1, 2), n_off=0):
    P = C.P
    nblocks = [(n0, min(512, N - n0)) for n0 in range(0, N, 512)]
    it = iter(nblocks)
    pend = []
    nb = next(it, None)
    if nb is not None:
        pend.append(load_wblock(C, w_ap, k0, KC, nb[0] + n_off, nb[1]))
    bsel = 0
    tk = None
    for (n0, nw) in nblocks:
        nb = next(it, None)
        if nb is not None:
            pend.append(load_wblock(C, w_ap, k0, KC, nb[0] + n_off, nb[1]))
        slot, wtok = pend.pop(0)
        ws = C.wslots[slot]
        for t in range(C.NT):
            b = banks[bsel % len(banks)]
            bsel += 1
            ps = C.banks[b]
            for k in range(KC):
                tk = P.op("pe", lambda e, ps=ps, k=k, t=t, ws=ws, nw=nw: e.matmul(
                    out=ps[:, 0:nw], lhsT=actT[:, k, t * 128:(t + 1) * 128], rhs=ws[:, k, 0:nw],
                    start=(k == 0), stop=(k == KC - 1)),
                    waits=[wtok, C.bank_free[b]] + list(act_ready), sig=(k == KC - 1))
            C.bank_free[b] = epilogue(t, n0, nw, ps, tk)
        C.wfree[slot] = tk
    return tk


def gemm_F(C, actT, KC, w_ap, k0, N, act_ready, epilogue, banks=(3, 4, 5), n_off=0):
    P = C.P
    nblocks = [(n0, min(512, N - n0)) for n0 in range(0, N, 512)]
    it = iter(nblocks)
    pend = []
    nb = next(it, None)
    if nb is not None:
        pend.append(load_wblock(C, w_ap, k0, KC, nb[0] + n_off, nb[1]))
    bsel = 0
    TH = C.T // 512
    tk = None
    for (n0, nw) in nblocks:
        nb = next(it, None)
        if nb is not None:
            pend.append(load_wblock(C, w_ap, k0, KC, nb[0] + n_off, nb[1]))
        slot, wtok = pend.pop(0)
        ws = C.wslots[slot]
        for j in range(nw // 128):
            for th in range(TH):
                b = banks[bsel % len(banks)]
                bsel += 1
                ps = C.banks[b]
                for k in range(KC):
                    tk = P.op("pe", lambda e, ps=ps, k=k, th=th, ws=ws, j=j: e.matmul(
                        out=ps, lhsT=ws[:, k, j * 128:(j + 1) * 128], rhs=actT[:, k, th * 512:(th + 1) * 512],
                        start=(k == 0), stop=(k == KC - 1)),
                        waits=[wtok, C.bank_free[b]] + list(act_ready), sig=(k == KC - 1))
                C.bank_free[b] = epilogue(n0 // 128 + j, th, ps, tk)
        C.wfree[slot] = tk
    return tk


def phase_A(P, C, h, gain, w_in, z, K=D_MODEL, N=IN_COLS):
    T = C.T
    KC = K // 128
    mark = P.top
    smark = P.sem_mark()
    C.alloc_wslots()
    actT = P.alloc([128, KC, T], BF16)
    gain_bc = P.alloc([128, K], F32)
    xt = P.alloc([128, K], F32)
    xb = P.alloc([128, K], BF16)
    oring = Ring(P, 4, [128, 512], F32)
    gtok = P.dma("sp", gain_bc, gain.to_broadcast([128, K]), P.dsem())
    xsem = P.dsem()

    def produce(t, xt, free):
        return [P.dma("sp", xt, h[t * 128:(t + 1) * 128, :], xsem, waits=[free])]

    ready = norm_transpose(C, produce, [(0, K)], gain_bc, gtok, actT, xt, xb)

    def epi(t, n0, nw, ps, mmtok):
        i = oring.next()
        ob = oring.bufs[i]
        tk = P.op("act", lambda e: e.activation(out=ob[:, 0:nw], in_=ps[:, 0:nw], func=AF.Copy),
                  waits=[mmtok, oring.free[i]])
        oring.free[i] = P.dma("sp", z[t * 128:(t + 1) * 128, n0:n0 + nw], ob[:, 0:nw], oring.sems[i], waits=[tk])
        return tk

    gemm_T(C, actT, KC, w_in, 0, N, ready, epi)
    P.barrier()
    P.top = mark
    P.sem_release(smark)


class SubAlloc:
    def __init__(self, P, base, size):
        self.P, self.base, self.size, self.top = P, base, size, 0

    def alloc(self, shape, dt):
        esz = 4 if dt == F32 else 2
        nb = (int(np.prod(shape[1:])) * esz + 31) // 32 * 32
        off = self.base + self.top
        self.top += nb
        assert self.top <= self.size, ("suballoc overflow", self.top, self.size)
        return self.P.view(off, shape, dt)


def gelu_tanh(P, x, tmp, waits):
    t1 = P.op("dve", lambda e: e.tensor_tensor(out=tmp, in0=x, in1=x, op=ALU.mult), waits=waits)
    t2 = P.op("dve", lambda e: e.tensor_scalar(out=tmp, in0=tmp, scalar1=0.044715, scalar2=1.0,
                                               op0=ALU.mult, op1=ALU.add), waits=[t1])
    t3 = P.op("dve", lambda e: e.tensor_tensor(out=tmp, in0=tmp, in1=x, op=ALU.mult), waits=[t2])
    t4 = P.op("act", lambda e: e.activation(out=tmp, in_=tmp, func=AF.Sigmoid, scale=1.5957691216057308),
              waits=[t3])
    return P.op("dve", lambda e: e.tensor_tensor(out=x, in0=x, in1=tmp, op=ALU.mult), waits=[t4])


def phase_C(P, C, d, n_ffq=8):
    T, NT = C.T, C.NT
    mark = P.top
    smark = P.sem_mark()
    C.alloc_wslots()
    actT = P.alloc([128, 32, T], BF16)
    r1 = P.top
    P.top += 32 * 1024
    xt = P.view(r1, [128, 4096], F32)
    gain_bc = P.view(r1 + 16384, [128, 4096], F32)
    h1T = P.view(r1, [128, 16, T], BF16)
    wp = P.view(r1, [128, 2, 4096], BF16)
    xb = P.alloc([128, 4096], BF16)
    oring = Ring(P, 4, [128, 512], F32)
    hring = Ring(P, 4, [128, 512], F32)
    rring = Ring(P, 2, [128, 512], F32)
    pT = P.alloc([128, 2, T], BF16)
    pld = P.alloc([128, 256], F32)
    pbf = P.alloc([128, 256], BF16)

    S = SubAlloc(P, C.wbase, 64 * 1024)
    convw = S.alloc([128, 3072], F32)
    Xs = [S.alloc([128, 512], F32) for _ in range(3)]
    Cs = [S.alloc([128, 512], F32) for _ in range(3)]
    Bt = S.alloc([128, 512], F32)
    cacc = S.alloc([128, 512], F32)
    U = S.alloc([128, 1024], F32)
    V = S.alloc([128, 1024], F32)
    gtmp = S.alloc([128, 1024], F32)
    Vb = S.alloc([128, 1024], BF16)
    sgu_bc = S.alloc([128, 1024], F32)
    wsp = S.alloc([128, 8, 128], F32)
    wspb = S.alloc([128, 8, 128], BF16)
    WsT = S.alloc([128, 8, 128], BF16)
    tril = S.alloc([128, 128], F32)
    bsp = S.alloc([128, 8], F32)

    csem = P.dsem()
    P.dma("sp", convw, d["conv_w"].to_broadcast([128, 3072]), csem)
    P.dma("sp", sgu_bc, d["sgu_gain"].to_broadcast([128, 1024]), csem)
    P.dma("sp", wsp, d["w_sp"].rearrange("g t s -> t g s"), csem)
    P.dma("sp", tril, d["tril"], csem)
    ctok = P.dma("sp", bsp, d["b_sp"].rearrange("g t -> t g"), csem, allow_slow_non_contiguous=True)
    gtok = P.dma("sp", gain_bc, d["mix_out_norm"].to_broadcast([128, 4096]), P.dsem())
    tk = None
    for g in range(8):
        tk = P.op("dve", lambda e, g=g: e.tensor_tensor(out=wspb[:, g, :], in0=wsp[:, g, :], in1=tril, op=ALU.mult),
                  waits=[ctok])
    tp = C.tp[0]
    for g in range(8):
        tk2 = P.op("pe", lambda e, g=g: e.transpose(out=tp[:, g, :], in_=wspb[:, g, :], identity=C.ident),
                   waits=[tk, C.ident_tok], sig=(g == 7))
    wst_tok = P.op("dve", lambda e: e.tensor_copy(out=WsT, in_=tp), waits=[tk2])
    C.tp_free[0] = wst_tok

    xsem = P.dsem()
    osem = P.dsem()
    gsem = P.dsem()
    st = {"conv_free": None, "gm_free": None, "gss_free": None}
    zc, zg, halo, ob = d["zc"], d["zg"], d["halo"], d["ob"]

    def produce(t, xt, free):
        toks = [P.dma("sp", xt[:, 1024:3072], ob[t * 128:(t + 1) * 128, :], osem, waits=[free])]
        for hc in range(2):
            c0 = hc * 512
            lt = None
            for s in range(3):
                for buf, coff in ((Xs[s], 0), (Cs[s], 2048)):
                    if t == 0 and s > 0:
                        P.dma("sp", buf[0:s, :], halo[2 - s:2, coff + c0:coff + c0 + 512], xsem,
                              waits=[st["conv_free"]])
                        lt = P.dma("sp", buf[s:128, :], zc[0:128 - s, coff + c0:coff + c0 + 512], xsem)
                    else:
                        lt = P.dma("sp", buf, zc[t * 128 - s:t * 128 - s + 128, coff + c0:coff + c0 + 512], xsem,
                                   waits=[st["conv_free"]])
            lt = P.dma("sp", Bt, zc[t * 128:(t + 1) * 128, 1024 + c0:1024 + c0 + 512], xsem)
            k = None
            for s in range(3):
                k = P.op("dve", lambda e, s=s: e.tensor_tensor(out=Xs[s], in0=Xs[s], in1=Cs[s], op=ALU.mult),
                         waits=[lt, k])
            k = P.op("dve", lambda e, c0=c0: e.tensor_tensor(
                out=cacc, in0=Xs[0], in1=convw[:, 2048 + c0:2048 + c0 + 512], op=ALU.mult), waits=[k, ctok])
            for s in (1, 2):
                k = P.op("dve", lambda e, s=s, c0=c0: e.tensor_tensor(
                    out=Xs[s], in0=Xs[s], in1=convw[:, (2 - s) * 1024 + c0:(2 - s) * 1024 + c0 + 512],
                    op=ALU.mult), waits=[k])
                k = P.op("dve", lambda e, s=s: e.tensor_tensor(out=cacc, in0=cacc, in1=Xs[s], op=ALU.add),
                         waits=[k])
            k = P.op("dve", lambda e, c0=c0: e.tensor_tensor(out=xt[:, c0:c0 + 512], in0=cacc, in1=Bt,
                                                              op=ALU.mult), waits=[k, free])
            st["conv_free"] = k
            toks.append(k)
        P.dma("sp", U, zg[t * 128:(t + 1) * 128, 0:1024], gsem, waits=[st["gm_free"]])
        lt = P.dma("sp", V, zg[t * 128:(t + 1) * 128, 1024:2048], gsem)
        k = gelu_tanh(P, U, gtmp, [lt])
        k = gelu_tanh(P, V, gtmp, [k])
        k = P.op("act", lambda e: e.activation(out=gtmp, in_=V, func=AF.Square, accum_out=C.ss[:, 4:5]),
                 waits=[k, st["gss_free"]])
        k = rstd_from_ss(C, 4, 1024, [k])
        k = P.op("dve", lambda e: e.scalar_tensor_tensor(out=Vb, in0=V, scalar=C.ss[:, 4:5], in1=sgu_bc,
                                                         op0=ALU.mult, op1=ALU.mult), waits=[k, ctok])
        st["gss_free"] = k
        mm = []
        for g in range(8):
            b = g // 4
            ps = C.banks[b]
            mm.append(P.op("pe", lambda e, ps=ps, g=g: e.matmul(
                out=ps[:, (g % 4) * 128:(g % 4 + 1) * 128], lhsT=WsT[:, g, :], rhs=Vb[:, g * 128:(g + 1) * 128],
                start=True, stop=True), waits=[k, wst_tok, C.bank_free[b]], sig=(g % 4 == 3)))
        for g in range(8):
            ps = C.banks[g // 4]
            k = P.op("dve", lambda e, ps=ps, g=g: e.scalar_tensor_tensor(
                out=xt[:, 3072 + g * 128:3072 + (g + 1) * 128], in0=ps[:, (g % 4) * 128:(g % 4 + 1) * 128],
                scalar=bsp[:, g:g + 1], in1=U[:, g * 128:(g + 1) * 128], op0=ALU.add, op1=ALU.mult),
                waits=[mm[g // 4 * 4 + 3], free])
            if g % 4 == 3:
                C.bank_free[g // 4] = k
        st["gm_free"] = k
        toks.append(k)
        return toks

    norm_transpose(C, produce, [(0, 1024), (1024, 3072), (3072, 4096)], gain_bc, gtok, actT, xt, xb)
    P.barrier()

    def resid_epi(src, dst):
        def epi(t, n0, nw, ps, mmtok):
            i = hring.next()
            hb = hring.bufs[i]
            lt = P.dma("act", hb[:, 0:nw], src[t * 128:(t + 1) * 128, n0:n0 + nw], hring.sems[i],
                       waits=[hring.free[i]])
            j = oring.next()
            ob_ = oring.bufs[j]
            tk = P.op("dve", lambda e: e.tensor_tensor(out=ob_[:, 0:nw], in0=ps[:, 0:nw], in1=hb[:, 0:nw],
                                                       op=ALU.add), waits=[mmtok, lt, oring.free[j]])
            hring.free[i] = tk
            oring.free[j] = P.dma("sp", dst[t * 128:(t + 1) * 128, n0:n0 + nw], ob_[:, 0:nw], oring.sems[j],
                                  waits=[tk])
            return tk
        return epi

    def full_row_producer(src, sem):
        def produce(t, xt, free):
            return [P.dma("sp", xt, src[t * 128:(t + 1) * 128, :], sem, waits=[free])]
        return produce

    hs = d["hs"]
    C.w_ready = d.get("tok_w_out", [])
    gemm_T(C, actT, 32, d["w_out"], 0, 4096, [], resid_epi(d["h_in"], hs))
    P.barrier()

    gtok = P.dma("sp", gain_bc, d["mlp_norm"].to_broadcast([128, 4096]), P.dsem())
    norm_transpose(C, full_row_producer(hs, P.dsem()), [(0, 4096)], gain_bc, gtok, actT, xt, xb)
    P.barrier()
    st2 = {"h1_free": None}
    FQ = D_FF // n_ffq
    KQ = FQ // 128

    def relu2_epi(j, th, ps, mmtok):
        i = rring.next()
        rb = rring.bufs[i]
        a = P.op("act", lambda e: e.activation(out=rb, in_=ps, func=AF.Relu), waits=[mmtok, rring.free[i]])
        rring.free[i] = P.op("dve", lambda e: e.tensor_tensor(
            out=h1T[:, j, th * 512:(th + 1) * 512], in0=rb, in1=rb, op=ALU.mult), waits=[a, st2["h1_free"]])
        st2["last_sq"] = rring.free[i]
        return a

    for q in range(n_ffq):
        C.w_ready = d.get("tok_w_mlp_in", [])
        gemm_F(C, actT, 32, d["w_mlp_in"], 0, FQ, [], relu2_epi, n_off=q * FQ)
        C.w_ready = d.get("tok_w_mlp_out", [])
        P._wait("act", [f for f in oring.free if f is not None])
        st2["h1_free"] = gemm_T(C, h1T, KQ, d["w_mlp_out"], q * KQ, 4096, [st2["last_sq"]], resid_epi(hs, hs))
    P.barrier()

    gtok = P.dma("sp", gain_bc, d["ple_norm"].to_broadcast([128, 4096]), P.dsem())
    norm_transpose(C, full_row_producer(hs, P.dsem()), [(0, 4096)], gain_bc, gtok, actT, xt, xb)
    P.barrier()
    wp_tok = P.dma("pool" if C.w_cast else "sp", wp, d["w_ple_proj"].rearrange("(k p) n -> p k n", p=128), P.dsem(),
                   waits=d.get("tok_w_ple_proj", []))
    psem = P.dsem()
    ptok = None
    pfree = None
    for t in range(NT):
        lt = P.dma("sp", pld, d["p"][t * 128:(t + 1) * 128, :], psem, waits=[pfree])
        k = P.op("dve", lambda e: e.tensor_copy(out=pbf, in_=pld), waits=[lt, ptok])
        pfree = k
        tp = C.tp[1]
        for kk in range(2):
            k2 = P.op("pe", lambda e, kk=kk: e.transpose(out=tp[:, kk, :], in_=pbf[:, kk * 128:(kk + 1) * 128],
                                                         identity=C.ident), waits=[k, C.tp_free[1]], sig=(kk == 1))
        ptok = P.op("act", lambda e, t=t: e.activation(out=pT[:, :, t * 128:(t + 1) * 128], in_=tp[:, 0:2, :],
                                                       func=AF.Copy), waits=[k2])
        C.tp_free[1] = ptok
        ptok2 = k2
    h_out = d["h_out"]
    b2sel = [0]

    def ple_epi(t, n0, nw, ps, mmtok):
        b = 3 + b2sel[0] % 3
        b2sel[0] += 1
        ps2 = C.banks[b]
        for kk in range(2):
            m2 = P.op("pe", lambda e, kk=kk: e.matmul(
                out=ps2[:, 0:nw], lhsT=pT[:, kk, t * 128:(t + 1) * 128], rhs=wp[:, kk, n0:n0 + nw],
                start=(kk == 0), stop=(kk == 1)), waits=[wp_tok, ptok, C.bank_free[b]], sig=(kk == 1))
        i = rring.next()
        rb = rring.bufs[i]
        a = P.op("act", lambda e: e.activation(out=rb[:, 0:nw], in_=ps[:, 0:nw], func=AF.Sigmoid),
                 waits=[mmtok, rring.free[i]])
        hi = hring.next()
        hb = hring.bufs[hi]
        lt = P.dma("act", hb[:, 0:nw], hs[t * 128:(t + 1) * 128, n0:n0 + nw], hring.sems[hi],
                   waits=[hring.free[hi]])
        j = oring.next()
        ob_ = oring.bufs[j]
        k1 = P.op("dve", lambda e: e.tensor_tensor(out=ob_[:, 0:nw], in0=ps2[:, 0:nw], in1=rb[:, 0:nw],
                                                   op=ALU.mult), waits=[a, m2, oring.free[j]])
        C.bank_free[b] = k1
        rring.free[i] = k1
        k2 = P.op("dve", lambda e: e.tensor_tensor(out=ob_[:, 0:nw], in0=ob_[:, 0:nw], in1=hb[:, 0:nw],
                                                   op=ALU.add), waits=[k1, lt])
        hring.free[hi] = k2
        oring.free[j] = P.dma("sp", h_out[t * 128:(t + 1) * 128, n0:n0 + nw], ob_[:, 0:nw], oring.sems[j],
                              waits=[k2])
        return a

    C.w_ready = d.get("tok_w_ple_gate", [])
    gemm_T(C, actT, 32, d["w_ple_gate"], 0, 4096, [], ple_epi)
    P.barrier()
    P.top = mark
    P.sem_release(smark)
    C.w_ready = []


NOFF = 35


def nsa_tables(g, T=SEQ):
    n_cmp = T // 16 - 1
    ncc = (n_cmp + 127) // 128
    nqb = T // 512
    slopes = np.array([2.0 ** (-(4 * g + r + 1) / 2.0) for r in range(4)], np.float64)
    ik = np.arange(128)[:, None]
    iq = np.arange(512)[None, :]
    t = {}
    t["negslope"] = np.tile(-slopes[None, :], (128, 1))
    t["D0"] = (iq - ik).astype(np.float64)
    t["D0c"] = (iq - 16 * ik).astype(np.float64)
    offb = np.zeros((4, NOFF))
    for r in range(4):
        for dl in range(-3, 32):
            offb[r, dl + 3] = -slopes[r] * 128.0 * dl
    t["OFFB"] = np.tile(offb.reshape(1, -1), (128, 1))
    offc = np.zeros((4, nqb * ncc))
    cpm = np.zeros((nqb * ncc, 128, 512))
    for pq in range(nqb):
        for c in range(ncc):
            off = 512 * pq - 2048 * c - 31
            for r in range(4):
                offc[r, pq * ncc + c] = -slopes[r] * off
            ok = ((iq - 16 * ik + off) >= 0) & ((128 * c + ik) < n_cmp)
            cpm[pq * ncc + c] = np.where(ok, 0.0, NEG)
    t["OFFC"] = np.tile(offc.reshape(1, -1), (128, 1))
    t["CPM"] = cpm.transpose(1, 0, 2)
    cm = np.zeros((4, 128, 512))
    for i in range(4):
        cm[i] = np.where((-128 * i + iq - ik) >= 0, 0.0, NEG)
    t["CM"] = cm.transpose(1, 0, 2)
    wm = np.zeros((8, 128, 512))
    for i in range(8):
        dist = 128 * (4 - i) + iq - ik
        wm[i] = np.where((dist >= 0) & (dist < 512), 0.0, NEG)
    t["WM"] = wm.transpose(1, 0, 2)
    nkc = T // 128
    e = np.zeros((128, nkc, 128))
    for c in range(nkc):
        for k in range(128):
            e[2 * c + k // 64, c, k] = 1.0 if 2 * c + k // 64 < 128 else 0.0
    t["E"] = e[:, :, :]
    nslc = T // 64
    addt = np.zeros((128, nkc, nslc))
    for qt in range(nkc):
        for p in range(128):
            qb = (128 * qt + p) // 64
            row = np.zeros(nslc)
            row[qb + 1:] = -1e30
            if qb - 1 >= 0:
                row[qb - 1] = 1e9
            row[qb] = 2e9
            row[0] = 3e9
            addt[p, qt] = row
    t["ADDT"] = addt
    ovl = np.zeros((ncc * 128, nslc))
    for n in range(n_cmp):
        for m in range(nslc):
            o = min(16 * n + 32, 64 * m + 64) - max(16 * n, 64 * m)
            ovl[n, m] = max(o, 0) / 32.0
    t["OVL"] = ovl.reshape(ncc, 128, nslc).transpose(1, 0, 2)
    return {k: np.ascontiguousarray(v.reshape(v.shape[0], -1), dtype=np.float32) for k, v in t.items()}


def phase_B(P, C, d, T=SEQ):
    NKC = T // 128
    NQB = T // 512
    n_cmp = T // 16 - 1
    NCC = (n_cmp + 127) // 128
    NSLC = T // 64
    VC = 129 + NSLC
    mark = P.top
    smark = P.sem_mark()
    A = P.alloc
    KT = A([128, 4, T], BF16)
    Vs = A([128, NKC, 130], BF16)
    Vw = A([128, NKC, 130], BF16)
    Ttab = A([128, 4, 512], F32)
    Ttabc = A([128, 4, 512], F32)
    negslope = A([128, 4], F32)
    D0 = A([128, 512], F32)
    D0c = A([128, 512], F32)
    OFFB = A([128, 4 * NOFF], F32)
    OFFC = A([128, 4 * NQB * NCC], F32)
    CM = A([128, 4, 512], BF16)
    WM = A([128, 8, 512], BF16)
    CPM = A([128, NQB * NCC, 512], BF16)
    E = A([128, NKC, 128], BF16)
    ADDT = A([128, NKC, NSLC], F32)
    w1b = [A([128, 32, 128], BF16) for _ in range(2)]
    w2b = [A([128, 128], BF16) for _ in range(2)]
    peT = [A([128, 32], BF16) for _ in range(2)]
    kcT = A([128, NCC * 128], BF16)
    Vc = A([128, NCC, VC + 1], BF16)
    OVLb = A([128, NCC, NSLC], BF16)
    qg_bc = A([128, 128], F32)
    kg_bc = A([128, 384], F32)
    cbias = A([128, 2], F32)
    hx = A([128, NCC * 128], F32)
    hx2 = A([128, NCC * 128], F32)
    hidT = A([128, NCC * 128], BF16)
    kvring = Ring(P, 2, [128, 768], F32)
    xb4 = A([128, 4, 128], BF16)
    junk = A([128, 512], BF16)
    qring = Ring(P, 2, [128, 512], F32)
    qn = A([128, 512], BF16)
    qTs = [A([128, 4, 512], BF16) for _ in range(2)]
    glts = [A([128, 4, 12], F32) for _ in range(2)]
    gsigs = [A([128, 4, 12], F32) for _ in range(2)]
    Sp = Ring(P, 4, [128, 512], F32)
    Pb = Ring(P, 6, [128, 512], BF16)
    o_t = A([128, 4, 512], F32)
    imp = A([128, 4, NSLC], F32)
    impA = A([128, NSLC], F32)
    impB = A([128, NSLC], F32)
    m8 = A([128, 16], F32)
    selt = A([128, NSLC], F32)
    negm = A([128, 4, NSLC], BF16)
    negmT = A([128, 512], BF16)
    rv = A([128, 8], F32)
    kcn = A([128, 128], BF16)

    cs = P.dsem()
    for dst, key in ((negslope, "negslope"), (D0, "D0"), (D0c, "D0c"), (OFFB, "OFFB"), (OFFC, "OFFC"),
                     (ADDT.rearrange("p a b -> p (a b)"), "ADDT")):
        ctok = P.dma("sp", dst, d[key], cs)
    P.dma("sp", qg_bc, d["q_gain"].to_broadcast([128, 128]), cs)
    ctok = P.dma("sp", kg_bc, d["k_gain"].to_broadcast([128, 384]), cs)
    cs2 = P.dsem()
    for dst, key in ((CM, "CM"), (WM, "WM"), (CPM, "CPM"), (E, "E"), (OVLb, "OVL")):
        P.dma("pool", dst.rearrange("p a b -> p (a b)"), d[key], cs2)
    for kind in range(2):
        P.dma("pool", w1b[kind], d["cmp_w1"][kind].rearrange("l d e -> d l e"), cs2)
        P.dma("pool", w2b[kind], d["cmp_w2"][kind], cs2)
        c2tok = P.dma("pool", peT[kind], d["cmp_pos"][kind].rearrange("l d -> d l"), cs2,
                      allow_slow_non_contiguous=True)
    k = None
    for r in range(4):
        P.op("dve", lambda e, r=r: e.tensor_scalar(out=Ttab[:, r, :], in0=D0, scalar1=negslope[:, r:r + 1],
                                                   scalar2=None, op0=ALU.mult), waits=[ctok])
        k = P.op("dve", lambda e, r=r: e.tensor_scalar(out=Ttabc[:, r, :], in0=D0c, scalar1=negslope[:, r:r + 1],
                                                       scalar2=None, op0=ALU.mult), waits=[ctok])
    k = P.op("dve", lambda e: e.tensor_scalar(out=qg_bc, in0=qg_bc, scalar1=128.0 ** -0.5, scalar2=None,
                                              op0=ALU.mult), waits=[ctok])
    P.op("dve", lambda e: e.memset(Vs[:, :, 128:130], 1.0))
    P.op("dve", lambda e: e.memset(Vw[:, :, 128:130], 1.0))
    P.op("dve", lambda e: e.memset(hidT, 0.0))
    P.op("dve", lambda e: e.memset(Vc[:, :, 128:129], 1.0))
    P.op("dve", lambda e: e.tensor_copy(out=Vc[:, :, 129:129 + NSLC], in_=OVLb), waits=[c2tok])
    P.barrier()

    ss = C.ss

    def rstd_cols(c0, c1, n, waits):
        tk = P.op("act", lambda e: e.activation(out=ss[:, c0:c1], in_=ss[:, c0:c1], func=AF.Sqrt, scale=1.0 / n,
                                                bias=C.eps_t[:, 0:1]), waits=waits)
        return P.op("dve", lambda e: e.reciprocal(out=ss[:, c0:c1], in_=ss[:, c0:c1]), waits=[tk])

    def next_tp():
        pi = C.tp_i % len(C.tp)
        C.tp_i += 1
        return pi, C.tp[pi]

    kv = d["kv"]
    ss_free = None
    xb_free = None
    for t in range(NKC):
        i = kvring.next()
        kt = kvring.bufs[i]
        lt = P.dma("sp", kt, kv[t * 128:(t + 1) * 128, :], kvring.sems[i], waits=kvring.free[i] or [])
        P.op("act", lambda e, kt=kt: e.activation(out=junk[:, 0:128], in_=kt[:, 256:384], func=AF.Square,
                                                  accum_out=ss[:, 0:1]), waits=[lt, ss_free])
        a = P.op("act", lambda e, kt=kt: e.activation(out=junk[:, 128:256], in_=kt[:, 512:640], func=AF.Square,
                                                      accum_out=ss[:, 1:2]))
        rt = rstd_cols(0, 2, 128, [a])
        P.op("dve", lambda e, kt=kt: e.tensor_copy(out=xb4[:, 0:2, :].rearrange("p a b -> p (a b)"),
                                                   in_=kt[:, 0:256]), waits=[lt, xb_free])
        P.op("dve", lambda e, kt=kt: e.scalar_tensor_tensor(out=xb4[:, 2, :], in0=kt[:, 256:384], scalar=ss[:, 0:1],
                                                            in1=kg_bc[:, 128:256], op0=ALU.mult, op1=ALU.mult),
             waits=[rt])
        nk = P.op("dve", lambda e, kt=kt: e.scalar_tensor_tensor(out=xb4[:, 3, :], in0=kt[:, 512:640],
                                                                 scalar=ss[:, 1:2], in1=kg_bc[:, 256:384],
                                                                 op0=ALU.mult, op1=ALU.mult))
        ss_free = nk
        P.op("act", lambda e, kt=kt, t=t: e.activation(out=Vs[:, t, 0:128], in_=kt[:, 384:512], func=AF.Copy),
             waits=[lt])
        pk = P.op("act", lambda e, kt=kt, t=t: e.activation(out=Vw[:, t, 0:128], in_=kt[:, 640:768], func=AF.Copy))
        pi, tp = next_tp()
        for j in range(4):
            tk = P.op("pe", lambda e, tp=tp, j=j: e.transpose(out=tp[:, j, :], in_=xb4[:, j, :], identity=C.ident),
                      waits=[nk, C.tp_free[pi], C.ident_tok], sig=(j == 3))
        xb_free = tk
        ev = P.op("act", lambda e, tp=tp, t=t: e.activation(out=KT[:, :, t * 128:(t + 1) * 128], in_=tp[:, 0:4, :],
                                                            func=AF.Copy), waits=[tk])
        C.tp_free[pi] = ev
        kvring.free[i] = [ev, nk, pk]
    P.barrier()

    for kind in range(2):
        X = KT[:, kind, :]
        b0, b1 = C.banks[0], C.banks[1]
        for l in range(32):
            tk = P.op("pe", lambda e, l=l, kind=kind: e.matmul(out=b0[:, 0:1], lhsT=w1b[kind][:, l, :],
                                                               rhs=peT[kind][:, l:l + 1], start=(l == 0),
                                                               stop=(l == 31)), sig=(l == 31))
        cb = P.op("dve", lambda e, kind=kind: e.tensor_copy(out=cbias[:, kind:kind + 1], in_=b0[:, 0:1]), waits=[tk])
        for l in range(32):
            tk = P.op("pe", lambda e, l=l, kind=kind, X=X: e.matmul(
                out=b1[:, 0:n_cmp], lhsT=w1b[kind][:, l, :], rhs=X[:, l:l + 16 * (n_cmp - 1) + 1:16],
                start=(l == 0), stop=(l == 31)), sig=(l == 31))
        a = P.op("act", lambda e, kind=kind: e.activation(out=hx[:, 0:n_cmp], in_=b1[:, 0:n_cmp], func=AF.Identity,
                                                          bias=cbias[:, kind:kind + 1]), waits=[tk, cb])
        g = gelu_tanh(P, hx[:, 0:n_cmp], hx2[:, 0:n_cmp], [a])
        hk = P.op("dve", lambda e: e.tensor_copy(out=hidT[:, 0:n_cmp], in_=hx[:, 0:n_cmp]), waits=[g])
        for c in range(NCC):
            tk = P.op("pe", lambda e, c=c, kind=kind: e.matmul(out=b0[:, 0:128], lhsT=hidT[:, c * 128:(c + 1) * 128],
                                                               rhs=w2b[kind], start=True, stop=True), waits=[hk, cb])
            if kind == 0:
                a = P.op("act", lambda e: e.activation(out=junk[:, 0:128], in_=b0[:, 0:128], func=AF.Square,
                                                       accum_out=ss[:, 2:3]), waits=[tk])
                rt = rstd_cols(2, 3, 128, [a])
                nk = P.op("dve", lambda e: e.scalar_tensor_tensor(out=kcn, in0=b0[:, 0:128], scalar=ss[:, 2:3],
                                                                  in1=kg_bc[:, 0:128], op0=ALU.mult, op1=ALU.mult),
                          waits=[rt])
                pi, tp = next_tp()
                tk2 = P.op("pe", lambda e, tp=tp: e.transpose(out=tp[:, 0, :], in_=kcn, identity=C.ident),
                           waits=[nk, C.tp_free[pi]])
                ev = P.op("dve", lambda e, tp=tp, c=c: e.tensor_copy(out=kcT[:, c * 128:(c + 1) * 128],
                                                                     in_=tp[:, 0, :]), waits=[tk2])
                C.tp_free[pi] = ev
                cb = ev
            else:
                cb = P.op("dve", lambda e, c=c: e.tensor_copy(out=Vc[:, c, 0:128], in_=b0[:, 0:128]), waits=[tk])
        P.barrier()

    q_dram, gl_dram, ob_dram = d["q"], d["gl"], d["ob"]
    Sbank = [0, 1]
    state = {"si": 0, "pend": [], "o_free": None, "qT_free": [None, None], "gl_free": None, "imp_free": None,
             "negm_free": None, "ss_free": None, "qn_free": None}
    LAG = 4

    def emit_pv(tile):
        i2, ex_tok, pvs, V, after = tile
        pb = Pb.bufs[i2]
        tk = None
        for (acc, bank, j, start, stop) in pvs:
            tk = P.op("pe", lambda e, acc=acc, j=j, start=start, stop=stop, pb=pb, V=V: e.matmul(
                out=acc, lhsT=pb[:, j * 128:(j + 1) * 128], rhs=V, start=start, stop=stop,
                skip_group_check=True),
                waits=[ex_tok] + ([C.bank_free[bank]] if start else []))
        Pb.free[i2] = tk
        if after is not None:
            after(tk)

    def push(kT, qTr, aux, ttab, bias_ap, V, pvs, after, waits):
        b = Sbank[state["si"] % 2]
        state["si"] += 1
        S = C.banks[b]
        n = 1 + len(aux)
        tk = P.op("pe", lambda e: e.matmul(out=S, lhsT=kT, rhs=qTr, start=True, stop=(n == 1)),
                  waits=list(waits) + [C.bank_free[b]], sig=(n == 1))
        for ai, (al, ar) in enumerate(aux):
            tk = P.op("pe", lambda e, al=al, ar=ar, ai=ai: e.matmul(out=S, lhsT=al, rhs=ar, start=False,
                                                                    stop=(ai == n - 2)), sig=(ai == n - 2))
        i1 = Sp.next()
        sp_ = Sp.bufs[i1]
        dk = P.op("dve", lambda e: e.tensor_tensor(out=sp_, in0=S, in1=ttab, op=ALU.add), waits=[tk, Sp.free[i1]])
        C.bank_free[b] = dk
        i2 = Pb.next()
        pb = Pb.bufs[i2]
        ak = P.op("act", lambda e: e.activation(out=pb, in_=sp_, func=AF.Exp, bias=bias_ap), waits=[dk, Pb.free[i2]])
        Sp.free[i1] = ak
        state["pend"].append((i2, ak, pvs, V, after))
        while len(state["pend"]) > LAG:
            emit_pv(state["pend"].pop(0))

    def bank_starts(pvs, seen):
        out = []
        for (acc, bank, j, start, stop) in pvs:
            out.append((acc, bank, j, bank not in seen, stop))
            seen.add(bank)
        return out

    def flush():
        while state["pend"]:
            emit_pv(state["pend"].pop(0))

    def acc_aps(setidx, ncol):
        base = 2 + 2 * setidx
        return [(C.banks[base + j // 2][:, (j % 2) * ncol:(j % 2) * ncol + ncol], base + j // 2) for j in range(4)]

    def make_evac(r, br, accs, first, pq):
        gsig = gsigs[pq % 2]

        def after(pvtok):
            ncol = VC if br == 0 else 129
            k = None
            for h2 in range(2):
                bk = C.banks[accs[2 * h2][1]]
                src = bk[:, 0:2 * ncol].rearrange("p (a b) -> p a b", a=2)[:, :, 128:129].rearrange("p a b -> p (a b)")
                k = P.op("dve", lambda e, src=src, h2=h2: e.tensor_scalar(
                    out=rv[:, 2 * h2:2 * h2 + 2], in0=src, scalar1=1e-30, scalar2=None, op0=ALU.add),
                    waits=[pvtok, k])
            k = P.op("dve", lambda e: e.reciprocal(out=rv[:, 0:4], in_=rv[:, 0:4]), waits=[k])
            gs = gsig[:, :, r * 3 + br:r * 3 + br + 1].rearrange("p a b -> p (a b)")
            k = P.op("dve", lambda e, gs=gs: e.tensor_tensor(out=rv[:, 4:8], in0=rv[:, 0:4], in1=gs, op=ALU.mult),
                     waits=[k, state["gsig_toks"][pq]])
            for j in range(4):
                acc, bank = accs[j]
                if br == 0:
                    if r == 0:
                        k = P.op("dve", lambda e, acc=acc, j=j: e.tensor_scalar(
                            out=imp[:, j, :], in0=acc[:, 129:129 + NSLC], scalar1=rv[:, j:j + 1], scalar2=None,
                            op0=ALU.mult), waits=[k, state["imp_free"]])
                    else:
                        k = P.op("dve", lambda e, acc=acc, j=j: e.scalar_tensor_tensor(
                            out=imp[:, j, :], in0=acc[:, 129:129 + NSLC], scalar=rv[:, j:j + 1], in1=imp[:, j, :],
                            op0=ALU.mult, op1=ALU.add), waits=[k])
                if first:
                    k = P.op("dve", lambda e, acc=acc, j=j: e.tensor_scalar(
                        out=o_t[:, j, r * 128:(r + 1) * 128], in0=acc[:, 0:128], scalar1=rv[:, 4 + j:5 + j],
                        scalar2=None, op0=ALU.mult), waits=[k, state["o_free"]])
                else:
                    k = P.op("dve", lambda e, acc=acc, j=j: e.scalar_tensor_tensor(
                        out=o_t[:, j, r * 128:(r + 1) * 128], in0=acc[:, 0:128], scalar=rv[:, 4 + j:5 + j],
                        in1=o_t[:, j, r * 128:(r + 1) * 128], op0=ALU.mult, op1=ALU.add), waits=[k])
                C.bank_free[bank] = k
            state["last_evac"] = k
            if br == 0 and r == 3:
                state["cmp_evac"] = k
        return after

    setsel = [0]
    state["gsig_toks"] = {}
    state["qtoks"] = {}
    state["blk_done"] = {}

    def q_prep(pq):
        qT = qTs[pq % 2]
        glt = glts[pq % 2]
        gsig = gsigs[pq % 2]
        prev_gs = state["gsig_toks"].get(pq - 2)
        prev_done = state["blk_done"].get(pq - 2)
        qtok = None
        for j in range(4):
            t = 4 * pq + j
            i = qring.next()
            qt_ = qring.bufs[i]
            lt = P.dma("sp", qt_, q_dram[t * 128:(t + 1) * 128, :], qring.sems[i], waits=[qring.free[i]])
            gt = P.dma("sp", glt[:, j, :], gl_dram[t * 128:(t + 1) * 128, :], qring.sems[i],
                       waits=[prev_gs])
            a = None
            for r in range(4):
                a = P.op("act", lambda e, r=r, qt_=qt_: e.activation(
                    out=junk[:, r * 128:(r + 1) * 128], in_=qt_[:, r * 128:(r + 1) * 128], func=AF.Square,
                    accum_out=ss[:, r:r + 1]), waits=[gt, state["ss_free"]])
            rt = rstd_cols(0, 4, 128, [a])
            nk = None
            for r in range(4):
                nk = P.op("dve", lambda e, r=r, qt_=qt_: e.scalar_tensor_tensor(
                    out=qn[:, r * 128:(r + 1) * 128], in0=qt_[:, r * 128:(r + 1) * 128], scalar=ss[:, r:r + 1],
                    in1=qg_bc, op0=ALU.mult, op1=ALU.mult), waits=[rt, state["qn_free"]])
            state["ss_free"] = nk
            qring.free[i] = nk
            pi, tp = next_tp()
            for r in range(4):
                tk = P.op("pe", lambda e, tp=tp, r=r: e.transpose(out=tp[:, r, :], in_=qn[:, r * 128:(r + 1) * 128],
                                                                  identity=C.ident),
                          waits=[nk, C.tp_free[pi]], sig=(r == 3))
            state["qn_free"] = tk
            qtok = P.op("act", lambda e, tp=tp, j=j, qT=qT: e.activation(
                out=qT[:, :, j * 128:(j + 1) * 128], in_=tp[:, 0:4, :], func=AF.Copy),
                waits=[tk, state["qT_free"][pq % 2]])
            C.tp_free[pi] = qtok
        state["gsig_toks"][pq] = P.op("act", lambda e: e.activation(out=gsig, in_=glt, func=AF.Sigmoid),
                                      waits=[gt, prev_done])
        state["qtoks"][pq] = qtok

    q_prep(0)
    for pq in range(NQB):
        qT = qTs[pq % 2]
        qtok = state["qtoks"][pq]
        for r in range(4):
            accs = acc_aps(setsel[0] % 2, VC)
            setsel[0] += 1
            cl = [c for c in range(NCC) if 511 + 512 * pq - 2048 * c - 31 >= 0]
            seen = set()
            for c in cl:
                idx = pq * NCC + c
                pvs = bank_starts([(accs[j][0], accs[j][1], j, c == cl[0], c == cl[-1]) for j in range(4)], seen)
                push(kcT[:, c * 128:(c + 1) * 128], qT[:, r, :], [(C.ident, CPM[:, idx, :])], Ttabc[:, r, :],
                     OFFC[:, r * NQB * NCC + idx:r * NQB * NCC + idx + 1], Vc[:, c, 0:VC], pvs,
                     make_evac(r, 0, accs, True, pq) if c == cl[-1] else None, [qtok])
        for r in range(4):
            accs = acc_aps(setsel[0] % 2, 129)
            setsel[0] += 1
            cl = list(range(max(0, 4 * pq - 4), 4 * pq + 4))
            seen = set()
            for c in cl:
                dl = 4 * pq - c
                pvs = []
                for j in range(4):
                    if 4 * pq + j - 4 <= c <= 4 * pq + j:
                        pvs.append((accs[j][0], accs[j][1], j, c == max(0, 4 * pq + j - 4), c == 4 * pq + j))
                pvs = bank_starts(pvs, seen)
                push(KT[:, 3, c * 128:(c + 1) * 128], qT[:, r, :], [(C.ident, WM[:, 4 - dl, :])], Ttab[:, r, :],
                     OFFB[:, r * NOFF + dl + 3:r * NOFF + dl + 4], Vw[:, c, 0:129], pvs,
                     make_evac(r, 2, accs, False, pq) if c == cl[-1] else None, [qtok])
        if pq + 1 < NQB:
            q_prep(pq + 1)
        k = state["cmp_evac"]
        for j in range(4):
            t = 4 * pq + j
            k = P.op("dve", lambda e, j=j, t=t: e.tensor_tensor(out=impA, in0=imp[:, j, :], in1=ADDT[:, t, :],
                                                                op=ALU.add), waits=[k])
            k = P.op("dve", lambda e: e.max(out=m8[:, 0:8], in_=impA), waits=[k])
            k = P.op("dve", lambda e: e.match_replace(out=impB, in_to_replace=m8[:, 0:8], in_values=impA,
                                                      imm_value=-3.0e38), waits=[k])
            k = P.op("dve", lambda e: e.max(out=m8[:, 8:16], in_=impB), waits=[k])
            k = P.op("dve", lambda e: e.tensor_scalar(out=selt, in0=impA, scalar1=m8[:, 15:16], scalar2=None,
                                                      op0=ALU.is_ge), waits=[k])
            k = P.op("dve", lambda e, j=j: e.tensor_scalar(out=negm[:, j, :], in0=selt, scalar1=1.0, scalar2=-NEG,
                                                           op0=ALU.subtract, op1=ALU.mult),
                     waits=[k, state["negm_free"]])
        state["imp_free"] = k
        pi, tp = next_tp()
        for j in range(4):
            tk = P.op("pe", lambda e, tp=tp, j=j: e.transpose(out=tp[0:NSLC, j, :], in_=negm[:, j, :],
                                                              identity=C.ident),
                      waits=[k, C.tp_free[pi]], sig=(j == 3))
        state["negm_free"] = tk
        ntok = P.op("act", lambda e, tp=tp: e.activation(
            out=negmT[0:NSLC, :].rearrange("p (a b) -> p a b", a=4), in_=tp[0:NSLC, 0:4, :], func=AF.Copy),
            waits=[tk, state.get("negmT_free")])
        C.tp_free[pi] = ntok
        for r in range(4):
            accs = acc_aps(setsel[0] % 2, 129)
            setsel[0] += 1
            cl = list(range(4 * pq + 4))
            seen = set()
            for c in cl:
                dl = 4 * pq - c
                aux = [(E[0:NSLC, c, :], negmT[0:NSLC, :])]
                if dl <= 0:
                    aux.append((C.ident, CM[:, -dl, :]))
                pvs = []
                for j in range(4):
                    if c <= 4 * pq + j:
                        pvs.append((accs[j][0], accs[j][1], j, c == 0, c == 4 * pq + j))
                pvs = bank_starts(pvs, seen)
                push(KT[:, 2, c * 128:(c + 1) * 128], qT[:, r, :], aux, Ttab[:, r, :],
                     OFFB[:, r * NOFF + dl + 3:r * NOFF + dl + 4], Vs[:, c, 0:129], pvs,
                     make_evac(r, 1, accs, False, pq) if c == cl[-1] else None, [qtok, ntok])
        flush()
        state["negmT_free"] = Tok(P.psem["pe"], P.cnt["pe"])
        state["qT_free"][pq % 2] = Tok(P.psem["pe"], P.cnt["pe"])
        state["blk_done"][pq] = state["last_evac"]
        if "osem" not in state:
            state["osem"] = P.dsem()
        osem = state["osem"]
        for j in range(4):
            t = 4 * pq + j
            stt = P.dma("sp", ob_dram[t * 128:(t + 1) * 128, :], o_t[:, j, :], osem, waits=[state["last_evac"]])
        state["o_free"] = stt
    P.barrier()
    P.top = mark
    P.sem_release(smark)


W_SHAPES = {"w_in": (4096, IN_COLS), "w_out": (4096, 4096), "w_mlp_in": (4096, D_FF), "w_mlp_out": (D_FF, 4096),
            "w_ple_gate": (4096, 4096), "w_ple_proj": (256, 4096)}
SMALL = {"mix_norm": 4096, "conv_w": 3072, "q_gain": 128, "k_gain": 384, "sgu_gain": 1024, "mix_out_norm": 4096,
         "mlp_norm": 4096, "ple_norm": 4096}
NQKV = 512 + 768 + 12
G8 = [list(range(8))]
G4 = [[0, 1, 2, 3], [4, 5, 6, 7]]


def collective(P, kind, src, dst, groups, waits, bg=False):
    P._wait("pool", waits)
    sem = P.dsem()
    sem.bg = bg
    sem.n += 1
    s_ = sem.sem
    P.q["pool"].append(lambda e: e.collective_compute(kind, ALU.bypass, replica_groups=groups, ins=[src.opt()],
                                                      outs=[dst.opt()]).then_inc(s_))
    return Tok(s_, sem.n)


def build_fused(depth=DEPTH, T=1024, TS=SEQ):
    P = PB(num_devices=8)
    nc = P.nc
    ext = {}

    def xin(name, shape):
        ext[name] = P.dram(name, shape, F32, "ExternalInput")
        return ext[name]

    x = xin("x", [T, 4096])
    p = xin("p", [depth, T, 256])
    wsh = {k: xin(k, [depth, K // 8, N]) for k, (K, N) in W_SHAPES.items()}
    sm = {k: xin(k, [depth, n]) for k, n in SMALL.items()}
    cmp_pos = xin("cmp_pos", [depth, 2, 32, 128])
    cmp_w1 = xin("cmp_w1", [depth, 2, 32, 128, 128])
    cmp_w2 = xin("cmp_w2", [depth, 2, 128, 128])
    w_sp = xin("w_sp", [depth, 8, 128, 128])
    b_sp = xin("b_sp", [depth, 8, 128])
    tabs = {k: xin(k, list(v.shape)) for k, v in nsa_tables(0, TS).items()}
    ident = xin("ident", [128, 128])
    tril = xin("tril", [128, 128])
    zeros = xin("zeros", [2, 3072])
    out = P.dram("out", [T, 4096], F32, "ExternalOutput")

    wfull = {k: [P.dram(f"{k}_f{i}", [K, N], BF16) for i in range(depth)] for k, (K, N) in W_SHAPES.items()}
    wstage = {k: [P.dram(f"{k}_s{i}", [K // 8, N], BF16) for i in range(depth)] for k, (K, N) in W_SHAPES.items()}
    hbuf = [P.dram(f"hbuf{i}", [T, 4096]) for i in range(2)]
    hs = P.dram("hs", [T, 4096])
    z = P.dram("z", [T, IN_COLS])
    zsend = P.dram("zsend", [T, 5168])
    zq_all = P.dram("zq_all", [4 * T, 5168])
    qkv_mine = P.dram("qkv_mine", [4 * T, NQKV])
    halo_send = P.dram("halo_send", [2, 3072])
    halo_pad = P.dram("halo_pad", [10, 3072])
    halo_mine = P.dram("halo_mine", [2, 3072])
    ob_mine = P.dram("ob_mine", [4 * T, 512])
    ob_all = P.dram("ob_all", [16 * T, 512])
    ob_tok = P.dram("ob_tok", [T, 2048])

    C = Ctx(P, T, ident)
    C.w_cast = False
    gtok = {}
    stsem = P.dsem()
    stsem.bg = True

    def gather_w(k, i):
        t = P.dma("pool", wstage[k][i], wsh[k][i], stsem)
        gtok[(k, i)] = collective(P, "AllGather", wstage[k][i], wfull[k][i], G8, [t], bg=True)

    misc = P.dsem()
    P.dma("sp", halo_pad[0:2, :], zeros, misc)
    gather_w("w_in", 0)
    gather_w("w_out", 0)
    SPE = mybir.EngineType.SP
    rk = {}

    for i in range(depth):
        h_in = x if i == 0 else hbuf[(i - 1) % 2]
        h_out = out if i == depth - 1 else hbuf[i % 2]
        C.w_ready = [gtok[("w_in", i)]]
        phase_A(P, C, h_in, sm["mix_norm"][i:i + 1, :], wfull["w_in"][i], z)
        C.w_ready = []
        P.dma("sp", zsend, z[:, OFF_Q:OFF_GU], misc)
        t1 = P.dma("sp", halo_send, z[T - 2:T, 0:3072], misc)
        x1 = collective(P, "AllGather", zsend, zq_all, G4, [t1])
        x1h = collective(P, "AllGather", halo_send, halo_pad[2:10, :], G4, [t1])
        gather_w("w_mlp_in", i)
        P._wait("sp", [x1, x1h])

        def relayout(e):
            if "rank" not in rk:
                rk["rank"] = e.snap(nc.partition_id([SPE]) % 4, min_val=0, max_val=3)
            rank = rk["rank"]
            e.dma_start(out=qkv_mine[:, 0:512], in_=zq_all[:, bass.ds(rank * 512, 512)]).then_inc(misc.sem, 16)
            e.dma_start(out=qkv_mine[:, 512:1280].rearrange("t (b x) -> t b x", b=6),
                        in_=zq_all[:, 2048:5120].rearrange("t (b x) -> t b x", b=6)[:, :, bass.ds(rank * 128, 128)]
                        ).then_inc(misc.sem, 16)
            e.dma_start(out=qkv_mine[:, 1280:1292], in_=zq_all[:, bass.ds(5120 + rank * 12, 12)]
                        ).then_inc(misc.sem, 16)
            e.dma_start(out=halo_mine, in_=halo_pad[bass.ds(rank * 2, 2), :]).then_inc(misc.sem, 16)

        P.q["sp"].append(relayout)
        misc.n += 64
        P.barrier()
        dB = dict(tabs)
        dB.update({"q": qkv_mine[:, 0:512], "kv": qkv_mine[:, 512:1280], "gl": qkv_mine[:, 1280:1292], "ob": ob_mine,
                   "q_gain": sm["q_gain"][i:i + 1, :], "k_gain": sm["k_gain"][i:i + 1, :], "cmp_pos": cmp_pos[i],
                   "cmp_w1": cmp_w1[i], "cmp_w2": cmp_w2[i]})
        phase_B(P, C, dB, TS)
        x2 = collective(P, "AllGather", ob_mine, ob_all, G4, [])
        gather_w("w_mlp_out", i)
        gather_w("w_ple_gate", i)
        gather_w("w_ple_proj", i)
        if i + 1 < depth:
            gather_w("w_in", i + 1)
            gather_w("w_out", i + 1)
        P._wait("sp", [x2])

        def relayout2(e):
            rank = rk["rank"]
            src = ob_all.rearrange("(r j t) c -> j r t c", r=4, j=4)[bass.ds(rank, 1), :, :, :]
            e.dma_start(out=ob_tok.rearrange("(o t) (r c) -> o r t c", o=1, r=4), in_=src).then_inc(misc.sem, 16)

        P.q["sp"].append(relayout2)
        misc.n += 16
        P.barrier()
        dC = {"h_in": h_in, "hs": hs, "h_out": h_out, "zc": z[:, 0:3072], "halo": halo_mine,
              "zg": z[:, OFF_GU:IN_COLS], "ob": ob_tok, "p": p[i], "conv_w": sm["conv_w"][i:i + 1, :],
              "sgu_gain": sm["sgu_gain"][i:i + 1, :], "w_sp": w_sp[i], "b_sp": b_sp[i], "tril": tril,
              "mix_out_norm": sm["mix_out_norm"][i:i + 1, :], "mlp_norm": sm["mlp_norm"][i:i + 1, :],
              "ple_norm": sm["ple_norm"][i:i + 1, :]}
        for k in ("w_out", "w_mlp_in", "w_mlp_out", "w_ple_gate", "w_ple_proj"):
            dC[k] = wfull[k][i]
            dC["tok_" + k] = [gtok[(k, i)]]
        phase_C(P, C, dC)
    return P.finish()


_NC_CACHE = {}


def _ext(P, name, shape):
    return P.dram(name, list(shape), F32, "ExternalInput")


def build_A():
    P = PB()
    h = _ext(P, "h", [1024, 4096])
    gain = _ext(P, "gain", [1, 4096])
    w = _ext(P, "w", [4096, IN_COLS])
    ident = _ext(P, "ident", [128, 128])
    z = P.dram("z", [1024, IN_COLS], F32, "ExternalOutput")
    C = Ctx(P, 1024, ident)
    phase_A(P, C, h, gain, w, z)
    return P.finish()


B_SHAPES = {"q": (SEQ, 512), "kv": (SEQ, 768), "gl": (SEQ, 12), "q_gain": (1, 128), "k_gain": (1, 384),
            "cmp_pos": (2, 32, 128), "cmp_w1": (2, 32, 128, 128), "cmp_w2": (2, 128, 128), "ident": (128, 128)}


def build_B():
    P = PB()
    d = {k: _ext(P, k, v.shape) for k, v in nsa_tables(0).items()}
    d.update({k: _ext(P, k, shp) for k, shp in B_SHAPES.items()})
    d["ob"] = P.dram("ob", [SEQ, 512], F32, "ExternalOutput")
    C = Ctx(P, SEQ, d["ident"])
    phase_B(P, C, d, SEQ)
    return P.finish()


C_SHAPES = {"h_in": (1024, 4096), "zc": (1024, 3072), "halo": (2, 3072), "zg": (1024, 2048), "ob": (1024, 2048),
            "p": (1024, 256), "conv_w": (1, 3072), "sgu_gain": (1, 1024), "w_sp": (8, 128, 128), "b_sp": (8, 128),
            "tril": (128, 128), "mix_out_norm": (1, 4096), "w_out": (4096, 4096), "mlp_norm": (1, 4096),
            "w_mlp_in": (4096, D_FF), "w_mlp_out": (D_FF, 4096), "ple_norm": (1, 4096), "w_ple_proj": (256, 4096),
            "w_ple_gate": (4096, 4096), "ident": (128, 128)}


def build_C():
    P = PB()
    d = {k: _ext(P, k, shp) for k, shp in C_SHAPES.items()}
    d["h_out"] = P.dram("h_out", [1024, 4096], F32, "ExternalOutput")
    d["hs"] = P.dram("hs", [1024, 4096], F32, "Internal")
    C = Ctx(P, 1024, d["ident"])
    phase_C(P, C, d)
    return P.finish()


def build_CA():
    P = PB()
    d = {k: _ext(P, k, shp) for k, shp in C_SHAPES.items()}
    gain_n = _ext(P, "gain_next", [1, 4096])
    w_n = _ext(P, "w_in_next", [4096, IN_COLS])
    h_ext = P.dram("h_out", [1024, 4096], F32, "ExternalOutput")
    z = P.dram("z", [1024, IN_COLS], F32, "ExternalOutput")
    d["h_out"] = P.dram("hn", [1024, 4096], F32, "Internal")
    d["hs"] = P.dram("hs", [1024, 4096], F32, "Internal")
    C = Ctx(P, 1024, d["ident"])
    phase_C(P, C, d)
    P.dma("sp", h_ext, d["h_out"], P.dsem())
    phase_A(P, C, d["h_out"], gain_n, w_n, z)
    return P.finish()


def _prog(name, fn):
    if name not in _NC_CACHE:
        _NC_CACHE[name] = fn()
    return _NC_CACHE[name]


def kernel(**inputs):
    f32 = np.float32
    g = lambda k: np.asarray(inputs[k], dtype=f32)
    cores = list(range(8))
    ident = np.eye(128, dtype=f32)
    tril = np.tril(np.ones((128, 128), f32))
    tabs = [nsa_tables(gi) for gi in range(4)]
    h = np.ascontiguousarray(g("x")).reshape(8, 1024, 4096)
    p = g("p")
    ncA, ncB, ncC, ncCA = _prog("A", build_A), _prog("B", build_B), _prog("C", build_C), _prog("CA", build_CA)
    w_in0 = np.ascontiguousarray(g("w_in")[0])
    res = run_bass_kernel_spmd(ncA, [{"h": h[c], "gain": g("mix_norm")[0].reshape(1, 4096), "w": w_in0,
                                      "ident": ident} for c in cores], core_ids=cores)
    zb = np.stack([np.asarray(r["z"]) for r in res.results]).reshape(BATCH, SEQ, IN_COLS)
    del res, w_in0
    for i in range(DEPTH):
        shB = {"q_gain": g("q_gain")[i].reshape(1, 128), "k_gain": g("k_gain")[i].reshape(1, 384),
               "cmp_pos": np.ascontiguousarray(g("cmp_pos")[i]), "cmp_w1": np.ascontiguousarray(g("cmp_w1")[i]),
               "cmp_w2": np.ascontiguousarray(g("cmp_w2")[i]), "ident": ident}
        insB = []
        for c in cores:
            b, gi = c // 4, c % 4
            m = dict(shB)
            m.update(tabs[gi])
            m["q"] = np.ascontiguousarray(zb[b, :, OFF_Q + gi * 512:OFF_Q + (gi + 1) * 512])
            m["kv"] = np.ascontiguousarray(np.concatenate(
                [zb[b, :, OFF_KV + br * 512 + gi * 128:OFF_KV + br * 512 + (gi + 1) * 128] for br in range(6)], -1))
            m["gl"] = np.ascontiguousarray(zb[b, :, OFF_GATE + gi * 12:OFF_GATE + (gi + 1) * 12])
            insB.append(m)
        res = run_bass_kernel_spmd(ncB, insB, core_ids=cores)
        obb = np.zeros((BATCH, SEQ, 2048), f32)
        for c in cores:
            b, gi = c // 4, c % 4
            obb[b, :, gi * 512:(gi + 1) * 512] = np.asarray(res.results[c]["ob"])
        del res, insB
        shC = {"conv_w": g("conv_w")[i].reshape(1, 3072), "sgu_gain": g("sgu_gain")[i].reshape(1, 1024),
               "w_sp": np.ascontiguousarray(g("w_sp")[i]), "b_sp": np.ascontiguousarray(g("b_sp")[i]), "tril": tril,
               "mix_out_norm": g("mix_out_norm")[i].reshape(1, 4096), "w_out": np.ascontiguousarray(g("w_out")[i]),
               "mlp_norm": g("mlp_norm")[i].reshape(1, 4096), "w_mlp_in": np.ascontiguousarray(g("w_mlp_in")[i]),
               "w_mlp_out": np.ascontiguousarray(g("w_mlp_out")[i]), "ple_norm": g("ple_norm")[i].reshape(1, 4096),
               "w_ple_proj": np.ascontiguousarray(g("w_ple_proj")[i]),
               "w_ple_gate": np.ascontiguousarray(g("w_ple_gate")[i]), "ident": ident}
        insC = []
        for c in cores:
            b, j = c // 4, c % 4
            t0 = j * 1024
            m = dict(shC)
            m["h_in"] = h[c]
            m["zc"] = np.ascontiguousarray(zb[b, t0:t0 + 1024, 0:3072])
            m["halo"] = (np.ascontiguousarray(zb[b, t0 - 2:t0, 0:3072]) if j > 0 else np.zeros((2, 3072), f32))
            m["zg"] = np.ascontiguousarray(zb[b, t0:t0 + 1024, OFF_GU:IN_COLS])
            m["ob"] = np.ascontiguousarray(obb[b, t0:t0 + 1024, :])
            m["p"] = np.ascontiguousarray(p[i, b, t0:t0 + 1024, :])
            insC.append(m)
        if i + 1 < DEPTH:
            gn = g("mix_norm")[i + 1].reshape(1, 4096)
            wn = np.ascontiguousarray(g("w_in")[i + 1])
            for m in insC:
                m["gain_next"] = gn
                m["w_in_next"] = wn
            res = run_bass_kernel_spmd(ncCA, insC, core_ids=cores)
            h = np.stack([np.asarray(r["h_out"]) for r in res.results])
            zb = np.stack([np.asarray(r["z"]) for r in res.results]).reshape(BATCH, SEQ, IN_COLS)
            del wn
        else:
            res = run_bass_kernel_spmd(ncC, insC, core_ids=cores)
            h = np.stack([np.asarray(r["h_out"]) for r in res.results])
        del res, insC, shC, obb
    return np.ascontiguousarray(h.reshape(BATCH, SEQ, D_MODEL)).astype(f32)
```
